# Optimizing a Trainium2 kernel written in Bass

```python
import math
import jax
import jax.numpy as jnp
from jax import lax
import numpy as np

D_MODEL = 1024
BATCH = 16
SEQ = 4096
DEPTH = 4
DEC_BATCH = 8
DEC_SEQ = 64
PAST_LEN = 2048

CHUNK = 64
N_MEM = 256
H_GDN = 4
DK_GDN = D_MODEL // (2 * H_GDN)
DV_GDN = D_MODEL // (2 * H_GDN)
CONV_W = 4
W_GDN_QK = H_GDN * DK_GDN
W_GDN_V = H_GDN * DV_GDN
CONV_CH = 2 * W_GDN_QK + W_GDN_V
H_DIFF = 4
D_DIFF = D_MODEL // (4 * H_DIFF)
W_DIFF = H_DIFF * 2 * D_DIFF
MIX_W = W_GDN_V + W_DIFF
H_MEM = 4
D_MEM = D_MODEL // H_MEM
D_FF = 4 * D_MODEL
Q_BLOCK = 128
EPS = 1e-6
SPLIT_AT = (CONV_CH, CONV_CH + W_GDN_V, CONV_CH + W_GDN_V + H_GDN, CONV_CH + W_GDN_V + 2 * H_GDN,
            CONV_CH + W_GDN_V + 2 * H_GDN + W_DIFF, CONV_CH + W_GDN_V + 2 * H_GDN + 2 * W_DIFF)
IN_COLS = CONV_CH + W_GDN_V + 2 * H_GDN + 3 * W_DIFF

kernel_name = 'hymba_gdn_diffattn_streaming_step'


def _rmsnorm(x, g):
    xf = x.astype(jnp.float32)
    y = xf * lax.rsqrt(jnp.mean(xf * xf, axis=-1, keepdims=True) + EPS)
    return (y * g.astype(jnp.float32)).astype(x.dtype)


def _l2norm(x):
    return x * lax.rsqrt(jnp.sum(x * x, axis=-1, keepdims=True) + EPS)


def _causal_conv(x, w, buf):
    L = x.shape[1]
    xp = jnp.concatenate([buf.astype(x.dtype), x], axis=1)
    y = w[0] * xp[:, 0:L]
    for i in range(1, CONV_W):
        y = y + w[i] * xp[:, i:i + L]
    return y, xp[:, L:]


def _gdn_chunked(q, k, v, g, beta, S0):
    B, L, H, DK = q.shape
    DV = v.shape[-1]
    C = min(CHUNK, L)
    n = L // C

    def blocks(t):
        t = t.reshape((B, n, C, H) + t.shape[3:])
        return jnp.moveaxis(jnp.moveaxis(t, 1, 0), 3, 2)

    qc, kc, vc, gc, bc = blocks(q), blocks(k), blocks(v), blocks(g), blocks(beta)
    G = jnp.cumsum(gc, axis=-1)
    tril = jnp.tril(jnp.ones((C, C), bool))
    strict = jnp.tril(jnp.ones((C, C), bool), -1)
    diff = G[..., :, None] - G[..., None, :]
    gam = jnp.where(tril, jnp.exp(jnp.where(tril, diff, 0.0)), 0.0)
    kk = jnp.einsum('nbhid,nbhjd->nbhij', kc, kc)
    a_sys = jnp.where(strict, bc[..., :, None] * kk * gam, 0.0) + jnp.eye(C, dtype=jnp.float32)
    u = lax.linalg.triangular_solve(a_sys, bc[..., None] * vc, left_side=True, lower=True, unit_diagonal=True)
    w = lax.linalg.triangular_solve(a_sys, (bc * jnp.exp(G))[..., None] * kc, left_side=True, lower=True,
                                    unit_diagonal=True)
    qk = jnp.einsum('nbhid,nbhjd->nbhij', qc, kc) * gam
    qg = qc * jnp.exp(G)[..., None]
    kd = kc * jnp.exp(G[..., -1:] - G)[..., None]
    gl = jnp.exp(G[..., -1])

    def step(S, xs):
        u_c, w_c, qk_c, qg_c, kd_c, gl_c = xs
        vn = u_c - jnp.einsum('bhcd,bhde->bhce', w_c, S)
        o = jnp.einsum('bhcd,bhde->bhce', qg_c, S) + jnp.einsum('bhij,bhje->bhie', qk_c, vn)
        S = S * gl_c[..., None, None] + jnp.einsum('bhcd,bhce->bhde', kd_c, vn)
        return S, o

    S, o = lax.scan(step, S0, (u, w, qk, qg, kd, gl))
    o = jnp.moveaxis(jnp.moveaxis(o, 2, 3), 0, 1).reshape(B, L, H, DV)
    return o, S


def _diff_attention(q, k, v, q_pos, k_pos, lam):
    B, Lq = q.shape[0], q.shape[1]
    n_blk = max(Lq // Q_BLOCK, 1)
    blk = Lq // n_blk
    qb = jnp.moveaxis(q.reshape((B, n_blk, blk) + q.shape[2:]), 1, 0)
    pb = q_pos.reshape(n_blk, blk)
    kf = k.astype(jnp.float32)
    vf = v.astype(jnp.float32)
    scale = D_DIFF ** -0.5

    def one(args):
        qi, pi = args
        s = jnp.einsum('bqhcd,bkhcd->bhcqk', qi.astype(jnp.float32), kf) * scale
        visible = k_pos[None, :] < ((pi // CHUNK + 1) * CHUNK)[:, None]
        s = jnp.where(visible, s, -jnp.inf)
        p = jax.nn.softmax(s, axis=-1)
        a = p[:, :, 0] - lam * p[:, :, 1]
        return jnp.einsum('bhqk,bkhe->bqhe', a, vf)

    o = lax.map(one, (qb, pb))
    return jnp.moveaxis(o, 0, 1).reshape(B, Lq, H_DIFF, 2 * D_DIFF)


def _mem_kv(mem, g, wk, wv):
    B, M, _ = mem.shape
    m = _rmsnorm(mem, g)
    return (m @ wk).reshape(B, M, H_MEM, D_MEM), (m @ wv).reshape(B, M, H_MEM, D_MEM)


def _mem_attn(x, mk, mv, wq, wo):
    B, L, _ = x.shape
    q = (x @ wq).reshape(B, L, H_MEM, D_MEM)
    s = jnp.einsum('blhd,bmhd->bhlm', q.astype(jnp.float32), mk.astype(jnp.float32)) * (D_MEM ** -0.5)
    p = jax.nn.softmax(s, axis=-1)
    o = jnp.einsum('bhlm,bmhd->blhd', p, mv.astype(jnp.float32)).astype(x.dtype)
    return o.reshape(B, L, D_MODEL) @ wo


def _layer(x, mem_k, mem_v, kc, vc, S0, conv_buf, lam_init, p):
    B, L, _ = x.shape
    dt = x.dtype
    xn = _rmsnorm(x, p['ln_mix'])
    proj = xn @ p['w_in']
    conv_in, z, a, b, qd, kd, vd = jnp.split(proj, SPLIT_AT, axis=-1)
    c, conv_new = _causal_conv(conv_in, p['conv_w'], conv_buf)
    c = jax.nn.silu(c.astype(jnp.float32))
    qa, ka, va = jnp.split(c, (W_GDN_QK, 2 * W_GDN_QK), axis=-1)
    qa = _l2norm(qa.reshape(B, L, H_GDN, DK_GDN)) * (DK_GDN ** -0.5)
    ka = _l2norm(ka.reshape(B, L, H_GDN, DK_GDN))
    va = va.reshape(B, L, H_GDN, DV_GDN)
    g = -jnp.exp(p['a_log'].astype(jnp.float32)) * jax.nn.softplus(a.astype(jnp.float32) + p['dt_bias'].astype(jnp.float32))
    beta = jax.nn.sigmoid(b.astype(jnp.float32))
    oa, S_new = _gdn_chunked(qa, ka, va, g, beta, S0.astype(jnp.float32))
    oa = _rmsnorm(oa, p['gdn_norm']) * jax.nn.silu(z.astype(jnp.float32).reshape(B, L, H_GDN, DV_GDN))
    oa = oa.reshape(B, L, W_GDN_V).astype(dt)
    kd = kd.reshape(B, L, H_DIFF, 2 * D_DIFF)
    vd = vd.reshape(B, L, H_DIFF, 2 * D_DIFF)
    if kc is None:
        k_all, v_all = kd, vd
    else:
        k_all = jnp.concatenate([kc.astype(dt), kd], axis=1)
        v_all = jnp.concatenate([vc.astype(dt), vd], axis=1)
    Lk = k_all.shape[1]
    k_pos = jnp.arange(Lk)
    q_pos = Lk - L + jnp.arange(L)
    lam = (jnp.exp(jnp.sum(p['lq1'].astype(jnp.float32) * p['lk1'].astype(jnp.float32)))
           - jnp.exp(jnp.sum(p['lq2'].astype(jnp.float32) * p['lk2'].astype(jnp.float32))) + lam_init)
    ob = _diff_attention(qd.reshape(B, L, H_DIFF, 2, D_DIFF), k_all.reshape(B, Lk, H_DIFF, 2, D_DIFF),
                         v_all, q_pos, k_pos, lam)
    ob = (_rmsnorm(ob, p['diff_norm']) * (1.0 - lam_init)).reshape(B, L, W_DIFF).astype(dt)
    h = x + jnp.concatenate([oa, ob], axis=-1) @ p['w_out']
    h = h + _mem_attn(_rmsnorm(h, p['ln_mem_q']), mem_k, mem_v, p['w_mem_q'], p['w_mem_o'])
    hn = _rmsnorm(h, p['ln_ffn'])
    h = h + jnp.square(jax.nn.relu(hn @ p['w_ff1'])) @ p['w_ff2']
    return h, kd, vd, S_new.astype(dt), conv_new


def setup_inputs(seed: int = 0) -> dict:
    key = jax.random.key(seed)
    ks = jax.random.split(key, 40)
    f32 = jnp.float32

    def nrm(k, shape, s):
        return jax.random.normal(k, shape, f32) * s

    def gain(k, shape):
        return 1.0 + 0.02 * jax.random.normal(k, shape, f32)

    dt0 = jnp.exp(jax.random.uniform(ks[9], (DEPTH, H_GDN), f32, math.log(1e-3), math.log(1e-1)))
    return {
        'x_prompt': nrm(ks[0], (BATCH, SEQ, D_MODEL), 1.0),
        'x_sample': nrm(ks[1], (DEC_BATCH, DEC_SEQ, D_MODEL), 1.0),
        'mem_prompt': nrm(ks[2], (BATCH, N_MEM, D_MODEL), 1.0),
        'cache_diff_k': nrm(ks[3], (DEPTH, DEC_BATCH, PAST_LEN, H_DIFF, 2 * D_DIFF), 1.0),
        'cache_diff_v': nrm(ks[4], (DEPTH, DEC_BATCH, PAST_LEN, H_DIFF, 2 * D_DIFF), 1.0),
        'cache_mem_k': nrm(ks[5], (DEPTH, DEC_BATCH, N_MEM, H_MEM, D_MEM), 1.0),
        'cache_mem_v': nrm(ks[6], (DEPTH, DEC_BATCH, N_MEM, H_MEM, D_MEM), 1.0),
        'state_gdn': nrm(ks[7], (DEPTH, DEC_BATCH, H_GDN, DK_GDN, DV_GDN), DK_GDN ** -0.5),
        'state_gdn_conv': nrm(ks[8], (DEPTH, DEC_BATCH, CONV_W - 1, CONV_CH), 1.0),
        'ln_mix': gain(ks[10], (DEPTH, D_MODEL)),
        'w_in': nrm(ks[11], (DEPTH, D_MODEL, IN_COLS), D_MODEL ** -0.5),
        'conv_w': nrm(ks[12], (DEPTH, CONV_W, CONV_CH), CONV_W ** -0.5),
        'a_log': jnp.log(jax.random.uniform(ks[13], (DEPTH, H_GDN), f32, 1.0, 16.0)),
        'dt_bias': dt0 + jnp.log(-jnp.expm1(-dt0)),
        'gdn_norm': gain(ks[14], (DEPTH, DV_GDN)),
        'lambda_q1': nrm(ks[15], (DEPTH, D_DIFF), 0.1),
        'lambda_k1': nrm(ks[16], (DEPTH, D_DIFF), 0.1),
        'lambda_q2': nrm(ks[17], (DEPTH, D_DIFF), 0.1),
        'lambda_k2': nrm(ks[18], (DEPTH, D_DIFF), 0.1),
        'diff_norm': gain(ks[19], (DEPTH, 2 * D_DIFF)),
        'w_out': nrm(ks[20], (DEPTH, MIX_W, D_MODEL), MIX_W ** -0.5),
        'ln_mem_q': gain(ks[21], (DEPTH, D_MODEL)),
        'ln_mem_kv': gain(ks[22], (DEPTH, D_MODEL)),
        'w_mem_q': nrm(ks[23], (DEPTH, D_MODEL, D_MODEL), D_MODEL ** -0.5),
        'w_mem_k': nrm(ks[24], (DEPTH, D_MODEL, D_MODEL), D_MODEL ** -0.5),
        'w_mem_v': nrm(ks[25], (DEPTH, D_MODEL, D_MODEL), D_MODEL ** -0.5),
        'w_mem_o': nrm(ks[26], (DEPTH, D_MODEL, D_MODEL), D_MODEL ** -0.5),
        'ln_ffn': gain(ks[27], (DEPTH, D_MODEL)),
        'w_ff1': nrm(ks[28], (DEPTH, D_MODEL, D_FF), D_MODEL ** -0.5),
        'w_ff2': nrm(ks[29], (DEPTH, D_FF, D_MODEL), D_FF ** -0.5),
        'ln_final': gain(ks[30], (D_MODEL,)),
    }


def reference(x_prompt, x_sample, mem_prompt, cache_diff_k, cache_diff_v, cache_mem_k, cache_mem_v,
              state_gdn, state_gdn_conv, ln_mix, w_in, conv_w, a_log, dt_bias, gdn_norm,
              lambda_q1, lambda_k1, lambda_q2, lambda_k2, diff_norm, w_out, ln_mem_q, ln_mem_kv,
              w_mem_q, w_mem_k, w_mem_v, w_mem_o, ln_ffn, w_ff1, w_ff2, ln_final):
    Bp = x_prompt.shape[0]
    hp, hs = x_prompt, x_sample
    pk, pv, pS, pc, pmk, pmv = [], [], [], [], [], []
    sk, sv, sS, sc = [], [], [], []
    for l in range(DEPTH):
        p = dict(ln_mix=ln_mix[l], w_in=w_in[l], conv_w=conv_w[l], a_log=a_log[l], dt_bias=dt_bias[l],
                 gdn_norm=gdn_norm[l], lq1=lambda_q1[l], lk1=lambda_k1[l], lq2=lambda_q2[l], lk2=lambda_k2[l],
                 diff_norm=diff_norm[l], w_out=w_out[l], ln_mem_q=ln_mem_q[l], w_mem_q=w_mem_q[l],
                 w_mem_o=w_mem_o[l], ln_ffn=ln_ffn[l], w_ff1=w_ff1[l], w_ff2=w_ff2[l])
        lam_init = 0.8 - 0.6 * math.exp(-0.3 * l)
        mk, mv = _mem_kv(mem_prompt, ln_mem_kv[l], w_mem_k[l], w_mem_v[l])
        S0 = jnp.zeros((Bp, H_GDN, DK_GDN, DV_GDN), jnp.float32)
        buf0 = jnp.zeros((Bp, CONV_W - 1, CONV_CH), x_prompt.dtype)
        hp, kp_, vp_, Sp_, cp_ = _layer(hp, mk, mv, None, None, S0, buf0, lam_init, p)
        pk.append(kp_); pv.append(vp_); pS.append(Sp_); pc.append(cp_); pmk.append(mk); pmv.append(mv)
        hs, ks_, vs_, Ss_, cs_ = _layer(hs, cache_mem_k[l], cache_mem_v[l], cache_diff_k[l], cache_diff_v[l],
                                        state_gdn[l], state_gdn_conv[l], lam_init, p)
        sk.append(ks_); sv.append(vs_); sS.append(Ss_); sc.append(cs_)
    y_prompt = _rmsnorm(hp, ln_final)
    y_sample = _rmsnorm(hs, ln_final)
    return (y_prompt, y_sample,
            jnp.stack(pk), jnp.stack(pv), jnp.stack(pS), jnp.stack(pc), jnp.stack(pmk), jnp.stack(pmv),
            jnp.stack(sk), jnp.stack(sv), jnp.stack(sS), jnp.stack(sc))
```

```python
import math
import numpy as np
import concourse.bass as bass
import concourse.mybir as mybir
from concourse.bass_utils import run_bass_kernel_spmd
from contextlib import ExitStack

F32 = mybir.dt.float32
F32R = mybir.dt.float32r
BF16 = mybir.dt.bfloat16
AF = mybir.ActivationFunctionType
ALU = mybir.AluOpType

D = 1024
KC = 8
N_MEM = 256
CONV_CH = 1536
IN_COLS = 3592
D_FF = 4096
EPS = 1e-6
BIG = 30000.0
VW = 132
NSLOT = 3


class Cfg:
    def __init__(self, depth=4, seq=4096, nps=2, past=2048, dec=64, ncores=8, stop=99):
        self.depth, self.seq, self.nps, self.past, self.dec, self.ncores = depth, seq, nps, past, dec, ncores
        self.stop = stop


class StopBuild(Exception):
    pass


FULL = Cfg()

PIECES = ([("c0", "w_in", 0, 512), ("c1", "w_in", 512, 512), ("c2", "w_in", 1024, 512), ("z", "w_in", 1536, 512),
           ("ab", "w_in", 2048, 8), ("qd", "w_in", 2056, 512), ("kd", "w_in", 2568, 512), ("vd", "w_in", 3080, 512),
           ("o0", "w_out", 0, 512), ("o1", "w_out", 512, 512),
           ("mq0", "w_mem_q", 0, 512), ("mq1", "w_mem_q", 512, 512),
           ("mo0", "w_mem_o", 0, 512), ("mo1", "w_mem_o", 512, 512)]
          + [("f1_%d" % i, "w_ff1", 512 * i, 512) for i in range(8)]
          + [("f2_%d" % i, "w_ff2", 128 * i, 128) for i in range(8)]
          + [("mk0", "w_mem_k", 0, 512), ("mk1", "w_mem_k", 512, 512),
             ("mv0", "w_mem_v", 0, 512), ("mv1", "w_mem_v", 512, 512)])
PIDX = {p[0]: i for i, p in enumerate(PIECES)}
NPIECE = len(PIECES)
MAIN_ORDER = [p[0] for p in PIECES[:30]]
MEM_ORDER = ["mk0", "mk1", "mv0", "mv1"]


ALLBUFS = []


class Buf:
    __slots__ = ("name", "wr", "rds", "dsem", "dcnt")

    def __init__(self, name):
        self.name, self.wr, self.rds, self.dsem, self.dcnt = name, None, {}, None, 0
        ALLBUFS.append(self)


class V:
    __slots__ = ("ap", "bufs")

    def __init__(self, ap, bufs):
        self.ap, self.bufs = ap, bufs


class TB:
    def __init__(self, h, name, nsub=1, subs=None):
        self.h = h
        if subs is None:
            subs = [[Buf("%s.%d" % (name, i))] for i in range(nsub)]
        self.subs = subs
        self.nsub = len(subs)
        self.bufs = []
        for sl_ in subs:
            for b in sl_:
                if b not in self.bufs:
                    self.bufs.append(b)

    def __getitem__(self, idx):
        ap = self.h[idx]
        if self.nsub > 1 and isinstance(idx, tuple) and len(idx) >= 2 and isinstance(idx[1], int):
            return V(ap, self.subs[idx[1]])
        return V(ap, self.bufs)

    def v(self, ap, sub=None):
        return V(ap, self.bufs if sub is None else self.subs[sub])


def alias(tb, el0, shape, dt, nsub=1):
    n = 1
    for d_ in shape[1:]:
        n *= d_
    nb = n * (2 if dt == F32 else 1)
    flat = tb.h[:].rearrange("p a b -> p (a b)")[:, el0:el0 + nb]
    if dt == F32:
        flat = flat.bitcast(F32)
    if len(shape) == 3:
        ap = flat.rearrange("p (a b) -> p a b", b=shape[2])
    elif len(shape) == 4:
        ap = flat.rearrange("p (a b c) -> p a b c", b=shape[2], c=shape[3])
    else:
        ap = flat
    per = nb // nsub
    subs = []
    for i in range(nsub):
        lo, hi = el0 + i * per, el0 + (i + 1) * per - 1
        bl = []
        for j in range(lo // 512, hi // 512 + 1):
            bl.extend(tb.subs[j])
        subs.append(bl)
    return TB(ap, "alias", subs=subs)


class Eng:
    def __init__(self, name, h, sem, sid, selfraw):
        self.name, self.h, self.sem, self.sid, self.cnt, self.seen, self.selfraw = name, h, sem, sid, 0, {}, selfraw


class K:
    def __init__(self, nc, es):
        self.nc, self.es = nc, es
        self.sems = {}
        self.nsem = 0
        self.PE = self._eng("pe", nc.tensor, False)
        self.ACT = self._eng("act", nc.scalar, True)
        self.DVE = self._eng("dve", nc.vector, True)
        self.POOL = self._eng("pool", nc.gpsimd, True)
        self.SP = self._eng("sp", nc.sync, False)
        self.nins = 0

    def newsem(self, name):
        s = self.es.enter_context(self.nc.semaphore(name))
        self.nsem += 1
        sid = self.nsem
        self.sems[sid] = s
        return s, sid

    def _eng(self, name, h, selfraw):
        s, sid = self.newsem("e_" + name)
        return Eng(name, h, s, sid, selfraw)

    def sb(self, name, shape, dt, nsub=1):
        h = self.es.enter_context(self.nc.sbuf_tensor(name, list(shape), dt))
        return TB(h, name, nsub)

    def _wait(self, eng, need):
        for sid, val in need.items():
            if eng.seen.get(sid, 0) < val:
                eng.h.wait_ge(self.sems[sid], val)
                eng.seen[sid] = val

    def _deps(self, eng, rd, wr):
        need = {}

        def add(ev, raw):
            sid, val = ev
            if sid == eng.sid and not eng.selfraw:
                return
            if need.get(sid, 0) < val:
                need[sid] = val
        for b in rd:
            if b.wr is not None:
                add(b.wr, True)
        for b in wr:
            if b.wr is not None:
                add(b.wr, False)
            for sid, val in b.rds.items():
                add((sid, val), False)
        self._wait(eng, need)

    def op(self, eng, fn, ins, outs, inc=True):
        rd = [b for v in ins if isinstance(v, V) for b in v.bufs]
        wr = [b for v in outs for b in v.bufs]
        self._deps(eng, rd, wr)
        i = fn()
        self.nins += 1
        inc = True
        if inc:
            eng.cnt += 1
            i.then_inc(eng.sem, 1)
            ev = (eng.sid, eng.cnt)
        else:
            ev = (eng.sid, eng.cnt + 1)
        for b in rd:
            if b.rds.get(ev[0], 0) < ev[1]:
                b.rds[ev[0]] = ev[1]
        for b in wr:
            b.wr = ev
            b.rds = {}
        return i

    def dma(self, eng, out, in_, sembuf=None, slow=False):
        rd, wr = in_.bufs, out.bufs
        self._deps(eng, rd, wr)
        sb_ = sembuf if sembuf is not None else (wr[0] if wr else rd[0])
        if sb_.dsem is None:
            sb_.dsem = self.newsem("d_" + sb_.name.replace(".", "_"))
        sem, sid = sb_.dsem
        sb_.dcnt += 1
        kw = {}
        if slow:
            kw["allow_slow_non_contiguous"] = True
        eng.h.dma_start(out=out.ap, in_=in_.ap, **kw).then_inc(sem, 16)
        self.nins += 1
        ev = (sid, 16 * sb_.dcnt)
        for b in rd:
            if b.rds.get(ev[0], 0) < ev[1]:
                b.rds[ev[0]] = ev[1]
        for b in wr:
            b.wr = ev
            b.rds = {}

    def mm(self, out, lhsT, rhs, start=True, stop=True, inc=True):
        nc = self.nc
        return self.op(self.PE, lambda: nc.tensor.matmul(out.ap, lhsT=lhsT.ap, rhs=rhs.ap, start=start, stop=stop,
                                                         skip_group_check=True),
                       [lhsT, rhs], [out], inc=inc)

    def tr(self, out, in_, ident):
        nc = self.nc
        return self.op(self.PE, lambda: nc.tensor.transpose(out.ap, in_.ap, ident.ap), [in_, ident], [out])

    def act(self, out, in_, func, bias=None, scale=None, accum=None):
        nc = self.nc
        kw = {}
        ins = [in_]
        if bias is not None:
            kw["bias"] = bias.ap if isinstance(bias, V) else bias
            ins.append(bias)
        if scale is not None:
            kw["scale"] = scale.ap if isinstance(scale, V) else scale
            ins.append(scale)
        outs = [out]
        if accum is not None:
            kw["accum_out"] = accum.ap
            outs.append(accum)
        return self.op(self.ACT, lambda: nc.scalar.activation(out=out.ap, in_=in_.ap, func=func, **kw), ins, outs)

    def _e(self, eng):
        return self.DVE if eng == "v" else self.POOL

    def tt(self, out, a, b, op, eng="v"):
        e = self._e(eng)
        return self.op(e, lambda: e.h.tensor_tensor(out=out.ap, in0=a.ap, in1=b.ap, op=op), [a, b], [out])

    def ts(self, out, a, s1, op0, s2=None, op1=None, eng="v"):
        e = self._e(eng)
        kw = {}
        if op1 is not None:
            kw["op1"] = op1
        a1 = s1.ap if isinstance(s1, V) else s1
        a2 = s2.ap if isinstance(s2, V) else s2
        return self.op(e, lambda: e.h.tensor_scalar(out=out.ap, in0=a.ap, scalar1=a1, scalar2=a2, op0=op0, **kw),
                       [a, s1, s2], [out])

    def stt(self, out, a, s, b, op0, op1):
        e = self.DVE
        a1 = s.ap if isinstance(s, V) else s
        return self.op(e, lambda: e.h.scalar_tensor_tensor(out=out.ap, in0=a.ap, scalar=a1, in1=b.ap, op0=op0, op1=op1),
                       [a, s, b], [out])

    def cp(self, out, in_, eng="v"):
        e = self._e(eng)
        return self.op(e, lambda: e.h.tensor_copy(out=out.ap, in_=in_.ap), [in_], [out])

    def recip(self, out, in_):
        e = self.DVE
        return self.op(e, lambda: e.h.reciprocal(out=out.ap, in_=in_.ap), [in_], [out])

    def memset(self, out, val, eng="p"):
        e = self._e(eng)
        return self.op(e, lambda: e.h.memset(out.ap, val), [], [out])


def dram(nc, name, shape, dt, kind):
    return nc.dram_tensor(name, list(shape), dt, kind=kind)


def build(cfg):
    nc = bass.Bass("TRN2", target_bir_lowering=False)
    del ALLBUFS[:]
    DEPTH, SEQ, NPS, PAST, DEC = cfg.depth, cfg.seq, cfg.nps, cfg.past, cfg.dec
    NSEQ = NPS + 1
    HMAX = max(SEQ - 512, PAST, 128)

    def din(name, shape):
        return dram(nc, name, shape, F32, "ExternalInput")

    def dout(name, shape):
        return dram(nc, name, shape, F32, "ExternalOutput")
    I = {}
    I["x_prompt"] = din("x_prompt", [NPS, SEQ, D])
    I["x_sample"] = din("x_sample", [1, DEC, D])
    I["mem_prompt"] = din("mem_prompt", [NPS, N_MEM, D])
    I["cache_diff_k"] = din("cache_diff_k", [DEPTH, PAST, 512])
    I["cache_diff_v"] = din("cache_diff_v", [DEPTH, PAST, 512])
    I["cache_mem_k"] = din("cache_mem_k", [DEPTH, N_MEM, D])
    I["cache_mem_v"] = din("cache_mem_v", [DEPTH, N_MEM, D])
    I["state_gdn"] = din("state_gdn", [DEPTH, 4, 128, 128])
    I["state_gdn_conv"] = din("state_gdn_conv", [DEPTH, 3, CONV_CH])
    for nm, shp in [("ln_mix", [DEPTH, D]), ("w_in", [DEPTH, D, IN_COLS]), ("conv_w", [DEPTH, 4, CONV_CH]),
                    ("a_log", [DEPTH, 4]), ("dt_bias", [DEPTH, 4]), ("gdn_norm", [DEPTH, 128]),
                    ("lambda_q1", [DEPTH, 64]), ("lambda_k1", [DEPTH, 64]), ("lambda_q2", [DEPTH, 64]),
                    ("lambda_k2", [DEPTH, 64]), ("diff_norm", [DEPTH, 128]), ("w_out", [DEPTH, D, D]),
                    ("ln_mem_q", [DEPTH, D]), ("ln_mem_kv", [DEPTH, D]), ("w_mem_q", [DEPTH, D, D]),
                    ("w_mem_k", [DEPTH, D, D]), ("w_mem_v", [DEPTH, D, D]), ("w_mem_o", [DEPTH, D, D]),
                    ("ln_ffn", [DEPTH, D]), ("w_ff1", [DEPTH, D, D_FF]), ("w_ff2", [DEPTH, D_FF, D]),
                    ("ln_final", [1, D]), ("consts", [9, 128, 128])]:
        I[nm] = din(nm, shp)
    O = {}
    O["y_prompt"] = dout("y_prompt", [NPS, SEQ, D])
    O["y_sample"] = dout("y_sample", [1, DEC, D])
    O["ndk_p"] = dout("ndk_p", [DEPTH, NPS, SEQ, 512])
    O["ndv_p"] = dout("ndv_p", [DEPTH, NPS, SEQ, 512])
    O["ngdn_p"] = dout("ngdn_p", [DEPTH, NPS, 4, 128, 128])
    O["nconv_p"] = dout("nconv_p", [DEPTH, NPS, 3, CONV_CH])
    O["nmk_p"] = dout("nmk_p", [DEPTH, NPS, N_MEM, D])
    O["nmv_p"] = dout("nmv_p", [DEPTH, NPS, N_MEM, D])
    O["ndk_s"] = dout("ndk_s", [DEPTH, 1, DEC, 512])
    O["ndv_s"] = dout("ndv_s", [DEPTH, 1, DEC, 512])
    O["ngdn_s"] = dout("ngdn_s", [DEPTH, 1, 4, 128, 128])
    O["nconv_s"] = dout("nconv_s", [DEPTH, 1, 3, CONV_CH])
    HL = max(SEQ, PAST)
    wbf = dram(nc, "wbf", [DEPTH, NPIECE, 128, 4096], BF16, "Internal")
    khist = dram(nc, "khist", [DEPTH, NSEQ, 4, 128, HL], BF16, "Internal")
    vhist = dram(nc, "vhist", [DEPTH, NSEQ, HL, 4 * VW], BF16, "Internal")
    mkT_sc = dram(nc, "mkT_sc", [DEPTH, NSEQ, 128, 8 * N_MEM], BF16, "Internal")
    mv_sc = dram(nc, "mv_sc", [DEPTH, NSEQ, N_MEM, D], BF16, "Internal")

    es = ExitStack()
    with es:
        k = K(nc, es)
        PE, ACT, DVE, POOL, SP = k.PE, k.ACT, k.DVE, k.POOL, k.SP
        cst = k.sb("cst", [128, 9, 128], F32)
        onesb = k.sb("onesb", [128, 128], BF16)
        identb = k.sb("identb", [128, 128], BF16)
        gains = k.sb("gains", [128, DEPTH, 4, KC], F32)
        gfin = k.sb("gfin", [128, KC], F32)
        cw = k.sb("cw", [128, DEPTH, 12, 4], F32)
        dtb8 = k.sb("dtb8", [128, DEPTH, 8], F32)
        negA = k.sb("negA", [128, DEPTH, 4], F32)
        gnorm = k.sb("gnorm", [128, DEPTH, 128], F32)
        dnorm = k.sb("dnorm", [128, DEPTH, 128], F32)
        nlam = k.sb("nlam", [128, DEPTH], F32)
        lamt = k.sb("lamt", [128, 4, 64], F32)
        lams = k.sb("lams", [128, 4], F32)
        hT = k.sb("hT", [128, KC, 512], F32, nsub=KC)
        xn = k.sb("xn", [128, KC, 512], BF16, nsub=KC)
        Sst = k.sb("Sst", [128, DEPTH * 4, 128], F32, nsub=DEPTH * 4)
        Sbf = k.sb("Sbf", [128, 4, 128], BF16, nsub=4)
        halo = k.sb("halo", [128, DEPTH, 12, 3], F32, nsub=DEPTH)
        wslot = [k.sb("wslot%d" % i, [128, 4096], BF16) for i in range(NSLOT)]
        cin = [k.sb("cin%d" % i, [128, 515], F32) for i in range(2)]
        ctmp = [k.sb("ctmp%d" % i, [128, 512], F32) for i in range(2)]
        sqb = [k.sb("sqb%d" % i, [128, 512], BF16) for i in range(2)]
        lnv = k.sb("lnv", [128, 512], F32)
        rstd = k.sb("rstd", [128, 512], F32)
        kvf = k.sb("kvf", [128, 8, 512], F32, nsub=8)
        qkb = k.sb("qkb", [128, 8, 512], BF16, nsub=8)
        zg = k.sb("zg", [128, 4, 512], F32, nsub=4)
        vde = k.sb("vde", [128, 4, 4 * VW], BF16, nsub=4)
        stage = [k.sb("stage%d" % i, [128, 512], F32) for i in range(2)]
        h1 = k.sb("h1", [128, 32, 512], BF16, nsub=32)
        assert HMAX <= 3584
        kTh = [alias(h1, 0, [128, HMAX], BF16)]
        vh = [alias(h1, 3584, [128, HMAX // 128, VW], BF16)]
        tokb = alias(h1, 7296, [128, 4, D], F32, nsub=4)
        Pt = [k.sb("Pt%d" % i, [128, 2, 128], BF16) for i in range(3)]
        mixT = k.sb("mixT", [128, KC, 512], BF16, nsub=KC)
        qmT = qkb
        qdT = TB(mixT.h[:, 0:4, :], "qdT", subs=mixT.subs[0:4])
        qd1 = k.sb("qd1", [128, 4, 512], BF16, nsub=4)
        kdT = TB(mixT.h[:, 4:8, :], "kdT", subs=mixT.subs[4:8])
        memk = [k.sb("memk%d" % i, [128, 8, N_MEM], BF16) for i in range(1)]
        memv = [k.sb("memv%d" % i, [128, 2, D], BF16) for i in range(1)]
        Pm = [k.sb("Pm%d" % i, [128, 512], BF16) for i in range(4)]
        rinv = lnv
        rtmp = ctmp
        abx = k.sb("abx", [128, 8], F32)
        abm = k.sb("abm", [128, 8], F32)
        abl = k.sb("abl", [128, 8], F32)
        gtok = k.sb("gtok", [128, 4], F32)
        lnb = k.sb("lnb", [128, 4], F32)
        beta = k.sb("beta", [128, 4], F32)
        Gtok = k.sb("Gtok", [128, 4], F32)
        negG = k.sb("negG", [128, 4], F32)
        GpL = k.sb("GpL", [128, 4], F32)
        bexpG = k.sb("bexpG", [128, 4], F32)
        dl = k.sb("dl", [128, 4], F32)
        elast = k.sb("elast", [128, 4], F32)
        glt = k.sb("glt", [128, 4], F32)
        glb = k.sb("glb", [128, 4], F32)
        osb = [k.sb("osb%d" % i, [128, 128], F32) for i in range(2)]
        gbc = [k.sb("gbc%d" % i, [128, 128], F32) for i in range(2)]
        lbc = [k.sb("lbc%d" % i, [128, 128], F32) for i in range(2)]
        expGr = [k.sb("expGr%d" % i, [128, 128], F32) for i in range(2)]
        E2 = [k.sb("E2_%d" % i, [128, 128], F32) for i in range(2)]
        Eb = [k.sb("Eb_%d" % i, [128, 128], F32) for i in range(2)]
        E3 = [k.sb("E3_%d" % i, [128, 128], F32) for i in range(2)]
        Pk = [k.sb("Pk%d" % i, [128, 128], F32) for i in range(3)]
        Mk = [k.sb("Mk%d" % i, [128, 128], F32) for i in range(3)]
        Rk = [k.sb("Rk%d" % i, [128, 128], F32) for i in range(3)]
        TTb = [k.sb("TTb%d" % i, [128, 128], BF16) for i in range(2)]
        PkS = [k.sb("PkS%d" % i, [64, 64], F32) for i in range(3)]
        MkS = [k.sb("MkS%d" % i, [64, 64], F32) for i in range(3)]
        RkS = [k.sb("RkS%d" % i, [64, 64], F32) for i in range(3)]
        Ao1 = [k.sb("Ao1_%d" % i, [128, 128], F32) for i in range(2)]
        Ao2 = [k.sb("Ao2_%d" % i, [128, 128], F32) for i in range(2)]
        identR = k.sb("identR", [128, 128], F32R)
        qkT = [k.sb("qkT%d" % i, [128, 128], BF16) for i in range(2)]
        qgT = [k.sb("qgT%d" % i, [128, 128], BF16) for i in range(2)]
        kbg = [k.sb("kbg%d" % i, [128, 128], BF16) for i in range(2)]
        kdec = [k.sb("kdec%d" % i, [128, 128], BF16) for i in range(2)]
        bvt = [k.sb("bvt%d" % i, [128, 128], BF16) for i in range(2)]
        nwT = [k.sb("nwT%d" % i, [128, 128], BF16) for i in range(2)]
        vnb = [k.sb("vnb%d" % i, [128, 128], BF16) for i in range(2)]
        junk = k.sb("junk", [128, 128], F32)
        ss = [k.sb("ss%d" % i, [128, 1], F32) for i in range(2)]
        sl = [k.sb("sl%d" % i, [128, 1], F32) for i in range(2)]
        sr = [k.sb("sr%d" % i, [128, 1], F32) for i in range(2)]
        r0 = [k.sb("r0_%d" % i, [128, 1], F32) for i in range(2)]
        r1 = [k.sb("r1_%d" % i, [128, 1], F32) for i in range(2)]
        t0 = [k.sb("t0_%d" % i, [128, 128], F32) for i in range(2)]
        od = [k.sb("od%d" % i, [128, 128], F32) for i in range(2)]
        onecol = k.sb("onecol", [128, 16, 4, 4], BF16)
        psb = [TB(es.enter_context(nc.psum_tensor("ps%d" % i, [128, 512], F32)), "ps%d" % i) for i in range(8)]
        rr = {"g": 0, "h": 0}

        def pb(pool="g"):
            i = rr[pool]
            rr[pool] = (i + 1) % 4
            return psb[i + (0 if pool == "g" else 4)]
        rot = {}

        def R(lst, key):
            i = rot.get(key, 0)
            rot[key] = (i + 1) % len(lst)
            return lst[i]

        def DV(t, name):
            return TB(t.ap() if hasattr(t, "ap") and callable(t.ap) else t, name)

        def dv(ap, buf):
            return V(ap, [buf])

        b_in = Buf("inputs")
        b_out = Buf("outputs")
        b_wbf = [Buf("wbf%d" % l) for l in range(DEPTH)]
        b_kh = [[Buf("kh%d_%d" % (l, s)) for s in range(NSEQ)] for l in range(DEPTH)]
        b_vh = [[Buf("vh%d_%d" % (l, s)) for s in range(NSEQ)] for l in range(DEPTH)]
        b_mk = [[Buf("mk%d_%d" % (l, s)) for s in range(NSEQ)] for l in range(DEPTH)]
        b_mv = [[Buf("mv%d_%d" % (l, s)) for s in range(NSEQ)] for l in range(DEPTH)]

        def inp(name):
            return I[name].ap()

        def IN(ap):
            return V(ap, [])

        def OUT(ap):
            return V(ap, [])

        ident = cst[:, 0, :]
        utri = cst[:, 1, :]

        def C_(i, P, F):
            return cst.v(cst.h[0:P, i, 0:F])

        def stage_(n):
            if cfg.stop <= n:
                raise StopBuild()
        try:
            k.dma(SP, cst[:], IN(inp("consts").rearrange("c p f -> p c f")))
            k.cp(onesb[:], cst[:, 5, :])
            k.cp(identb[:], cst[:, 0, :])
            for gi, nm in enumerate(["ln_mix", "ln_mem_q", "ln_ffn", "ln_mem_kv"]):
                for l in range(DEPTH):
                    k.dma(SP, gains.v(gains.h[:, l, gi, :]), IN(inp(nm)[l].rearrange("(kc p) -> p kc", p=128)), slow=True)
            k.dma(SP, gfin[:], IN(inp("ln_final").rearrange("o (kc p) -> p (o kc)", p=128)), slow=True)
            for l in range(DEPTH):
                for i in range(4):
                    k.dma(SP, cw.v(cw.h[:, l, :, i]), IN(inp("conv_w")[l, i].rearrange("(j p) -> p j", p=128)), slow=True)
            k.memset(dtb8[:], 0.0)
            k.dma(SP, dtb8.v(dtb8.h[:, :, 0:4]), IN(inp("dt_bias").partition_broadcast(128)), slow=True)
            k.dma(SP, negA[:], IN(inp("a_log").partition_broadcast(128)), slow=True)
            k.act(negA[:], negA[:], AF.Exp)
            k.ts(negA[:], negA[:], -1.0, ALU.mult)
            k.dma(SP, gnorm[:], IN(inp("gdn_norm").partition_broadcast(128)), slow=True)
            k.dma(SP, dnorm[:], IN(inp("diff_norm").partition_broadcast(128)), slow=True)
            for l in range(DEPTH):
                lam_init = 0.8 - 0.6 * math.exp(-0.3 * l)
                k.ts(dnorm.v(dnorm.h[:, l, :]), dnorm.v(dnorm.h[:, l, :]), 1.0 - lam_init, ALU.mult)
                for j, nm in enumerate(["lambda_q1", "lambda_k1", "lambda_q2", "lambda_k2"]):
                    k.dma(SP, lamt.v(lamt.h[:, j, :]), IN(inp(nm)[l].partition_broadcast(128)), slow=True)
                k.tt(lamt.v(lamt.h[:, 0, :]), lamt.v(lamt.h[:, 0, :]), lamt.v(lamt.h[:, 1, :]), ALU.mult)
                k.tt(lamt.v(lamt.h[:, 2, :]), lamt.v(lamt.h[:, 2, :]), lamt.v(lamt.h[:, 3, :]), ALU.mult)
                k.op(DVE, lambda: nc.vector.reduce_sum(out=lams.h[:, 0:1], in_=lamt.h[:, 0, :], axis=mybir.AxisListType.X),
                     [lamt[:]], [lams[:]])
                k.op(DVE, lambda: nc.vector.reduce_sum(out=lams.h[:, 1:2], in_=lamt.h[:, 2, :], axis=mybir.AxisListType.X),
                     [lamt[:]], [lams[:]])
                k.act(lams.v(lams.h[:, 2:4]), lams.v(lams.h[:, 0:2]), AF.Exp)
                k.tt(lams.v(lams.h[:, 0:1]), lams.v(lams.h[:, 3:4]), lams.v(lams.h[:, 2:3]), ALU.subtract)
                k.ts(nlam.v(nlam.h[:, l:l + 1]), lams.v(lams.h[:, 0:1]), -lam_init, ALU.add)
            k.memset(onecol[:], 0.0)
            k.memset(onecol.v(onecol.h[:, :, :, 0:1]), 1.0)
            k.memset(vde[:], 0.0)
            k.memset(qd1[:], 0.0)
            k.act(identR[:], cst[:, 0, :], AF.Copy)
            for b4 in range(4):
                k.memset(vde.v(vde.h[:, b4, :].rearrange("p (h c) -> p h c", c=VW)[:, :, 128:129]), 1.0)

            stage_(1)
            for l in range(DEPTH):
                for pi, (pn, src, c0, ncol) in enumerate(PIECES):
                    w = inp(src)[l]
                    if src == "w_ff2":
                        s_ap = w.rearrange("(kc p) n -> p kc n", p=128)[:, :, c0:c0 + ncol]
                        d_ap = wbf.ap()[l, pi].rearrange("p (kc n) -> p kc n", n=ncol)
                    else:
                        s_ap = w.rearrange("(kc p) n -> p kc n", p=128)[:, :, c0:c0 + ncol]
                        d_ap = wbf.ap()[l, pi][:, 0:8 * ncol].rearrange("p (kc n) -> p kc n", n=ncol)
                    k.dma(POOL, V(d_ap, [b_wbf[l]]), IN(s_ap), sembuf=b_wbf[l])

            stage_(2)
            order = []
            for s in range(NPS):
                for l in range(DEPTH):
                    for pn in MEM_ORDER:
                        order.append((l, pn))
            tiles = []
            for s in range(NSEQ):
                L = SEQ if s < NPS else DEC
                T = min(512, L)
                for t in range(L // T):
                    tiles.append((s, t, T))
            for (s, t, T) in tiles:
                for l in range(DEPTH):
                    for pn in MAIN_ORDER:
                        order.append((l, pn))
            ws = {"issued": 0, "pos": 0}

            def wissue():
                i = ws["issued"]
                if i >= len(order):
                    return
                l, pn = order[i]
                pi = PIDX[pn]
                ncol = PIECES[pi][3]
                nk = 32 if PIECES[pi][1] == "w_ff2" else 8
                slot = wslot[i % NSLOT]
                k.dma(SP, slot.v(slot.h[:, 0:nk * ncol]), V(wbf.ap()[l, pi][:, 0:nk * ncol], [b_wbf[l]]))
                ws["issued"] = i + 1

            def wget(l, pn):
                i = ws["pos"]
                assert order[i] == (l, pn), (order[i], l, pn)
                while ws["issued"] < min(i + NSLOT, len(order)):
                    wissue()
                ws["pos"] = i + 1
                slot = wslot[i % NSLOT]
                pi = PIDX[pn]
                ncol = PIECES[pi][3]
                nk = 32 if PIECES[pi][1] == "w_ff2" else 8
                view = slot.h[:, 0:nk * ncol].rearrange("p (kc n) -> p kc n", n=ncol)
                return lambda kc, c0, c1: slot.v(view[:, kc, c0:c1])

            def rmsnorm_fm(src, gain_fn, T, dst, dst_is_bf=True):
                p = pb("g")
                for kc in range(KC):
                    s = R(sqb, "sqb")
                    k.act(s[:, 0:T], src[:, kc, 0:T], AF.Square)
                    k.mm(p[:, 0:T], onesb[:], s[:, 0:T], start=(kc == 0), stop=(kc == KC - 1), inc=(kc == KC - 1))
                k.act(lnv[:, 0:T], p[:, 0:T], AF.Ln, bias=epsc[:, 0:1], scale=1.0 / D)
                k.act(rstd[:, 0:T], lnv[:, 0:T], AF.Exp, scale=-0.5)
                for kc in range(KC):
                    k.stt(dst[:, kc, 0:T], src[:, kc, 0:T], gain_fn(kc), rstd[:, 0:T], ALU.mult, ALU.mult)

            epsc = k.sb("epsc", [128, 2], F32)
            k.memset(epsc.v(epsc.h[:, 0:1]), EPS)
            k.memset(epsc.v(epsc.h[:, 1:2]), 1.0)

            def proj_fm(wv, ncols, T, evac, rhs=None):
                src = xn if rhs is None else rhs
                for j in range(ncols // 128):
                    p = pb("g")
                    for kc in range(KC):
                        k.mm(p[:, 0:T], wv(kc, j * 128, (j + 1) * 128), src[:, kc, 0:T], start=(kc == 0), stop=(kc == KC - 1),
                             inc=(kc == KC - 1))
                    evac(j, p)

            def proj_tm(wv, ncols, T, evac, src=None):
                src = xn if src is None else src
                TB_ = min(128, T)
                for blk in range(T // TB_):
                    p = pb("g")
                    for kc in range(KC):
                        k.mm(p[0:TB_, 0:ncols], src[:, kc, blk * TB_:(blk + 1) * TB_], wv(kc, 0, ncols), start=(kc == 0),
                             stop=(kc == KC - 1), inc=(kc == KC - 1))
                    evac(blk, TB_, p)

            memT = alias(h1, 0, [128, KC, N_MEM], F32, nsub=KC)
            mnb1 = alias(h1, 4096, [128, KC, N_MEM], BF16, nsub=KC)
            mnb = [mnb1 for _ in range(max(NPS, 1))]
            mrs1 = alias(h1, 6144, [128, N_MEM], F32)
            mrs = [mrs1 for _ in range(max(NPS, 1))]
            mkTs = TB(xn.h[:, 0:4, :].rearrange("p a (b m) -> p (a b) m", m=N_MEM), "mkTs", subs=[sum(xn.subs[0:4], [])])
            mvs = TB(xn.h[:, 4:8, :], "mvs", subs=[sum(xn.subs[4:8], [])])
            for s in range(NPS):
                for mb in range(2):
                    k.dma(SP, tokb[0:128, mb, :], IN(inp("mem_prompt")[s, mb * 128:(mb + 1) * 128, :]))
                stage_(2.5)
                for kc in range(KC):
                    p = pb("g")
                    for mb in range(2):
                        k.tr(p[:, mb * 128:(mb + 1) * 128], tokb[0:128, mb, kc * 128:(kc + 1) * 128], ident)
                    k.cp(memT[:, kc, :], p[:, 0:N_MEM])
                stage_(2.6)
                p = pb("g")
                for kc in range(KC):
                    sq = R(sqb, "sqb")
                    k.act(sq[:, 0:N_MEM], memT[:, kc, :], AF.Square)
                    k.mm(p[:, 0:N_MEM], onesb[:], sq[:, 0:N_MEM], start=(kc == 0), stop=(kc == KC - 1), inc=(kc == KC - 1))
                stage_(2.7)
                k.act(lnv[:, 0:N_MEM], p[:, 0:N_MEM], AF.Ln, bias=epsc[:, 0:1], scale=1.0 / D)
                stage_(2.8)
                k.act(mrs[s][:], lnv[:, 0:N_MEM], AF.Exp, scale=-0.5)
                stage_(3.1)
                for l in range(DEPTH):
                    for kc in range(KC):
                        k.stt(mnb[s][:, kc, :], memT[:, kc, :], gains.v(gains.h[:, l, 3, kc:kc + 1]), mrs[s][:], ALU.mult, ALU.mult)
                    stage_(3.2)
                    for half in range(2):
                        wv = wget(l, "mk%d" % half)
                        for mb in range(2):
                            p = pb("g")
                            for kc in range(KC):
                                k.mm(p[:, 0:512], mnb[s][:, kc, mb * 128:(mb + 1) * 128], wv(kc, 0, 512), start=(kc == 0),
                                     stop=(kc == KC - 1), inc=(kc == KC - 1))
                            st = R(stage, "stage")
                            k.cp(st[:], p[:])
                            k.dma(POOL, OUT(O["nmk_p"].ap()[l, s, mb * 128:(mb + 1) * 128, half * 512:(half + 1) * 512]), st[:])
                        for j in range(4):
                            p = pb("g")
                            for kc in range(KC):
                                k.mm(p[:, 0:N_MEM], wv(kc, j * 128, (j + 1) * 128), mnb[s][:, kc, :], start=(kc == 0),
                                     stop=(kc == KC - 1), inc=(kc == KC - 1))
                            k.act(mkTs.v(mkTs.h[:, half * 4 + j, :]), p[:, 0:N_MEM], AF.Copy)
                    stage_(3.4)
                    for half in range(2):
                        wv = wget(l, "mv%d" % half)
                        for mb in range(2):
                            p = pb("g")
                            for kc in range(KC):
                                k.mm(p[:, 0:512], mnb[s][:, kc, mb * 128:(mb + 1) * 128], wv(kc, 0, 512), start=(kc == 0),
                                     stop=(kc == KC - 1), inc=(kc == KC - 1))
                            st = R(stage, "stage")
                            k.cp(st[:], p[:])
                            k.dma(POOL, OUT(O["nmv_p"].ap()[l, s, mb * 128:(mb + 1) * 128, half * 512:(half + 1) * 512]), st[:])
                            k.act(mvs.v(mvs.h[:, mb * 2 + half, :]), st[:], AF.Copy)
                            stage_(3.45)
                    stage_(3.5)
                    k.dma(POOL, V(mkT_sc.ap()[l, s].rearrange("p (c m) -> p c m", m=N_MEM), [b_mk[l][s]]), mkTs[:])
                    for mb in range(2):
                        k.dma(POOL, V(mv_sc.ap()[l, s, mb * 128:(mb + 1) * 128, :], [b_mv[l][s]]),
                              mvs.v(mvs.h[:, 2 * mb:2 * mb + 2, :].rearrange("p a b -> p (a b)")))

            stage_(4)
            sS = NPS
            for l in range(DEPTH):
                k.dma(POOL, V(mv_sc.ap()[l, sS], [b_mv[l][sS]]), IN(inp("cache_mem_v")[l]), sembuf=b_mv[l][sS])
                for mb in range(2):
                    k.dma(SP, tokb[0:128, mb, :], IN(inp("cache_mem_k")[l, mb * 128:(mb + 1) * 128, :]))
                for kc in range(KC):
                    p = pb("g")
                    for mb in range(2):
                        k.tr(p[:, mb * 128:(mb + 1) * 128], tokb[0:128, mb, kc * 128:(kc + 1) * 128], ident)
                    k.act(mkTs.v(mkTs.h[:, kc, :]), p[:, 0:N_MEM], AF.Copy)
                k.dma(POOL, V(mkT_sc.ap()[l, sS].rearrange("p (c m) -> p c m", m=N_MEM), [b_mk[l][sS]]), mkTs[:])
                k.dma(POOL, V(vhist.ap()[l, sS, 0:PAST, :].rearrange("t (h c) -> t h c", c=VW)[:, :, 0:128], [b_vh[l][sS]]),
                      IN(inp("cache_diff_v")[l].rearrange("t (h c) -> t h c", c=128)), sembuf=b_vh[l][sS])
                for g0 in range(PAST // 128):
                    k.dma(POOL, V(vhist.ap()[l, sS, g0 * 128:(g0 + 1) * 128, :].rearrange("p (h c) -> p h c", c=VW)[:, :, 128:132],
                                  [b_vh[l][sS]]), onecol.v(onecol.h[:, 0, :, :]), sembuf=b_vh[l][sS], slow=True)
                for g0 in range(0, PAST // 128, 4):
                    for gb in range(4):
                        k.dma(SP, tokb[0:128, gb, 0:512], IN(inp("cache_diff_k")[l, (g0 + gb) * 128:(g0 + gb + 1) * 128, :]))
                    for h in range(4):
                        p = pb("g")
                        for gb in range(4):
                            k.tr(p[:, gb * 128:(gb + 1) * 128], tokb[0:128, gb, h * 128:(h + 1) * 128], ident)
                        k.act(kdT[:, h, :], p[:], AF.Copy)
                        k.dma(POOL, V(khist.ap()[l, sS, h, :, g0 * 128:(g0 + 4) * 128], [b_kh[l][sS]]), kdT[:, h, :])

            stage_(5)
            def gdn_chunk(l, s, blk, C, T):
                c0 = blk * C
                for h in range(4):
                    gb_ = R(gbc, "gbc")
                    lb_ = R(lbc, "lbc")
                    k.ts(gb_[0:C, :], C_(5, C, 128), gtok[0:C, h:h + 1], ALU.mult)
                    k.ts(lb_[0:C, :], C_(5, C, 128), lnb[0:C, h:h + 1], ALU.mult)
                    p = pb("h")
                    k.mm(p[:, 0:C], gb_[0:C, :], C_(1, C, C))
                    eg = R(expGr, "expGr")
                    k.act(eg[:, 0:C], p[:, 0:C], AF.Exp)
                    qg = R(qgT, "qgT")
                    k.tt(qg[:, 0:C], qkb[:, h, c0:c0 + C], eg[:, 0:C], ALU.mult)
                    p = pb("h")
                    k.mm(p[0:C, 0:C], gb_[0:C, 0:C], C_(1, C, C), start=True, stop=False, inc=False)
                    k.mm(p[0:C, 0:C], C_(0, C, C), C_(2, C, C), start=False, stop=True)
                    e2 = R(E2, "E2")
                    k.act(e2[0:C, 0:C], p[0:C, 0:C], AF.Exp, bias=GpL[0:C, h:h + 1], scale=-1.0)
                    p = pb("h")
                    k.mm(p[0:C, 0:C], gb_[0:C, 0:C], C_(1, C, C), start=True, stop=False, inc=False)
                    k.mm(p[0:C, 0:C], lb_[0:C, 0:C], C_(0, C, C), start=False, stop=False, inc=False)
                    k.mm(p[0:C, 0:C], C_(0, C, C), C_(3, C, C), start=False, stop=True)
                    eb = R(Eb, "Eb")
                    k.act(eb[0:C, 0:C], p[0:C, 0:C], AF.Exp, bias=negG[0:C, h:h + 1])
                    p = pb("h")
                    k.mm(p[0:C, 0:C], gb_[0:C, 0:C], C_(1, C, C), start=True, stop=False, inc=False)
                    k.mm(p[0:C, 0:C], C_(0, C, C), C_(4, C, C), start=False, stop=True)
                    e3 = R(E3, "E3")
                    k.act(e3[0:C, 0:C], p[0:C, 0:C], AF.Exp, bias=negG[0:C, h:h + 1])
                    kT_ = qkb[:, 4 + h, c0:c0 + C]
                    pk = pb("h")
                    k.mm(pk[0:C, 0:C], kT_, kT_)
                    Pk_, Mk_, Rk_ = (Pk, Mk, Rk) if C == 128 else (PkS, MkS, RkS)
                    pkey = "Pk" if C == 128 else "PkS"
                    mkey = "Mk" if C == 128 else "MkS"
                    rkey = "Rk" if C == 128 else "RkS"

                    def rv0(v):
                        return V(v.ap.bitcast(F32R), v.bufs) if C == 128 else v
                    P0 = R(Pk_, pkey)
                    M0 = R(Mk_, mkey)
                    k.stt(rv0(P0[0:C, 0:C]), pk[0:C, 0:C], -1.0, e2[0:C, 0:C], ALU.mult, ALU.mult)
                    k.stt(rv0(M0[0:C, 0:C]), pk[0:C, 0:C], -1.0, eb[0:C, 0:C], ALU.mult, ALU.mult)
                    pq = pb("h")
                    k.mm(pq[0:C, 0:C], kT_, qkb[:, h, c0:c0 + C])
                    qk_ = R(qkT, "qkT")
                    k.tt(qk_[0:C, 0:C], pq[0:C, 0:C], e3[0:C, 0:C], ALU.mult)
                    merge = (C == 128)

                    def rv(v):
                        return V(v.ap.bitcast(F32R), v.bufs) if merge else v
                    idm = identR[0:C, 0:C] if merge else C_(0, C, C)
                    if merge:
                        P0r, M0r = R(Pk_, pkey), R(Mk_, mkey)
                        a1, a2 = R(Ao1, "Ao1"), R(Ao2, "Ao2")
                        k.tt(rv(a1[:, :]), P0[:, :], C_(7, 128, 128), ALU.mult, eng="p")
                        k.tt(rv(a2[:, :]), P0[:, :], C_(8, 128, 128), ALU.mult, eng="p")
                        k.tt(rv(P0r[:, :]), P0[:, :], C_(6, 128, 128), ALU.mult, eng="p")
                        k.tt(rv(M0r[:, :]), M0[:, :], C_(6, 128, 128), ALU.mult, eng="p")
                        P0, M0 = P0r, M0r
                    Rc = R(Rk_, rkey)
                    k.tt(rv(Rc[0:C, 0:C]), M0[0:C, 0:C], C_(0, C, C), ALU.add, eng="p")
                    nlev = 4 if merge else 5
                    Pc, Mc = P0, M0
                    for lev in range(1, nlev + 1):
                        pp = pb("h")
                        k.mm(pp[0:C, 0:C], rv(Mc[0:C, 0:C]), rv(Pc[0:C, 0:C]))
                        Pn = R(Pk_, pkey)
                        k.act(rv(Pn[0:C, 0:C]), pp[0:C, 0:C], AF.Copy)
                        if lev < nlev:
                            pm = pb("h")
                            k.mm(pm[0:C, 0:C], rv(Pc[0:C, 0:C]), rv(Mc[0:C, 0:C]))
                            Mn = R(Mk_, mkey)
                            k.cp(rv(Mn[0:C, 0:C]), pm[0:C, 0:C])
                        pr = pb("h")
                        k.mm(pr[0:C, 0:C], idm, rv(Rc[0:C, 0:C]), start=True, stop=False, inc=False)
                        k.mm(pr[0:C, 0:C], rv(Pn[0:C, 0:C]), rv(Rc[0:C, 0:C]), start=False, stop=True)
                        last = (lev == nlev)
                        Rn = R(TTb, "TTb") if (last and not merge) else R(Rk_, rkey)
                        k.cp(rv(Rn[0:C, 0:C]) if not (last and not merge) else Rn[0:C, 0:C], pr[0:C, 0:C])
                        Pc = Pn
                        if lev < nlev:
                            Mc = Mn
                        Rc = Rn
                    if merge:
                        W = Rc
                        for mi, ao in enumerate((a1, a2)):
                            pt_ = pb("h")
                            k.op(k.PE, lambda: nc.tensor.transpose(pt_.h[:, 0:128].bitcast(F32R), W.h[:, :].bitcast(F32R),
                                                                   identR.h[:, :]), [W[:, :], identR[:, :]], [pt_[:, 0:128]])
                            Tbd = R(Pk_, pkey)
                            k.act(rv(Tbd[:, :]), pt_[:, 0:128], AF.Copy)
                            py = pb("h")
                            k.mm(py[:, 0:128], rv(ao[:, :]), rv(W[:, :]))
                            Ysb = R(Mk_, mkey)
                            k.cp(rv(Ysb[:, :]), py[:, 0:128])
                            px = pb("h")
                            k.mm(px[:, 0:128], rv(Tbd[:, :]), rv(Ysb[:, :]))
                            if mi == 0:
                                Wn = R(Rk_, rkey)
                                k.tt(rv(Wn[:, :]), px[:, 0:128], W[:, :], ALU.add)
                            else:
                                Wn = R(TTb, "TTb")
                                k.tt(Wn[:, :], px[:, 0:128], W[:, :], ALU.add)
                            W = Wn
                        Rc = W
                    TT = Rc
                    ptk = pb("h")
                    k.tr(ptk[0:C, 0:128], kvf[:, h, c0:c0 + C], ident)
                    kb_ = R(kbg, "kbg")
                    kd_ = R(kdec, "kdec")
                    k.ts(kb_[0:C, :], ptk[0:C, 0:128], bexpG[0:C, h:h + 1], ALU.mult)
                    k.ts(kd_[0:C, :], ptk[0:C, 0:128], elast[0:C, h:h + 1], ALU.mult)
                    ptv = pb("h")
                    k.tr(ptv[0:C, 0:128], kvf[:, 4 + h, c0:c0 + C], ident)
                    bv_ = R(bvt, "bvt")
                    k.ts(bv_[0:C, :], ptv[0:C, 0:128], beta[0:C, h:h + 1], ALU.mult)
                    pw = pb("h")
                    k.mm(pw[:, 0:C], kb_[0:C, :], TT[0:C, 0:C])
                    nw_ = R(nwT, "nwT")
                    k.act(nw_[:, 0:C], pw[:, 0:C], AF.Copy, scale=-1.0)
                    pv = pb("h")
                    k.mm(pv[0:C, 0:128], TT[0:C, 0:C], bv_[0:C, :], start=True, stop=False, inc=False)
                    k.mm(pv[0:C, 0:128], nw_[:, 0:C], Sbf[:, h, :], start=False, stop=True)
                    vn_ = R(vnb, "vnb")
                    k.cp(vn_[0:C, :], pv[0:C, 0:128])
                    po = pb("h")
                    k.mm(po[0:C, 0:128], qg[:, 0:C], Sbf[:, h, :], start=True, stop=False, inc=False)
                    k.mm(po[0:C, 0:128], qk_[0:C, 0:C], vn_[0:C, :], start=False, stop=True)
                    psn = pb("h")
                    k.mm(psn[:, 0:128], kd_[0:C, :], vn_[0:C, :])
                    Sv = Sst[:, l * 4 + h, :]
                    k.stt(Sv, Sv, glt[:, h:h + 1], psn[:, 0:128], ALU.mult, ALU.add)
                    k.cp(Sbf[:, h, :], Sv, eng="p")
                    s1, s2, s3 = R(ss, "ss"), R(sl, "sl"), R(sr, "sr")
                    ob_ = R(osb, "osb")
                    k.cp(ob_[0:C, :], po[0:C, 0:128])
                    k.act(junk[0:C, :], ob_[0:C, :], AF.Square, accum=s1[0:C, :])
                    k.act(s2[0:C, :], s1[0:C, :], AF.Ln, bias=epsc[0:C, 0:1], scale=1.0 / 128)
                    k.act(s3[0:C, :], s2[0:C, :], AF.Exp, scale=-0.5)
                    k.stt(tokb[0:C, blk, h * 128:(h + 1) * 128], ob_[0:C, :], s3[0:C, :], zg[0:C, blk, h * 128:(h + 1) * 128],
                          ALU.mult, ALU.mult)

            def tile_layer(l, s, t, T, nh, first_tile, last_tile):
                is_s = (s >= NPS)
                TB_ = min(128, T)
                NB = T // TB_
                C = TB_
                tok0 = t * T
                okey = "_s" if is_s else "_p"
                so = 0 if is_s else s
                lam_init = 0.8 - 0.6 * math.exp(-0.3 * l)
                if first_tile:
                    if is_s:
                        for h in range(4):
                            k.dma(SP, Sst[:, l * 4 + h, :], IN(inp("state_gdn")[l, h]))
                        for i in range(3):
                            k.dma(SP, halo.v(halo.h[:, l, :, i], sub=l),
                                  IN(inp("state_gdn_conv")[l, i].rearrange("(j p) -> p j", p=128)), slow=True)
                    else:
                        for h in range(4):
                            k.memset(Sst[:, l * 4 + h, :], 0.0)
                        k.memset(halo.v(halo.h[:, l, :, :], sub=l), 0.0)
                for h in range(4):
                    k.cp(Sbf[:, h, :], Sst[:, l * 4 + h, :], eng="p")
                rmsnorm_fm(hT, lambda kc: gains.v(gains.h[:, l, 0, kc:kc + 1]), T, xn)
                stage_(6.1)
                for pc in range(3):
                    wv = wget(l, "c%d" % pc)

                    def ev_conv(jj, p, pc=pc):
                        j = pc * 4 + jj
                        ci = R(cin, "cin")
                        k.cp(ci[:, 0:3], halo.v(halo.h[:, l, j, :], sub=l), eng="p")
                        k.act(ci[:, 3:3 + T], p[:, 0:T], AF.Copy)
                        k.cp(halo.v(halo.h[:, l, j, :], sub=l), ci[:, T:T + 3], eng="p")
                        ct = R(ctmp, "ctmp")
                        k.ts(ct[:, 0:T], ci[:, 0:T], cw.v(cw.h[:, l, j, 0:1]), ALU.mult)
                        for i in range(1, 4):
                            k.stt(ct[:, 0:T], ci[:, i:i + T], cw.v(cw.h[:, l, j, i:i + 1]), ct[:, 0:T], ALU.mult, ALU.add)
                        if j >= 8:
                            k.act(kvf[:, j - 4, 0:T], ct[:, 0:T], AF.Silu)
                            return
                        k.act(ct[:, 0:T], ct[:, 0:T], AF.Silu)
                        sq = R(sqb, "sqb")
                        k.act(sq[:, 0:T], ct[:, 0:T], AF.Square)
                        p2 = pb("h")
                        k.mm(p2[:, 0:T], onesb[:], sq[:, 0:T])
                        k.act(lnv[:, 0:T], p2[:, 0:T], AF.Ln, bias=epsc[:, 0:1])
                        k.act(rstd[:, 0:T], lnv[:, 0:T], AF.Exp, scale=-0.5)
                        if j < 4:
                            k.stt(qkb[:, j, 0:T], ct[:, 0:T], 128.0 ** -0.5, rstd[:, 0:T], ALU.mult, ALU.mult)
                        else:
                            k.tt(kvf[:, j - 4, 0:T], ct[:, 0:T], rstd[:, 0:T], ALU.mult)
                            k.cp(qkb[:, j, 0:T], kvf[:, j - 4, 0:T], eng="p")
                    proj_fm(wv, 512, T, ev_conv)
                if last_tile:
                    for i in range(3):
                        k.dma(POOL, OUT(O["nconv" + okey].ap()[l, so, i].rearrange("(j p) -> p j", p=128)),
                              halo.v(halo.h[:, l, :, i], sub=l), slow=True)
                stage_(6.2)
                wv = wget(l, "z")

                def ev_z(blk, TBx, p):
                    k.act(zg[0:TBx, blk, :], p[0:TBx, 0:512], AF.Silu)
                    k.tt(zg.v(zg.h[0:TBx, blk, :].rearrange("p (h c) -> p h c", c=128), sub=blk),
                         zg.v(zg.h[0:TBx, blk, :].rearrange("p (h c) -> p h c", c=128), sub=blk),
                         gnorm.v(gnorm.h[0:TBx, l:l + 1, :].to_broadcast([TBx, 4, 128])), ALU.mult, eng="p")
                proj_tm(wv, 512, T, ev_z)
                stage_(6.3)
                wv_ab = wget(l, "ab")
                for blk in range(NB):
                    p = pb("g")
                    for kc in range(KC):
                        k.mm(p[0:C, 0:8], xn[:, kc, blk * C:(blk + 1) * C], wv_ab(kc, 0, 8), start=(kc == 0), stop=(kc == KC - 1),
                             inc=(kc == KC - 1))
                    k.tt(abx[0:C, :], p[0:C, 0:8], dtb8.v(dtb8.h[0:C, l, :]), ALU.add)
                    k.act(abm[0:C, :], abx[0:C, :], AF.Abs)
                    k.act(abm[0:C, :], abm[0:C, :], AF.Exp, scale=-1.0)
                    k.act(abl[0:C, :], abm[0:C, :], AF.Ln, bias=epsc[0:C, 1:2])
                    k.stt(gtok[0:C, :], abx[0:C, 0:4], 0.0, abl[0:C, 0:4], ALU.max, ALU.add)
                    k.tt(gtok[0:C, :], gtok[0:C, :], negA.v(negA.h[0:C, l, :]), ALU.mult)
                    k.stt(lnb[0:C, :], abx[0:C, 4:8], 0.0, abl[0:C, 4:8], ALU.min, ALU.subtract)
                    k.act(beta[0:C, :], lnb[0:C, :], AF.Exp)
                    p = pb("h")
                    k.mm(p[0:C, 0:4], C_(1, C, C), gtok[0:C, :])
                    k.mm(p[:, 8:12], C_(5, C, 128), gtok[0:C, :])
                    k.cp(Gtok[0:C, :], p[0:C, 0:4])
                    k.ts(negG[0:C, :], p[0:C, 0:4], -1.0, ALU.mult)
                    k.tt(GpL[0:C, :], p[0:C, 0:4], lnb[0:C, :], ALU.add)
                    k.cp(glb[:, :], p[:, 8:12])
                    k.act(glt[:, :], glb[:, :], AF.Exp)
                    k.tt(dl[0:C, :], p[0:C, 8:12], Gtok[0:C, :], ALU.subtract)
                    k.act(elast[0:C, :], dl[0:C, :], AF.Exp)
                    k.act(bexpG[0:C, :], GpL[0:C, :], AF.Exp)
                    gdn_chunk(l, s, blk, C, T)
                if last_tile:
                    for h in range(4):
                        k.dma(POOL, OUT(O["ngdn" + okey].ap()[l, so, h]), Sst[:, l * 4 + h, :])
                stage_(6.4)
                wv = wget(l, "qd")
                def ev_qd(j, p):
                    k.memset(qdT[64:128, j, 0:T], 0.0)
                    k.act(qdT[0:64, j, 0:T], p[0:64, 0:T], AF.Copy, scale=0.125)
                    k.act(qd1[64:128, j, 0:T], p[64:128, 0:T], AF.Copy, scale=0.125)
                proj_fm(wv, 512, T, ev_qd)
                wv = wget(l, "kd")

                def ev_kd(j, p):
                    k.act(kdT[:, j, 0:T], p[:, 0:T], AF.Copy)
                    if not is_s and not last_tile:
                        k.dma(POOL, V(khist.ap()[l, s, j, :, tok0:tok0 + T], [b_kh[l][s]]), kdT[:, j, 0:T])
                proj_fm(wv, 512, T, ev_kd)

                def ev_kd_tm(blk, TBx, p):
                    st = R(stage, "stage")
                    k.cp(st[0:TBx, :], p[0:TBx, 0:512])
                    k.dma(POOL, OUT(O["ndk" + okey].ap()[l, so, tok0 + blk * TBx:tok0 + (blk + 1) * TBx, :]), st[0:TBx, :])
                proj_tm(wv, 512, T, ev_kd_tm)
                wv = wget(l, "vd")

                def ev_vd(blk, TBx, p):
                    st = R(stage, "stage")
                    k.cp(st[0:TBx, :], p[0:TBx, 0:512])
                    k.dma(POOL, OUT(O["ndv" + okey].ap()[l, so, tok0 + blk * TBx:tok0 + (blk + 1) * TBx, :]), st[0:TBx, :])
                    k.act(vde.v(vde.h[0:TBx, blk, :].rearrange("p (h c) -> p h c", c=VW)[:, :, 0:128], sub=blk),
                          st.v(st.h[0:TBx, 0:512].rearrange("p (h c) -> p h c", c=128)), AF.Copy)
                    if not is_s and not last_tile:
                        k.dma(POOL, V(vhist.ap()[l, s, tok0 + blk * TBx:tok0 + (blk + 1) * TBx, :], [b_vh[l][s]]),
                              vde[0:TBx, blk, :])
                proj_tm(wv, 512, T, ev_vd)
                stage_(6.5)
                nhb = nh // 128
                for h in range(4):
                    kt_, vh_ = R(kTh, "kTh"), R(vh, "vh")
                    if nh > 0:
                        k.dma(SP, kt_[:, 0:nh], V(khist.ap()[l, s, h, :, 0:nh], [b_kh[l][s]]))
                        k.dma(SP, vh_[:, 0:nhb, :],
                              V(vhist.ap()[l, s, 0:nh, h * VW:(h + 1) * VW].rearrange("(g p) c -> p g c", p=128), [b_vh[l][s]]))
                    for qb in range(NB):
                        q0 = qb * TB_
                        acc = [pb("h"), pb("h")]
                        nkb = nhb + qb + 1
                        for kb in range(nkb):
                            ps_ = pb("g")
                            if kb < nhb:
                                KP = 128
                                for c in range(2):
                                    k.mm(ps_[0:128, c * 128:c * 128 + TB_], kt_[:, kb * 128:(kb + 1) * 128],
                                         (qdT if c == 0 else qd1)[:, h, q0:q0 + TB_])
                            else:
                                j = kb - nhb
                                KP = TB_
                                for c in range(2):
                                    k.mm(ps_[0:KP, c * 128:c * 128 + TB_], kdT[:, h, j * TB_:(j + 1) * TB_],
                                         (qdT if c == 0 else qd1)[:, h, q0:q0 + TB_])
                            pt = R(Pt, "Pt")
                            k.act(pt.v(pt.h[0:KP, :, 0:TB_]), ps_.v(ps_.h[0:KP, 0:256].rearrange("p (c q) -> p c q", q=128)[:, :, 0:TB_]),
                                  AF.Exp)
                            if kb == nkb - 1 and TB_ == 128:
                                k.memset(pt.v(pt.h[64:128, :, 0:64]), 0.0)
                            for c in range(2):
                                if kb < nhb:
                                    rhs = vh_[:, kb, 0:129]
                                else:
                                    rhs = vde.v(vde.h[0:KP, kb - nhb, h * VW:h * VW + 129], sub=kb - nhb)
                                k.mm(acc[c][0:TB_, 0:129], pt.v(pt.h[0:KP, c, 0:TB_]), rhs, start=(kb == 0), stop=(kb == nkb - 1),
                                     inc=(kb == nkb - 1))
                        a0, a1 = acc
                        rr0, rr1 = R(r0, "r0"), R(r1, "r1")
                        k.recip(rr0[0:TB_, :], a0[0:TB_, 128:129])
                        k.recip(rr1[0:TB_, :], a1[0:TB_, 128:129])
                        k.tt(rr1[0:TB_, :], rr1[0:TB_, :], nlam.v(nlam.h[0:TB_, l:l + 1]), ALU.mult)
                        tt0, odd = R(t0, "t0"), R(od, "od")
                        k.ts(tt0[0:TB_, :], a0[0:TB_, 0:128], rr0[0:TB_, :], ALU.mult)
                        k.stt(odd[0:TB_, :], a1[0:TB_, 0:128], rr1[0:TB_, :], tt0[0:TB_, :], ALU.mult, ALU.add)
                        s1, s2, s3 = R(ss, "ss"), R(sl, "sl"), R(sr, "sr")
                        k.act(junk[0:TB_, :], odd[0:TB_, :], AF.Square, accum=s1[0:TB_, :])
                        k.act(s2[0:TB_, :], s1[0:TB_, :], AF.Ln, bias=epsc[0:TB_, 0:1], scale=1.0 / 128)
                        k.act(s3[0:TB_, :], s2[0:TB_, :], AF.Exp, scale=-0.5)
                        k.stt(tokb[0:TB_, qb, 512 + h * 128:512 + (h + 1) * 128], odd[0:TB_, :], s3[0:TB_, :],
                              dnorm.v(dnorm.h[0:TB_, l, :]), ALU.mult, ALU.mult)
                stage_(6.6)
                for kc in range(KC):
                    p = pb("g")
                    for blk in range(NB):
                        k.tr(p[:, blk * TB_:(blk + 1) * TB_], tokb[0:TB_, blk, kc * 128:(kc + 1) * 128], C_(0, TB_, TB_))
                    k.act(mixT[:, kc, 0:T], p[:, 0:T], AF.Copy)

                def ev_res(base):
                    def f(j, p):
                        jj = base + j
                        k.tt(hT[:, jj, 0:T], p[:, 0:T], hT[:, jj, 0:T], ALU.add)
                    return f
                for half in range(2):
                    wv = wget(l, "o%d" % half)
                    proj_fm(wv, 512, T, ev_res(half * 4), rhs=mixT)
                stage_(6.7)
                rmsnorm_fm(hT, lambda kc: gains.v(gains.h[:, l, 1, kc:kc + 1]), T, xn)
                mk_, mv_ = R(memk, "memk"), R(memv, "memv")
                k.dma(SP, mk_[:], V(mkT_sc.ap()[l, s].rearrange("p (c m) -> p c m", m=N_MEM), [b_mk[l][s]]))
                k.dma(SP, mv_[:], V(mv_sc.ap()[l, s].rearrange("(mb p) n -> p mb n", p=128), [b_mv[l][s]]))
                for half in range(2):
                    wv = wget(l, "mq%d" % half)
                    proj_fm(wv, 512, T, lambda j, p, half=half: k.act(qmT[:, half * 4 + j, 0:T], p[:, 0:T], AF.Copy, scale=1.0 / 16))
                for h in range(4):
                    pms = []
                    for mb in range(2):
                        p = pb("g")
                        for dc in range(2):
                            k.mm(p[:, 0:T], mk_.v(mk_.h[:, 2 * h + dc, mb * 128:(mb + 1) * 128]), qmT[:, 2 * h + dc, 0:T],
                                 start=(dc == 0), stop=(dc == 1), inc=(dc == 1))
                        pm_ = R(Pm, "Pm")
                        k.act(pm_[:, 0:T], p[:, 0:T], AF.Exp)
                        pms.append(pm_)
                    p = pb("g")
                    for mb in range(2):
                        k.mm(p[:, 0:T], onesb[:], pms[mb][:, 0:T], start=(mb == 0), stop=(mb == 1), inc=(mb == 1))
                    k.recip(rinv[:, 0:T], p[:, 0:T])
                    for dvc in range(2):
                        p = pb("g")
                        for mb in range(2):
                            k.mm(p[:, 0:T], mv_.v(mv_.h[:, mb, (2 * h + dvc) * 128:(2 * h + dvc + 1) * 128]), pms[mb][:, 0:T],
                                 start=(mb == 0), stop=(mb == 1), inc=(mb == 1))
                        k.tt(mixT[:, 2 * h + dvc, 0:T], p[:, 0:T], rinv[:, 0:T], ALU.mult)
                for half in range(2):
                    wv = wget(l, "mo%d" % half)
                    proj_fm(wv, 512, T, ev_res(half * 4), rhs=mixT)
                stage_(6.8)
                rmsnorm_fm(hT, lambda kc: gains.v(gains.h[:, l, 2, kc:kc + 1]), T, xn)
                for pc in range(8):
                    wv = wget(l, "f1_%d" % pc)

                    def ev_f1(j, p, pc=pc):
                        rt = R(rtmp, "rtmp")
                        k.act(rt[:, 0:T], p[:, 0:T], AF.Relu)
                        k.tt(h1[:, pc * 4 + j, 0:T], rt[:, 0:T], rt[:, 0:T], ALU.mult, eng="p")
                    proj_fm(wv, 512, T, ev_f1)
                for j in range(8):
                    wv = wget(l, "f2_%d" % j)
                    p = pb("g")
                    for kc in range(32):
                        k.mm(p[:, 0:T], wv(kc, 0, 128), h1[:, kc, 0:T], start=(kc == 0), stop=(kc == 31), inc=(kc == 31))
                    k.tt(hT[:, j, 0:T], p[:, 0:T], hT[:, j, 0:T], ALU.add)

            for (s, t, T) in tiles:
                is_s = s >= NPS
                L = DEC if is_s else SEQ
                TB_ = min(128, T)
                NB = T // TB_
                xin = inp("x_sample")[0] if is_s else inp("x_prompt")[s]
                for blk in range(NB):
                    k.dma(SP, tokb[0:TB_, blk, :], IN(xin[t * T + blk * TB_:t * T + (blk + 1) * TB_, :]))
                for kc in range(KC):
                    p = pb("g")
                    for blk in range(NB):
                        k.tr(p[:, blk * TB_:(blk + 1) * TB_], tokb[0:TB_, blk, kc * 128:(kc + 1) * 128], C_(0, TB_, TB_))
                    k.cp(hT[:, kc, 0:T], p[:, 0:T])
                nh = PAST if is_s else t * T
                for l in range(DEPTH):
                    tile_layer(l, s, t, T, nh, t == 0, t == L // T - 1)
                yT = kvf
                rmsnorm_fm(hT, lambda kc: gfin[:, kc:kc + 1], T, yT)
                for blk in range(NB):
                    for half in range(2):
                        p = pb("g")
                        for c4 in range(4):
                            kc = half * 4 + c4
                            k.tr(p[0:TB_, c4 * 128:(c4 + 1) * 128], yT[:, kc, blk * TB_:(blk + 1) * TB_], ident)
                        k.cp(tokb[0:TB_, blk, half * 512:(half + 1) * 512], p[0:TB_, :])
                    yo = O["y_sample"].ap()[0] if is_s else O["y_prompt"].ap()[s]
                    k.dma(POOL, OUT(yo[t * T + blk * TB_:t * T + (blk + 1) * TB_, :]), tokb[0:TB_, blk, :])
            assert ws["pos"] == len(order), (ws["pos"], len(order))
        except StopBuild:
            print('stopped early at', cfg.stop)
        final = {}
        for e in (PE, ACT, DVE, POOL):
            e.h.nop().then_inc(e.sem, 1)
            e.cnt += 1
            final[e.sid] = e.cnt
        for b in ALLBUFS:
            if b.dsem is not None:
                final[b.dsem[1]] = 16 * b.dcnt
        for sid, val in final.items():
            SP.h.wait_ge(k.sems[sid], val)
        print("instructions:", k.nins, "sems:", k.nsem)
    return nc


def make_consts():
    c = np.zeros((9, 128, 128), np.float32)
    i = np.arange(128)
    c[0] = np.eye(128)
    c[1] = (i[:, None] <= i[None, :])
    c[2] = BIG * (i[None, :] >= i[:, None])
    c[3] = -BIG * (i[None, :] <= i[:, None])
    c[4] = -BIG * (i[None, :] < i[:, None])
    c[5] = 1.0
    bi, bj = i[:, None], i[None, :]
    c[6] = (bi // 32 == bj // 32)
    c[7] = (bi // 64 == bj // 64) & (bi % 64 >= 32) & (bj % 64 < 32)
    c[8] = (bi >= 64) & (bj < 64)
    return c


def run(cfg, inputs, trace=False):
    nc = build(cfg)
    DEPTH, NPS = cfg.depth, cfg.nps
    f = lambda a: np.ascontiguousarray(np.asarray(a, dtype=np.float32))
    wnames = ["ln_mix", "w_in", "conv_w", "a_log", "dt_bias", "gdn_norm", "lambda_q1", "lambda_k1", "lambda_q2",
              "lambda_k2", "diff_norm", "w_out", "ln_mem_q", "ln_mem_kv", "w_mem_q", "w_mem_k", "w_mem_v", "w_mem_o",
              "ln_ffn", "w_ff1", "w_ff2"]
    shared = {n: f(inputs[n]) for n in wnames}
    shared["ln_final"] = f(inputs["ln_final"]).reshape(1, D)
    shared["consts"] = make_consts()
    in_maps = []
    for c in range(cfg.ncores):
        m = dict(shared)
        m["x_prompt"] = f(inputs["x_prompt"][c * NPS:(c + 1) * NPS])
        m["x_sample"] = f(inputs["x_sample"][c:c + 1])
        m["mem_prompt"] = f(inputs["mem_prompt"][c * NPS:(c + 1) * NPS])
        m["cache_diff_k"] = f(inputs["cache_diff_k"][:, c]).reshape(DEPTH, cfg.past, 512)
        m["cache_diff_v"] = f(inputs["cache_diff_v"][:, c]).reshape(DEPTH, cfg.past, 512)
        m["cache_mem_k"] = f(inputs["cache_mem_k"][:, c]).reshape(DEPTH, N_MEM, D)
        m["cache_mem_v"] = f(inputs["cache_mem_v"][:, c]).reshape(DEPTH, N_MEM, D)
        m["state_gdn"] = f(inputs["state_gdn"][:, c])
        m["state_gdn_conv"] = f(inputs["state_gdn_conv"][:, c])
        in_maps.append(m)
    res = run_bass_kernel_spmd(nc, in_maps, core_ids=list(range(cfg.ncores)))
    r = res.results
    cat0 = lambda n: np.concatenate([x[n] for x in r], axis=0)
    cat1 = lambda n: np.concatenate([x[n] for x in r], axis=1)
    B = cfg.ncores * NPS
    Bs = cfg.ncores
    return (cat0("y_prompt"), cat0("y_sample"),
            cat1("ndk_p").reshape(DEPTH, B, cfg.seq, 4, 128), cat1("ndv_p").reshape(DEPTH, B, cfg.seq, 4, 128),
            cat1("ngdn_p"), cat1("nconv_p"),
            cat1("nmk_p").reshape(DEPTH, B, N_MEM, 4, 256), cat1("nmv_p").reshape(DEPTH, B, N_MEM, 4, 256),
            cat1("ndk_s").reshape(DEPTH, Bs, cfg.dec, 4, 128), cat1("ndv_s").reshape(DEPTH, Bs, cfg.dec, 4, 128),
            cat1("ngdn_s"), cat1("nconv_s"))


def kernel(**inputs):
    return run(FULL, inputs)
```

```python
import math
import numpy as np
import concourse.bass as bass
import concourse.mybir as mybir
from concourse.bass_utils import run_bass_kernel_spmd
from contextlib import ExitStack

F32 = mybir.dt.float32
F32R = mybir.dt.float32r
BF16 = mybir.dt.bfloat16
AF = mybir.ActivationFunctionType
ALU = mybir.AluOpType

D = 1024
KC = 8
N_MEM = 256
CONV_CH = 1536
IN_COLS = 3592
D_FF = 4096
EPS = 1e-6
BIG = 30000.0
VW = 132
NSLOT = 3


class Cfg:
    def __init__(self, depth=4, seq=4096, nps=2, past=2048, dec=64, ncores=8, stop=99):
        self.depth, self.seq, self.nps, self.past, self.dec, self.ncores = depth, seq, nps, past, dec, ncores
        self.stop = stop


class StopBuild(Exception):
    pass


FULL = Cfg()

PIECES = ([("c0", "w_in", 0, 512), ("c1", "w_in", 512, 512), ("c2", "w_in", 1024, 512), ("z", "w_in", 1536, 512),
           ("ab", "w_in", 2048, 8), ("qd", "w_in", 2056, 512), ("kd", "w_in", 2568, 512), ("vd", "w_in", 3080, 512),
           ("o0", "w_out", 0, 512), ("o1", "w_out", 512, 512),
           ("mq0", "w_mem_q", 0, 512), ("mq1", "w_mem_q", 512, 512),
           ("mo0", "w_mem_o", 0, 512), ("mo1", "w_mem_o", 512, 512)]
          + [("f1_%d" % i, "w_ff1", 512 * i, 512) for i in range(8)]
          + [("f2_%d" % i, "w_ff2", 128 * i, 128) for i in range(8)]
          + [("mk0", "w_mem_k", 0, 512), ("mk1", "w_mem_k", 512, 512),
             ("mv0", "w_mem_v", 0, 512), ("mv1", "w_mem_v", 512, 512)])
PIDX = {p[0]: i for i, p in enumerate(PIECES)}
NPIECE = len(PIECES)
MAIN_ORDER = [p[0] for p in PIECES[:30]]
MEM_ORDER = ["mk0", "mk1", "mv0", "mv1"]


ALLBUFS = []


class Buf:
    __slots__ = ("name", "wr", "rds", "dsem", "dcnt")

    def __init__(self, name):
        self.name, self.wr, self.rds, self.dsem, self.dcnt = name, None, {}, None, 0
        ALLBUFS.append(self)


class V:
    __slots__ = ("ap", "bufs")

    def __init__(self, ap, bufs):
        self.ap, self.bufs = ap, bufs


class TB:
    def __init__(self, h, name, nsub=1, subs=None):
        self.h = h
        if subs is None:
            subs = [[Buf("%s.%d" % (name, i))] for i in range(nsub)]
        self.subs = subs
        self.nsub = len(subs)
        self.bufs = []
        for sl_ in subs:
            for b in sl_:
                if b not in self.bufs:
                    self.bufs.append(b)

    def __getitem__(self, idx):
        ap = self.h[idx]
        if self.nsub > 1 and isinstance(idx, tuple) and len(idx) >= 2 and isinstance(idx[1], int):
            return V(ap, self.subs[idx[1]])
        return V(ap, self.bufs)

    def v(self, ap, sub=None):
        return V(ap, self.bufs if sub is None else self.subs[sub])


def alias(tb, el0, shape, dt, nsub=1):
    n = 1
    for d_ in shape[1:]:
        n *= d_
    nb = n * (2 if dt == F32 else 1)
    flat = tb.h[:].rearrange("p a b -> p (a b)")[:, el0:el0 + nb]
    if dt == F32:
        flat = flat.bitcast(F32)
    if len(shape) == 3:
        ap = flat.rearrange("p (a b) -> p a b", b=shape[2])
    elif len(shape) == 4:
        ap = flat.rearrange("p (a b c) -> p a b c", b=shape[2], c=shape[3])
    else:
        ap = flat
    per = nb // nsub
    subs = []
    for i in range(nsub):
        lo, hi = el0 + i * per, el0 + (i + 1) * per - 1
        bl = []
        for j in range(lo // 512, hi // 512 + 1):
            bl.extend(tb.subs[j])
        subs.append(bl)
    return TB(ap, "alias", subs=subs)


class Eng:
    def __init__(self, name, h, sem, sid, selfraw):
        self.name, self.h, self.sem, self.sid, self.cnt, self.seen, self.selfraw = name, h, sem, sid, 0, {}, selfraw


class K:
    def __init__(self, nc, es):
        self.nc, self.es = nc, es
        self.sems = {}
        self.nsem = 0
        self.PE = self._eng("pe", nc.tensor, False)
        self.ACT = self._eng("act", nc.scalar, True)
        self.DVE = self._eng("dve", nc.vector, True)
        self.POOL = self._eng("pool", nc.gpsimd, True)
        self.SP = self._eng("sp", nc.sync, False)
        self.nins = 0

    def newsem(self, name):
        s = self.es.enter_context(self.nc.semaphore(name))
        self.nsem += 1
        sid = self.nsem
        self.sems[sid] = s
        return s, sid

    def _eng(self, name, h, selfraw):
        s, sid = self.newsem("e_" + name)
        return Eng(name, h, s, sid, selfraw)

    def sb(self, name, shape, dt, nsub=1):
        h = self.es.enter_context(self.nc.sbuf_tensor(name, list(shape), dt))
        return TB(h, name, nsub)

    def _wait(self, eng, need):
        for sid, val in need.items():
            if eng.seen.get(sid, 0) < val:
                eng.h.wait_ge(self.sems[sid], val)
                eng.seen[sid] = val

    def _deps(self, eng, rd, wr):
        need = {}

        def add(ev, raw):
            sid, val = ev
            if sid == eng.sid and not eng.selfraw:
                return
            if need.get(sid, 0) < val:
                need[sid] = val
        for b in rd:
            if b.wr is not None:
                add(b.wr, True)
        for b in wr:
            if b.wr is not None:
                add(b.wr, False)
            for sid, val in b.rds.items():
                add((sid, val), False)
        self._wait(eng, need)

    def op(self, eng, fn, ins, outs, inc=True):
        rd = [b for v in ins if isinstance(v, V) for b in v.bufs]
        wr = [b for v in outs for b in v.bufs]
        self._deps(eng, rd, wr)
        i = fn()
        self.nins += 1
        inc = True
        if inc:
            eng.cnt += 1
            i.then_inc(eng.sem, 1)
            ev = (eng.sid, eng.cnt)
        else:
            ev = (eng.sid, eng.cnt + 1)
        for b in rd:
            if b.rds.get(ev[0], 0) < ev[1]:
                b.rds[ev[0]] = ev[1]
        for b in wr:
            b.wr = ev
            b.rds = {}
        return i

    def dma(self, eng, out, in_, sembuf=None, slow=False):
        rd, wr = in_.bufs, out.bufs
        self._deps(eng, rd, wr)
        sb_ = sembuf if sembuf is not None else (wr[0] if wr else rd[0])
        if sb_.dsem is None:
            sb_.dsem = self.newsem("d_" + sb_.name.replace(".", "_"))
        sem, sid = sb_.dsem
        sb_.dcnt += 1
        kw = {}
        if slow:
            kw["allow_slow_non_contiguous"] = True
        eng.h.dma_start(out=out.ap, in_=in_.ap, **kw).then_inc(sem, 16)
        self.nins += 1
        ev = (sid, 16 * sb_.dcnt)
        for b in rd:
            if b.rds.get(ev[0], 0) < ev[1]:
                b.rds[ev[0]] = ev[1]
        for b in wr:
            b.wr = ev
            b.rds = {}

    def mm(self, out, lhsT, rhs, start=True, stop=True, inc=True):
        nc = self.nc
        return self.op(self.PE, lambda: nc.tensor.matmul(out.ap, lhsT=lhsT.ap, rhs=rhs.ap, start=start, stop=stop,
                                                         skip_group_check=True),
                       [lhsT, rhs], [out], inc=inc)

    def tr(self, out, in_, ident):
        nc = self.nc
        return self.op(self.PE, lambda: nc.tensor.transpose(out.ap, in_.ap, ident.ap), [in_, ident], [out])

    def act(self, out, in_, func, bias=None, scale=None, accum=None):
        nc = self.nc
        kw = {}
        ins = [in_]
        if bias is not None:
            kw["bias"] = bias.ap if isinstance(bias, V) else bias
            ins.append(bias)
        if scale is not None:
            kw["scale"] = scale.ap if isinstance(scale, V) else scale
            ins.append(scale)
        outs = [out]
        if accum is not None:
            kw["accum_out"] = accum.ap
            outs.append(accum)
        return self.op(self.ACT, lambda: nc.scalar.activation(out=out.ap, in_=in_.ap, func=func, **kw), ins, outs)

    def _e(self, eng):
        return self.DVE if eng == "v" else self.POOL

    def tt(self, out, a, b, op, eng="v"):
        e = self._e(eng)
        return self.op(e, lambda: e.h.tensor_tensor(out=out.ap, in0=a.ap, in1=b.ap, op=op), [a, b], [out])

    def ts(self, out, a, s1, op0, s2=None, op1=None, eng="v"):
        e = self._e(eng)
        kw = {}
        if op1 is not None:
            kw["op1"] = op1
        a1 = s1.ap if isinstance(s1, V) else s1
        a2 = s2.ap if isinstance(s2, V) else s2
        return self.op(e, lambda: e.h.tensor_scalar(out=out.ap, in0=a.ap, scalar1=a1, scalar2=a2, op0=op0, **kw),
                       [a, s1, s2], [out])

    def stt(self, out, a, s, b, op0, op1):
        e = self.DVE
        a1 = s.ap if isinstance(s, V) else s
        return self.op(e, lambda: e.h.scalar_tensor_tensor(out=out.ap, in0=a.ap, scalar=a1, in1=b.ap, op0=op0, op1=op1),
                       [a, s, b], [out])

    def cp(self, out, in_, eng="v"):
        e = self._e(eng)
        return self.op(e, lambda: e.h.tensor_copy(out=out.ap, in_=in_.ap), [in_], [out])

    def recip(self, out, in_):
        e = self.DVE
        return self.op(e, lambda: e.h.reciprocal(out=out.ap, in_=in_.ap), [in_], [out])

    def memset(self, out, val, eng="p"):
        e = self._e(eng)
        return self.op(e, lambda: e.h.memset(out.ap, val), [], [out])


def dram(nc, name, shape, dt, kind):
    return nc.dram_tensor(name, list(shape), dt, kind=kind)


def build(cfg):
    nc = bass.Bass("TRN2", target_bir_lowering=False)
    del ALLBUFS[:]
    DEPTH, SEQ, NPS, PAST, DEC = cfg.depth, cfg.seq, cfg.nps, cfg.past, cfg.dec
    NSEQ = NPS + 1
    HMAX = max(SEQ - 512, PAST, 128)

    def din(name, shape):
        return dram(nc, name, shape, F32, "ExternalInput")

    def dout(name, shape):
        return dram(nc, name, shape, F32, "ExternalOutput")
    I = {}
    I["x_prompt"] = din("x_prompt", [NPS, SEQ, D])
    I["x_sample"] = din("x_sample", [1, DEC, D])
    I["mem_prompt"] = din("mem_prompt", [NPS, N_MEM, D])
    I["cache_diff_k"] = din("cache_diff_k", [DEPTH, PAST, 512])
    I["cache_diff_v"] = din("cache_diff_v", [DEPTH, PAST, 512])
    I["cache_mem_k"] = din("cache_mem_k", [DEPTH, N_MEM, D])
    I["cache_mem_v"] = din("cache_mem_v", [DEPTH, N_MEM, D])
    I["state_gdn"] = din("state_gdn", [DEPTH, 4, 128, 128])
    I["state_gdn_conv"] = din("state_gdn_conv", [DEPTH, 3, CONV_CH])
    for nm, shp in [("ln_mix", [DEPTH, D]), ("w_in", [DEPTH, D, IN_COLS]), ("conv_w", [DEPTH, 4, CONV_CH]),
                    ("a_log", [DEPTH, 4]), ("dt_bias", [DEPTH, 4]), ("gdn_norm", [DEPTH, 128]),
                    ("lambda_q1", [DEPTH, 64]), ("lambda_k1", [DEPTH, 64]), ("lambda_q2", [DEPTH, 64]),
                    ("lambda_k2", [DEPTH, 64]), ("diff_norm", [DEPTH, 128]), ("w_out", [DEPTH, D, D]),
                    ("ln_mem_q", [DEPTH, D]), ("ln_mem_kv", [DEPTH, D]), ("w_mem_q", [DEPTH, D, D]),
                    ("w_mem_k", [DEPTH, D, D]), ("w_mem_v", [DEPTH, D, D]), ("w_mem_o", [DEPTH, D, D]),
                    ("ln_ffn", [DEPTH, D]), ("w_ff1", [DEPTH, D, D_FF]), ("w_ff2", [DEPTH, D_FF, D]),
                    ("ln_final", [1, D]), ("consts", [9, 128, 128])]:
        I[nm] = din(nm, shp)
    O = {}
    O["y_prompt"] = dout("y_prompt", [NPS, SEQ, D])
    O["y_sample"] = dout("y_sample", [1, DEC, D])
    O["ndk_p"] = dout("ndk_p", [DEPTH, NPS, SEQ, 512])
    O["ndv_p"] = dout("ndv_p", [DEPTH, NPS, SEQ, 512])
    O["ngdn_p"] = dout("ngdn_p", [DEPTH, NPS, 4, 128, 128])
    O["nconv_p"] = dout("nconv_p", [DEPTH, NPS, 3, CONV_CH])
    O["nmk_p"] = dout("nmk_p", [DEPTH, NPS, N_MEM, D])
    O["nmv_p"] = dout("nmv_p", [DEPTH, NPS, N_MEM, D])
    O["ndk_s"] = dout("ndk_s", [DEPTH, 1, DEC, 512])
    O["ndv_s"] = dout("ndv_s", [DEPTH, 1, DEC, 512])
    O["ngdn_s"] = dout("ngdn_s", [DEPTH, 1, 4, 128, 128])
    O["nconv_s"] = dout("nconv_s", [DEPTH, 1, 3, CONV_CH])
    HL = max(SEQ, PAST)
    wbf = dram(nc, "wbf", [DEPTH, NPIECE, 128, 4096], BF16, "Internal")
    khist = dram(nc, "khist", [DEPTH, NSEQ, 4, 128, HL], BF16, "Internal")
    vhist = dram(nc, "vhist", [DEPTH, NSEQ, HL, 4 * VW], BF16, "Internal")
    mkT_sc = dram(nc, "mkT_sc", [DEPTH, NSEQ, 128, 8 * N_MEM], BF16, "Internal")
    mv_sc = dram(nc, "mv_sc", [DEPTH, NSEQ, N_MEM, D], BF16, "Internal")

    es = ExitStack()
    with es:
        k = K(nc, es)
        PE, ACT, DVE, POOL, SP = k.PE, k.ACT, k.DVE, k.POOL, k.SP
        cst = k.sb("cst", [128, 9, 128], F32)
        onesb = k.sb("onesb", [128, 128], BF16)
        identb = k.sb("identb", [128, 128], BF16)
        gains = k.sb("gains", [128, DEPTH, 4, KC], F32)
        gfin = k.sb("gfin", [128, KC], F32)
        cw = k.sb("cw", [128, DEPTH, 12, 4], F32)
        dtb8 = k.sb("dtb8", [128, DEPTH, 8], F32)
        negA = k.sb("negA", [128, DEPTH, 4], F32)
        gnorm = k.sb("gnorm", [128, DEPTH, 128], F32)
        dnorm = k.sb("dnorm", [128, DEPTH, 128], F32)
        nlam = k.sb("nlam", [128, DEPTH], F32)
        lamt = k.sb("lamt", [128, 4, 64], F32)
        lams = k.sb("lams", [128, 4], F32)
        hT = k.sb("hT", [128, KC, 512], F32, nsub=KC)
        xn = k.sb("xn", [128, KC, 512], BF16, nsub=KC)
        Sst = k.sb("Sst", [128, DEPTH * 4, 128], F32, nsub=DEPTH * 4)
        Sbf = k.sb("Sbf", [128, 4, 128], BF16, nsub=4)
        halo = k.sb("halo", [128, DEPTH, 12, 3], F32, nsub=DEPTH)
        wslot = [k.sb("wslot%d" % i, [128, 4096], BF16) for i in range(NSLOT)]
        cin = [k.sb("cin%d" % i, [128, 515], F32) for i in range(2)]
        ctmp = [k.sb("ctmp%d" % i, [128, 512], F32) for i in range(2)]
        sqb = [k.sb("sqb%d" % i, [128, 512], BF16) for i in range(2)]
        lnv = k.sb("lnv", [128, 512], F32)
        rstd = k.sb("rstd", [128, 512], F32)
        kvf = k.sb("kvf", [128, 8, 512], F32, nsub=8)
        qkb = k.sb("qkb", [128, 8, 512], BF16, nsub=8)
        zg = k.sb("zg", [128, 4, 512], F32, nsub=4)
        vde = k.sb("vde", [128, 4, 4 * VW], BF16, nsub=4)
        stage = [k.sb("stage%d" % i, [128, 512], F32) for i in range(2)]
        h1 = k.sb("h1", [128, 32, 512], BF16, nsub=32)
        assert HMAX <= 3584
        kTh = [alias(h1, 0, [128, HMAX], BF16)]
        vh = [alias(h1, 3584, [128, HMAX // 128, VW], BF16)]
        tokb = alias(h1, 7296, [128, 4, D], F32, nsub=4)
        Pt = [k.sb("Pt%d" % i, [128, 2, 128], BF16) for i in range(3)]
        mixT = k.sb("mixT", [128, KC, 512], BF16, nsub=KC)
        qmT = qkb
        qdT = TB(mixT.h[:, 0:4, :], "qdT", subs=mixT.subs[0:4])
        qd1 = k.sb("qd1", [128, 4, 512], BF16, nsub=4)
        kdT = TB(mixT.h[:, 4:8, :], "kdT", subs=mixT.subs[4:8])
        memk = [k.sb("memk%d" % i, [128, 8, N_MEM], BF16) for i in range(1)]
        memv = [k.sb("memv%d" % i, [128, 2, D], BF16) for i in range(1)]
        Pm = [k.sb("Pm%d" % i, [128, 512], BF16) for i in range(4)]
        rinv = lnv
        rtmp = ctmp
        abx = k.sb("abx", [128, 8], F32)
        abm = k.sb("abm", [128, 8], F32)
        abl = k.sb("abl", [128, 8], F32)
        gtok = k.sb("gtok", [128, 4], F32)
        lnb = k.sb("lnb", [128, 4], F32)
        beta = k.sb("beta", [128, 4], F32)
        Gtok = k.sb("Gtok", [128, 4], F32)
        negG = k.sb("negG", [128, 4], F32)
        GpL = k.sb("GpL", [128, 4], F32)
        bexpG = k.sb("bexpG", [128, 4], F32)
        dl = k.sb("dl", [128, 4], F32)
        elast = k.sb("elast", [128, 4], F32)
        glt = k.sb("glt", [128, 4], F32)
        glb = k.sb("glb", [128, 4], F32)
        osb = [k.sb("osb%d" % i, [128, 128], F32) for i in range(2)]
        gbc = [k.sb("gbc%d" % i, [128, 128], F32) for i in range(2)]
        lbc = [k.sb("lbc%d" % i, [128, 128], F32) for i in range(2)]
        expGr = [k.sb("expGr%d" % i, [128, 128], F32) for i in range(2)]
        E2 = [k.sb("E2_%d" % i, [128, 128], F32) for i in range(2)]
        Eb = [k.sb("Eb_%d" % i, [128, 128], F32) for i in range(2)]
        E3 = [k.sb("E3_%d" % i, [128, 128], F32) for i in range(2)]
        Pk = [k.sb("Pk%d" % i, [128, 128], F32) for i in range(4)]
        Mk = [k.sb("Mk%d" % i, [128, 128], F32) for i in range(4)]
        Rk = [k.sb("Rk%d" % i, [128, 128], F32) for i in range(4)]
        TTb = [k.sb("TTb%d" % i, [128, 128], BF16) for i in range(2)]
        PkS = [k.sb("PkS%d" % i, [64, 64], F32) for i in range(4)]
        MkS = [k.sb("MkS%d" % i, [64, 64], F32) for i in range(4)]
        RkS = [k.sb("RkS%d" % i, [64, 64], F32) for i in range(4)]
        Ao1 = [k.sb("Ao1_%d" % i, [128, 128], F32) for i in range(2)]
        Ao2 = [k.sb("Ao2_%d" % i, [128, 128], F32) for i in range(2)]
        identR = k.sb("identR", [128, 128], F32R)
        qkT = [k.sb("qkT%d" % i, [128, 128], BF16) for i in range(2)]
        qgT = [k.sb("qgT%d" % i, [128, 128], BF16) for i in range(2)]
        kbg = [k.sb("kbg%d" % i, [128, 128], BF16) for i in range(2)]
        kdec = [k.sb("kdec%d" % i, [128, 128], BF16) for i in range(2)]
        bvt = [k.sb("bvt%d" % i, [128, 128], BF16) for i in range(2)]
        nwT = [k.sb("nwT%d" % i, [128, 128], BF16) for i in range(2)]
        vnb = [k.sb("vnb%d" % i, [128, 128], BF16) for i in range(2)]
        junk = [k.sb("junk%d" % i, [128, 128], F32) for i in range(2)]
        ss = [k.sb("ss%d" % i, [128, 1], F32) for i in range(2)]
        sl = [k.sb("sl%d" % i, [128, 1], F32) for i in range(2)]
        sr = [k.sb("sr%d" % i, [128, 1], F32) for i in range(2)]
        r0 = [k.sb("r0_%d" % i, [128, 1], F32) for i in range(2)]
        r1 = [k.sb("r1_%d" % i, [128, 1], F32) for i in range(2)]
        t0 = [k.sb("t0_%d" % i, [128, 128], F32) for i in range(2)]
        od = [k.sb("od%d" % i, [128, 128], F32) for i in range(2)]
        onecol = k.sb("onecol", [128, 16, 4, 4], BF16)
        psb = [TB(es.enter_context(nc.psum_tensor("ps%d" % i, [128, 512], F32)), "ps%d" % i) for i in range(8)]
        rr = {"g": 0, "h": 0, "a": 0}

        def pb(pool="g"):
            i = rr[pool]
            if pool == "a":
                rr[pool] = (i + 1) % 8
                return psb[i]
            rr[pool] = (i + 1) % 4
            return psb[i + (0 if pool == "g" else 4)]
        rot = {}

        def R(lst, key):
            i = rot.get(key, 0)
            rot[key] = (i + 1) % len(lst)
            return lst[i]

        def DV(t, name):
            return TB(t.ap() if hasattr(t, "ap") and callable(t.ap) else t, name)

        def dv(ap, buf):
            return V(ap, [buf])

        b_in = Buf("inputs")
        b_out = Buf("outputs")
        b_wbf = [Buf("wbf%d" % l) for l in range(DEPTH)]
        b_kh = [[Buf("kh%d_%d" % (l, s)) for s in range(NSEQ)] for l in range(DEPTH)]
        b_vh = [[Buf("vh%d_%d" % (l, s)) for s in range(NSEQ)] for l in range(DEPTH)]
        b_mk = [[Buf("mk%d_%d" % (l, s)) for s in range(NSEQ)] for l in range(DEPTH)]
        b_mv = [[Buf("mv%d_%d" % (l, s)) for s in range(NSEQ)] for l in range(DEPTH)]

        def inp(name):
            return I[name].ap()

        def IN(ap):
            return V(ap, [])

        def OUT(ap):
            return V(ap, [])

        ident = cst[:, 0, :]
        utri = cst[:, 1, :]

        def C_(i, P, F):
            return cst.v(cst.h[0:P, i, 0:F])

        def stage_(n):
            if cfg.stop <= n:
                raise StopBuild()
        try:
            k.dma(SP, cst[:], IN(inp("consts").rearrange("c p f -> p c f")))
            k.cp(onesb[:], cst[:, 5, :])
            k.cp(identb[:], cst[:, 0, :])
            for gi, nm in enumerate(["ln_mix", "ln_mem_q", "ln_ffn", "ln_mem_kv"]):
                for l in range(DEPTH):
                    k.dma(SP, gains.v(gains.h[:, l, gi, :]), IN(inp(nm)[l].rearrange("(kc p) -> p kc", p=128)), slow=True)
            k.dma(SP, gfin[:], IN(inp("ln_final").rearrange("o (kc p) -> p (o kc)", p=128)), slow=True)
            for l in range(DEPTH):
                for i in range(4):
                    k.dma(SP, cw.v(cw.h[:, l, :, i]), IN(inp("conv_w")[l, i].rearrange("(j p) -> p j", p=128)), slow=True)
            k.memset(dtb8[:], 0.0)
            k.dma(SP, dtb8.v(dtb8.h[:, :, 0:4]), IN(inp("dt_bias").partition_broadcast(128)), slow=True)
            k.dma(SP, negA[:], IN(inp("a_log").partition_broadcast(128)), slow=True)
            k.act(negA[:], negA[:], AF.Exp)
            k.ts(negA[:], negA[:], -1.0, ALU.mult)
            k.dma(SP, gnorm[:], IN(inp("gdn_norm").partition_broadcast(128)), slow=True)
            k.dma(SP, dnorm[:], IN(inp("diff_norm").partition_broadcast(128)), slow=True)
            for l in range(DEPTH):
                lam_init = 0.8 - 0.6 * math.exp(-0.3 * l)
                k.ts(dnorm.v(dnorm.h[:, l, :]), dnorm.v(dnorm.h[:, l, :]), 1.0 - lam_init, ALU.mult)
                for j, nm in enumerate(["lambda_q1", "lambda_k1", "lambda_q2", "lambda_k2"]):
                    k.dma(SP, lamt.v(lamt.h[:, j, :]), IN(inp(nm)[l].partition_broadcast(128)), slow=True)
                k.tt(lamt.v(lamt.h[:, 0, :]), lamt.v(lamt.h[:, 0, :]), lamt.v(lamt.h[:, 1, :]), ALU.mult)
                k.tt(lamt.v(lamt.h[:, 2, :]), lamt.v(lamt.h[:, 2, :]), lamt.v(lamt.h[:, 3, :]), ALU.mult)
                k.op(DVE, lambda: nc.vector.reduce_sum(out=lams.h[:, 0:1], in_=lamt.h[:, 0, :], axis=mybir.AxisListType.X),
                     [lamt[:]], [lams[:]])
                k.op(DVE, lambda: nc.vector.reduce_sum(out=lams.h[:, 1:2], in_=lamt.h[:, 2, :], axis=mybir.AxisListType.X),
                     [lamt[:]], [lams[:]])
                k.act(lams.v(lams.h[:, 2:4]), lams.v(lams.h[:, 0:2]), AF.Exp)
                k.tt(lams.v(lams.h[:, 0:1]), lams.v(lams.h[:, 3:4]), lams.v(lams.h[:, 2:3]), ALU.subtract)
                k.ts(nlam.v(nlam.h[:, l:l + 1]), lams.v(lams.h[:, 0:1]), -lam_init, ALU.add)
            k.memset(onecol[:], 0.0)
            k.memset(onecol.v(onecol.h[:, :, :, 0:1]), 1.0)
            k.memset(vde[:], 0.0)
            k.memset(qd1[:], 0.0)
            k.act(identR[:], cst[:, 0, :], AF.Copy)
            for b4 in range(4):
                k.memset(vde.v(vde.h[:, b4, :].rearrange("p (h c) -> p h c", c=VW)[:, :, 128:129]), 1.0)

            stage_(1)
            for l in range(DEPTH):
                for pi, (pn, src, c0, ncol) in enumerate(PIECES):
                    w = inp(src)[l]
                    if src == "w_ff2":
                        s_ap = w.rearrange("(kc p) n -> p kc n", p=128)[:, :, c0:c0 + ncol]
                        d_ap = wbf.ap()[l, pi].rearrange("p (kc n) -> p kc n", n=ncol)
                    else:
                        s_ap = w.rearrange("(kc p) n -> p kc n", p=128)[:, :, c0:c0 + ncol]
                        d_ap = wbf.ap()[l, pi][:, 0:8 * ncol].rearrange("p (kc n) -> p kc n", n=ncol)
                    k.dma(POOL, V(d_ap, [b_wbf[l]]), IN(s_ap), sembuf=b_wbf[l])

            stage_(2)
            order = []
            for s in range(NPS):
                for l in range(DEPTH):
                    for pn in MEM_ORDER:
                        order.append((l, pn))
            tiles = []
            for s in range(NSEQ):
                L = SEQ if s < NPS else DEC
                T = min(512, L)
                for t in range(L // T):
                    tiles.append((s, t, T))
            for (s, t, T) in tiles:
                for l in range(DEPTH):
                    for pn in MAIN_ORDER:
                        order.append((l, pn))
            ws = {"issued": 0, "pos": 0}

            def wissue():
                i = ws["issued"]
                if i >= len(order):
                    return
                l, pn = order[i]
                pi = PIDX[pn]
                ncol = PIECES[pi][3]
                nk = 32 if PIECES[pi][1] == "w_ff2" else 8
                slot = wslot[i % NSLOT]
                k.dma(SP, slot.v(slot.h[:, 0:nk * ncol]), V(wbf.ap()[l, pi][:, 0:nk * ncol], [b_wbf[l]]))
                ws["issued"] = i + 1

            def wget(l, pn):
                i = ws["pos"]
                assert order[i] == (l, pn), (order[i], l, pn)
                while ws["issued"] < min(i + NSLOT, len(order)):
                    wissue()
                ws["pos"] = i + 1
                slot = wslot[i % NSLOT]
                pi = PIDX[pn]
                ncol = PIECES[pi][3]
                nk = 32 if PIECES[pi][1] == "w_ff2" else 8
                view = slot.h[:, 0:nk * ncol].rearrange("p (kc n) -> p kc n", n=ncol)
                return lambda kc, c0, c1: slot.v(view[:, kc, c0:c1])

            def rmsnorm_fm(src, gain_fn, T, dst, dst_is_bf=True):
                p = pb("g")
                for kc in range(KC):
                    s = R(sqb, "sqb")
                    k.act(s[:, 0:T], src[:, kc, 0:T], AF.Square)
                    k.mm(p[:, 0:T], onesb[:], s[:, 0:T], start=(kc == 0), stop=(kc == KC - 1), inc=(kc == KC - 1))
                k.act(lnv[:, 0:T], p[:, 0:T], AF.Ln, bias=epsc[:, 0:1], scale=1.0 / D)
                k.act(rstd[:, 0:T], lnv[:, 0:T], AF.Exp, scale=-0.5)
                for kc in range(KC):
                    k.stt(dst[:, kc, 0:T], src[:, kc, 0:T], gain_fn(kc), rstd[:, 0:T], ALU.mult, ALU.mult)

            epsc = k.sb("epsc", [128, 2], F32)
            k.memset(epsc.v(epsc.h[:, 0:1]), EPS)
            k.memset(epsc.v(epsc.h[:, 1:2]), 1.0)

            def proj_fm(wv, ncols, T, evac, rhs=None):
                src = xn if rhs is None else rhs
                for j in range(ncols // 128):
                    p = pb("g")
                    for kc in range(KC):
                        k.mm(p[:, 0:T], wv(kc, j * 128, (j + 1) * 128), src[:, kc, 0:T], start=(kc == 0), stop=(kc == KC - 1),
                             inc=(kc == KC - 1))
                    evac(j, p)

            def proj_tm(wv, ncols, T, evac, src=None):
                src = xn if src is None else src
                TB_ = min(128, T)
                for blk in range(T // TB_):
                    p = pb("g")
                    for kc in range(KC):
                        k.mm(p[0:TB_, 0:ncols], src[:, kc, blk * TB_:(blk + 1) * TB_], wv(kc, 0, ncols), start=(kc == 0),
                             stop=(kc == KC - 1), inc=(kc == KC - 1))
                    evac(blk, TB_, p)

            memT = alias(h1, 0, [128, KC, N_MEM], F32, nsub=KC)
            mnb1 = alias(h1, 4096, [128, KC, N_MEM], BF16, nsub=KC)
            mnb = [mnb1 for _ in range(max(NPS, 1))]
            mrs1 = alias(h1, 6144, [128, N_MEM], F32)
            mrs = [mrs1 for _ in range(max(NPS, 1))]
            mkTs = TB(xn.h[:, 0:4, :].rearrange("p a (b m) -> p (a b) m", m=N_MEM), "mkTs", subs=[sum(xn.subs[0:4], [])])
            mvs = TB(xn.h[:, 4:8, :], "mvs", subs=[sum(xn.subs[4:8], [])])
            for s in range(NPS):
                for mb in range(2):
                    k.dma(SP, tokb[0:128, mb, :], IN(inp("mem_prompt")[s, mb * 128:(mb + 1) * 128, :]))
                stage_(2.5)
                for kc in range(KC):
                    p = pb("g")
                    for mb in range(2):
                        k.tr(p[:, mb * 128:(mb + 1) * 128], tokb[0:128, mb, kc * 128:(kc + 1) * 128], ident)
                    k.cp(memT[:, kc, :], p[:, 0:N_MEM])
                stage_(2.6)
                p = pb("g")
                for kc in range(KC):
                    sq = R(sqb, "sqb")
                    k.act(sq[:, 0:N_MEM], memT[:, kc, :], AF.Square)
                    k.mm(p[:, 0:N_MEM], onesb[:], sq[:, 0:N_MEM], start=(kc == 0), stop=(kc == KC - 1), inc=(kc == KC - 1))
                stage_(2.7)
                k.act(lnv[:, 0:N_MEM], p[:, 0:N_MEM], AF.Ln, bias=epsc[:, 0:1], scale=1.0 / D)
                stage_(2.8)
                k.act(mrs[s][:], lnv[:, 0:N_MEM], AF.Exp, scale=-0.5)
                stage_(3.1)
                for l in range(DEPTH):
                    for kc in range(KC):
                        k.stt(mnb[s][:, kc, :], memT[:, kc, :], gains.v(gains.h[:, l, 3, kc:kc + 1]), mrs[s][:], ALU.mult, ALU.mult)
                    stage_(3.2)
                    for half in range(2):
                        wv = wget(l, "mk%d" % half)
                        for mb in range(2):
                            p = pb("g")
                            for kc in range(KC):
                                k.mm(p[:, 0:512], mnb[s][:, kc, mb * 128:(mb + 1) * 128], wv(kc, 0, 512), start=(kc == 0),
                                     stop=(kc == KC - 1), inc=(kc == KC - 1))
                            st = R(stage, "stage")
                            k.cp(st[:], p[:])
                            k.dma(POOL, OUT(O["nmk_p"].ap()[l, s, mb * 128:(mb + 1) * 128, half * 512:(half + 1) * 512]), st[:])
                        for j in range(4):
                            p = pb("g")
                            for kc in range(KC):
                                k.mm(p[:, 0:N_MEM], wv(kc, j * 128, (j + 1) * 128), mnb[s][:, kc, :], start=(kc == 0),
                                     stop=(kc == KC - 1), inc=(kc == KC - 1))
                            k.act(mkTs.v(mkTs.h[:, half * 4 + j, :]), p[:, 0:N_MEM], AF.Copy)
                    stage_(3.4)
                    for half in range(2):
                        wv = wget(l, "mv%d" % half)
                        for mb in range(2):
                            p = pb("g")
                            for kc in range(KC):
                                k.mm(p[:, 0:512], mnb[s][:, kc, mb * 128:(mb + 1) * 128], wv(kc, 0, 512), start=(kc == 0),
                                     stop=(kc == KC - 1), inc=(kc == KC - 1))
                            st = R(stage, "stage")
                            k.cp(st[:], p[:])
                            k.dma(POOL, OUT(O["nmv_p"].ap()[l, s, mb * 128:(mb + 1) * 128, half * 512:(half + 1) * 512]), st[:])
                            k.act(mvs.v(mvs.h[:, mb * 2 + half, :]), st[:], AF.Copy)
                            stage_(3.45)
                    stage_(3.5)
                    k.dma(POOL, V(mkT_sc.ap()[l, s].rearrange("p (c m) -> p c m", m=N_MEM), [b_mk[l][s]]), mkTs[:])
                    for mb in range(2):
                        k.dma(POOL, V(mv_sc.ap()[l, s, mb * 128:(mb + 1) * 128, :], [b_mv[l][s]]),
                              mvs.v(mvs.h[:, 2 * mb:2 * mb + 2, :].rearrange("p a b -> p (a b)")))

            stage_(4)
            sS = NPS
            for l in range(DEPTH):
                k.dma(POOL, V(mv_sc.ap()[l, sS], [b_mv[l][sS]]), IN(inp("cache_mem_v")[l]), sembuf=b_mv[l][sS])
                for mb in range(2):
                    k.dma(SP, tokb[0:128, mb, :], IN(inp("cache_mem_k")[l, mb * 128:(mb + 1) * 128, :]))
                for kc in range(KC):
                    p = pb("g")
                    for mb in range(2):
                        k.tr(p[:, mb * 128:(mb + 1) * 128], tokb[0:128, mb, kc * 128:(kc + 1) * 128], ident)
                    k.act(mkTs.v(mkTs.h[:, kc, :]), p[:, 0:N_MEM], AF.Copy)
                k.dma(POOL, V(mkT_sc.ap()[l, sS].rearrange("p (c m) -> p c m", m=N_MEM), [b_mk[l][sS]]), mkTs[:])
                k.dma(POOL, V(vhist.ap()[l, sS, 0:PAST, :].rearrange("t (h c) -> t h c", c=VW)[:, :, 0:128], [b_vh[l][sS]]),
                      IN(inp("cache_diff_v")[l].rearrange("t (h c) -> t h c", c=128)), sembuf=b_vh[l][sS])
                for g0 in range(PAST // 128):
                    k.dma(POOL, V(vhist.ap()[l, sS, g0 * 128:(g0 + 1) * 128, :].rearrange("p (h c) -> p h c", c=VW)[:, :, 128:132],
                                  [b_vh[l][sS]]), onecol.v(onecol.h[:, 0, :, :]), sembuf=b_vh[l][sS], slow=True)
                for g0 in range(0, PAST // 128, 4):
                    for gb in range(4):
                        k.dma(SP, tokb[0:128, gb, 0:512], IN(inp("cache_diff_k")[l, (g0 + gb) * 128:(g0 + gb + 1) * 128, :]))
                    for h in range(4):
                        p = pb("g")
                        for gb in range(4):
                            k.tr(p[:, gb * 128:(gb + 1) * 128], tokb[0:128, gb, h * 128:(h + 1) * 128], ident)
                        k.act(kdT[:, h, :], p[:], AF.Copy)
                        k.dma(POOL, V(khist.ap()[l, sS, h, :, g0 * 128:(g0 + 4) * 128], [b_kh[l][sS]]), kdT[:, h, :])

            stage_(5)
            def gdn_chunk(l, s, blk, C, T):
                c0 = blk * C
                merge = (C == 128)
                Pk_, Mk_, Rk_ = (Pk, Mk, Rk) if merge else (PkS, MkS, RkS)

                def rv(v):
                    return V(v.ap.bitcast(F32R), v.bufs) if merge else v
                idm = identR[0:C, 0:C] if merge else C_(0, C, C)

                def head_gen(h):
                    par = h % 2
                    H = lambda lst: lst[par]
                    pp_ = {"P": 0, "M": 0, "R": 0}

                    def nxt(kind):
                        lst = {"P": Pk_, "M": Mk_, "R": Rk_}[kind]
                        i = pp_[kind]
                        pp_[kind] = 1 - i
                        return lst[2 * par + i]
                    gb_ = H(gbc)
                    lb_ = H(lbc)
                    k.ts(gb_[0:C, :], C_(5, C, 128), gtok[0:C, h:h + 1], ALU.mult)
                    k.ts(lb_[0:C, :], C_(5, C, 128), lnb[0:C, h:h + 1], ALU.mult)
                    yield
                    p = pb("a")
                    k.mm(p[:, 0:C], gb_[0:C, :], C_(1, C, C))
                    p2 = pb("a")
                    k.mm(p2[0:C, 0:C], gb_[0:C, 0:C], C_(1, C, C), start=True, stop=False)
                    k.mm(p2[0:C, 0:C], C_(0, C, C), C_(2, C, C), start=False, stop=True)
                    yield
                    eg = H(expGr)
                    k.act(eg[:, 0:C], p[:, 0:C], AF.Exp)
                    e2 = H(E2)
                    k.act(e2[0:C, 0:C], p2[0:C, 0:C], AF.Exp, bias=GpL[0:C, h:h + 1], scale=-1.0)
                    p3 = pb("a")
                    k.mm(p3[0:C, 0:C], gb_[0:C, 0:C], C_(1, C, C), start=True, stop=False)
                    k.mm(p3[0:C, 0:C], lb_[0:C, 0:C], C_(0, C, C), start=False, stop=False)
                    k.mm(p3[0:C, 0:C], C_(0, C, C), C_(3, C, C), start=False, stop=True)
                    p4 = pb("a")
                    k.mm(p4[0:C, 0:C], gb_[0:C, 0:C], C_(1, C, C), start=True, stop=False)
                    k.mm(p4[0:C, 0:C], C_(0, C, C), C_(4, C, C), start=False, stop=True)
                    yield
                    eb = H(Eb)
                    k.act(eb[0:C, 0:C], p3[0:C, 0:C], AF.Exp, bias=negG[0:C, h:h + 1])
                    e3 = H(E3)
                    k.act(e3[0:C, 0:C], p4[0:C, 0:C], AF.Exp, bias=negG[0:C, h:h + 1])
                    qg = H(qgT)
                    k.tt(qg[:, 0:C], qkb[:, h, c0:c0 + C], eg[:, 0:C], ALU.mult)
                    kT_ = qkb[:, 4 + h, c0:c0 + C]
                    pk = pb("a")
                    k.mm(pk[0:C, 0:C], kT_, kT_)
                    pq = pb("a")
                    k.mm(pq[0:C, 0:C], kT_, qkb[:, h, c0:c0 + C])
                    yield
                    P0 = nxt("P")
                    M0 = nxt("M")
                    k.stt(rv(P0[0:C, 0:C]), pk[0:C, 0:C], -1.0, e2[0:C, 0:C], ALU.mult, ALU.mult)
                    k.stt(rv(M0[0:C, 0:C]), pk[0:C, 0:C], -1.0, eb[0:C, 0:C], ALU.mult, ALU.mult)
                    qk_ = H(qkT)
                    k.tt(qk_[0:C, 0:C], pq[0:C, 0:C], e3[0:C, 0:C], ALU.mult)
                    yield
                    if merge:
                        P0r, M0r = nxt("P"), nxt("M")
                        a1, a2 = H(Ao1), H(Ao2)
                        k.tt(rv(P0r[:, :]), P0[:, :], C_(6, 128, 128), ALU.mult, eng="p")
                        k.tt(rv(M0r[:, :]), M0[:, :], C_(6, 128, 128), ALU.mult, eng="p")
                        k.tt(rv(a1[:, :]), P0[:, :], C_(7, 128, 128), ALU.mult, eng="p")
                        k.tt(rv(a2[:, :]), P0[:, :], C_(8, 128, 128), ALU.mult, eng="p")
                        P0, M0 = P0r, M0r
                    Rc = nxt("R")
                    k.tt(rv(Rc[0:C, 0:C]), M0[0:C, 0:C], C_(0, C, C), ALU.add, eng="p")
                    yield
                    nlev = 4 if merge else 5
                    Pc, Mc = P0, M0
                    for lev in range(1, nlev + 1):
                        pp = pb("a")
                        k.mm(pp[0:C, 0:C], rv(Mc[0:C, 0:C]), rv(Pc[0:C, 0:C]))
                        if lev < nlev:
                            pm = pb("a")
                            k.mm(pm[0:C, 0:C], rv(Pc[0:C, 0:C]), rv(Mc[0:C, 0:C]))
                        yield
                        Pn = nxt("P")
                        k.act(rv(Pn[0:C, 0:C]), pp[0:C, 0:C], AF.Copy)
                        if lev < nlev:
                            Mn = nxt("M")
                            k.cp(rv(Mn[0:C, 0:C]), pm[0:C, 0:C])
                        pr = pb("a")
                        k.mm(pr[0:C, 0:C], idm, rv(Rc[0:C, 0:C]), start=True, stop=False)
                        k.mm(pr[0:C, 0:C], rv(Pn[0:C, 0:C]), rv(Rc[0:C, 0:C]), start=False, stop=True)
                        yield
                        last = (lev == nlev)
                        if last and not merge:
                            Rn = H(TTb)
                            k.cp(Rn[0:C, 0:C], pr[0:C, 0:C])
                        else:
                            Rn = nxt("R")
                            k.cp(rv(Rn[0:C, 0:C]), pr[0:C, 0:C])
                        Pc = Pn
                        if lev < nlev:
                            Mc = Mn
                        Rc = Rn
                    if merge:
                        W = Rc
                        for mi, ao in enumerate((a1, a2)):
                            pt_ = pb("a")
                            k.op(k.PE, lambda: nc.tensor.transpose(pt_.h[:, 0:128].bitcast(F32R), W.h[:, :].bitcast(F32R),
                                                                   identR.h[:, :]), [W[:, :], identR[:, :]], [pt_[:, 0:128]])
                            py = pb("a")
                            k.mm(py[:, 0:128], rv(ao[:, :]), rv(W[:, :]))
                            yield
                            Tbd = nxt("P")
                            k.act(rv(Tbd[:, :]), pt_[:, 0:128], AF.Copy)
                            Ysb = nxt("M")
                            k.cp(rv(Ysb[:, :]), py[:, 0:128])
                            px = pb("a")
                            k.mm(px[:, 0:128], rv(Tbd[:, :]), rv(Ysb[:, :]))
                            yield
                            if mi == 0:
                                Wn = nxt("R")
                                k.tt(rv(Wn[:, :]), px[:, 0:128], W[:, :], ALU.add)
                            else:
                                Wn = H(TTb)
                                k.tt(Wn[:, :], px[:, 0:128], W[:, :], ALU.add)
                            W = Wn
                        Rc = W
                    TT = Rc
                    ptk = pb("a")
                    k.tr(ptk[0:C, 0:128], kvf[:, h, c0:c0 + C], ident)
                    ptv = pb("a")
                    k.tr(ptv[0:C, 0:128], kvf[:, 4 + h, c0:c0 + C], ident)
                    yield
                    kb_ = H(kbg)
                    kd_ = H(kdec)
                    k.ts(kb_[0:C, :], ptk[0:C, 0:128], bexpG[0:C, h:h + 1], ALU.mult)
                    k.ts(kd_[0:C, :], ptk[0:C, 0:128], elast[0:C, h:h + 1], ALU.mult)
                    bv_ = H(bvt)
                    k.ts(bv_[0:C, :], ptv[0:C, 0:128], beta[0:C, h:h + 1], ALU.mult)
                    pw = pb("a")
                    k.mm(pw[:, 0:C], kb_[0:C, :], TT[0:C, 0:C])
                    yield
                    nw_ = H(nwT)
                    k.act(nw_[:, 0:C], pw[:, 0:C], AF.Copy, scale=-1.0)
                    pv = pb("a")
                    k.mm(pv[0:C, 0:128], TT[0:C, 0:C], bv_[0:C, :], start=True, stop=False)
                    k.mm(pv[0:C, 0:128], nw_[:, 0:C], Sbf[:, h, :], start=False, stop=True)
                    yield
                    vn_ = H(vnb)
                    k.cp(vn_[0:C, :], pv[0:C, 0:128])
                    po = pb("a")
                    k.mm(po[0:C, 0:128], qg[:, 0:C], Sbf[:, h, :], start=True, stop=False)
                    k.mm(po[0:C, 0:128], qk_[0:C, 0:C], vn_[0:C, :], start=False, stop=True)
                    psn = pb("a")
                    k.mm(psn[:, 0:128], kd_[0:C, :], vn_[0:C, :])
                    yield
                    Sv = Sst[:, l * 4 + h, :]
                    k.stt(Sv, Sv, glt[:, h:h + 1], psn[:, 0:128], ALU.mult, ALU.add)
                    k.cp(Sbf[:, h, :], Sv, eng="p")
                    s1, s2, s3 = H(ss), H(sl), H(sr)
                    ob_ = H(osb)
                    k.cp(ob_[0:C, :], po[0:C, 0:128])
                    yield
                    k.act(H(junk)[0:C, :], ob_[0:C, :], AF.Square, accum=s1[0:C, :])
                    k.act(s2[0:C, :], s1[0:C, :], AF.Ln, bias=epsc[0:C, 0:1], scale=1.0 / 128)
                    k.act(s3[0:C, :], s2[0:C, :], AF.Exp, scale=-0.5)
                    yield
                    k.stt(tokb[0:C, blk, h * 128:(h + 1) * 128], ob_[0:C, :], s3[0:C, :], zg[0:C, blk, h * 128:(h + 1) * 128],
                          ALU.mult, ALU.mult)

                for pair in ((0, 1), (2, 3)):
                    gens = [head_gen(h) for h in pair]
                    while gens:
                        for g in list(gens):
                            try:
                                next(g)
                            except StopIteration:
                                gens.remove(g)

            def tile_layer(l, s, t, T, nh, first_tile, last_tile):
                is_s = (s >= NPS)
                TB_ = min(128, T)
                NB = T // TB_
                C = TB_
                tok0 = t * T
                okey = "_s" if is_s else "_p"
                so = 0 if is_s else s
                lam_init = 0.8 - 0.6 * math.exp(-0.3 * l)
                if first_tile:
                    if is_s:
                        for h in range(4):
                            k.dma(SP, Sst[:, l * 4 + h, :], IN(inp("state_gdn")[l, h]))
                        for i in range(3):
                            k.dma(SP, halo.v(halo.h[:, l, :, i], sub=l),
                                  IN(inp("state_gdn_conv")[l, i].rearrange("(j p) -> p j", p=128)), slow=True)
                    else:
                        for h in range(4):
                            k.memset(Sst[:, l * 4 + h, :], 0.0)
                        k.memset(halo.v(halo.h[:, l, :, :], sub=l), 0.0)
                for h in range(4):
                    k.cp(Sbf[:, h, :], Sst[:, l * 4 + h, :], eng="p")
                rmsnorm_fm(hT, lambda kc: gains.v(gains.h[:, l, 0, kc:kc + 1]), T, xn)
                stage_(6.1)
                for pc in range(3):
                    wv = wget(l, "c%d" % pc)

                    def ev_conv(jj, p, pc=pc):
                        j = pc * 4 + jj
                        ci = R(cin, "cin")
                        k.cp(ci[:, 0:3], halo.v(halo.h[:, l, j, :], sub=l), eng="p")
                        k.act(ci[:, 3:3 + T], p[:, 0:T], AF.Copy)
                        k.cp(halo.v(halo.h[:, l, j, :], sub=l), ci[:, T:T + 3], eng="p")
                        ct = R(ctmp, "ctmp")
                        k.ts(ct[:, 0:T], ci[:, 0:T], cw.v(cw.h[:, l, j, 0:1]), ALU.mult)
                        for i in range(1, 4):
                            k.stt(ct[:, 0:T], ci[:, i:i + T], cw.v(cw.h[:, l, j, i:i + 1]), ct[:, 0:T], ALU.mult, ALU.add)
                        if j >= 8:
                            k.act(kvf[:, j - 4, 0:T], ct[:, 0:T], AF.Silu)
                            return
                        k.act(ct[:, 0:T], ct[:, 0:T], AF.Silu)
                        sq = R(sqb, "sqb")
                        k.act(sq[:, 0:T], ct[:, 0:T], AF.Square)
                        p2 = pb("h")
                        k.mm(p2[:, 0:T], onesb[:], sq[:, 0:T])
                        k.act(lnv[:, 0:T], p2[:, 0:T], AF.Ln, bias=epsc[:, 0:1])
                        k.act(rstd[:, 0:T], lnv[:, 0:T], AF.Exp, scale=-0.5)
                        if j < 4:
                            k.stt(qkb[:, j, 0:T], ct[:, 0:T], 128.0 ** -0.5, rstd[:, 0:T], ALU.mult, ALU.mult)
                        else:
                            k.tt(kvf[:, j - 4, 0:T], ct[:, 0:T], rstd[:, 0:T], ALU.mult)
                            k.cp(qkb[:, j, 0:T], kvf[:, j - 4, 0:T], eng="p")
                    proj_fm(wv, 512, T, ev_conv)
                if last_tile:
                    for i in range(3):
                        k.dma(POOL, OUT(O["nconv" + okey].ap()[l, so, i].rearrange("(j p) -> p j", p=128)),
                              halo.v(halo.h[:, l, :, i], sub=l), slow=True)
                stage_(6.2)
                wv = wget(l, "z")

                def ev_z(blk, TBx, p):
                    k.act(zg[0:TBx, blk, :], p[0:TBx, 0:512], AF.Silu)
                    k.tt(zg.v(zg.h[0:TBx, blk, :].rearrange("p (h c) -> p h c", c=128), sub=blk),
                         zg.v(zg.h[0:TBx, blk, :].rearrange("p (h c) -> p h c", c=128), sub=blk),
                         gnorm.v(gnorm.h[0:TBx, l:l + 1, :].to_broadcast([TBx, 4, 128])), ALU.mult, eng="p")
                proj_tm(wv, 512, T, ev_z)
                stage_(6.3)
                wv_ab = wget(l, "ab")
                for blk in range(NB):
                    p = pb("g")
                    for kc in range(KC):
                        k.mm(p[0:C, 0:8], xn[:, kc, blk * C:(blk + 1) * C], wv_ab(kc, 0, 8), start=(kc == 0), stop=(kc == KC - 1),
                             inc=(kc == KC - 1))
                    k.tt(abx[0:C, :], p[0:C, 0:8], dtb8.v(dtb8.h[0:C, l, :]), ALU.add)
                    k.act(abm[0:C, :], abx[0:C, :], AF.Abs)
                    k.act(abm[0:C, :], abm[0:C, :], AF.Exp, scale=-1.0)
                    k.act(abl[0:C, :], abm[0:C, :], AF.Ln, bias=epsc[0:C, 1:2])
                    k.stt(gtok[0:C, :], abx[0:C, 0:4], 0.0, abl[0:C, 0:4], ALU.max, ALU.add)
                    k.tt(gtok[0:C, :], gtok[0:C, :], negA.v(negA.h[0:C, l, :]), ALU.mult)
                    k.stt(lnb[0:C, :], abx[0:C, 4:8], 0.0, abl[0:C, 4:8], ALU.min, ALU.subtract)
                    k.act(beta[0:C, :], lnb[0:C, :], AF.Exp)
                    p = pb("h")
                    k.mm(p[0:C, 0:4], C_(1, C, C), gtok[0:C, :])
                    k.mm(p[:, 8:12], C_(5, C, 128), gtok[0:C, :])
                    k.cp(Gtok[0:C, :], p[0:C, 0:4])
                    k.ts(negG[0:C, :], p[0:C, 0:4], -1.0, ALU.mult)
                    k.tt(GpL[0:C, :], p[0:C, 0:4], lnb[0:C, :], ALU.add)
                    k.cp(glb[:, :], p[:, 8:12])
                    k.act(glt[:, :], glb[:, :], AF.Exp)
                    k.tt(dl[0:C, :], p[0:C, 8:12], Gtok[0:C, :], ALU.subtract)
                    k.act(elast[0:C, :], dl[0:C, :], AF.Exp)
                    k.act(bexpG[0:C, :], GpL[0:C, :], AF.Exp)
                    gdn_chunk(l, s, blk, C, T)
                if last_tile:
                    for h in range(4):
                        k.dma(POOL, OUT(O["ngdn" + okey].ap()[l, so, h]), Sst[:, l * 4 + h, :])
                stage_(6.4)
                wv = wget(l, "qd")
                def ev_qd(j, p):
                    k.memset(qdT[64:128, j, 0:T], 0.0)
                    k.act(qdT[0:64, j, 0:T], p[0:64, 0:T], AF.Copy, scale=0.125)
                    k.act(qd1[64:128, j, 0:T], p[64:128, 0:T], AF.Copy, scale=0.125)
                proj_fm(wv, 512, T, ev_qd)
                wv = wget(l, "kd")

                def ev_kd(j, p):
                    k.act(kdT[:, j, 0:T], p[:, 0:T], AF.Copy)
                    if not is_s and not last_tile:
                        k.dma(POOL, V(khist.ap()[l, s, j, :, tok0:tok0 + T], [b_kh[l][s]]), kdT[:, j, 0:T])
                proj_fm(wv, 512, T, ev_kd)

                def ev_kd_tm(blk, TBx, p):
                    st = R(stage, "stage")
                    k.cp(st[0:TBx, :], p[0:TBx, 0:512])
                    k.dma(POOL, OUT(O["ndk" + okey].ap()[l, so, tok0 + blk * TBx:tok0 + (blk + 1) * TBx, :]), st[0:TBx, :])
                proj_tm(wv, 512, T, ev_kd_tm)
                wv = wget(l, "vd")

                def ev_vd(blk, TBx, p):
                    st = R(stage, "stage")
                    k.cp(st[0:TBx, :], p[0:TBx, 0:512])
                    k.dma(POOL, OUT(O["ndv" + okey].ap()[l, so, tok0 + blk * TBx:tok0 + (blk + 1) * TBx, :]), st[0:TBx, :])
                    k.act(vde.v(vde.h[0:TBx, blk, :].rearrange("p (h c) -> p h c", c=VW)[:, :, 0:128], sub=blk),
                          st.v(st.h[0:TBx, 0:512].rearrange("p (h c) -> p h c", c=128)), AF.Copy)
                    if not is_s and not last_tile:
                        k.dma(POOL, V(vhist.ap()[l, s, tok0 + blk * TBx:tok0 + (blk + 1) * TBx, :], [b_vh[l][s]]),
                              vde[0:TBx, blk, :])
                proj_tm(wv, 512, T, ev_vd)
                stage_(6.5)
                nhb = nh // 128
                for h in range(4):
                    kt_, vh_ = R(kTh, "kTh"), R(vh, "vh")
                    if nh > 0:
                        k.dma(SP, kt_[:, 0:nh], V(khist.ap()[l, s, h, :, 0:nh], [b_kh[l][s]]))
                        k.dma(SP, vh_[:, 0:nhb, :],
                              V(vhist.ap()[l, s, 0:nh, h * VW:(h + 1) * VW].rearrange("(g p) c -> p g c", p=128), [b_vh[l][s]]))
                    for qb in range(NB):
                        q0 = qb * TB_
                        acc = [pb("h"), pb("h")]
                        nkb = nhb + qb + 1
                        for kb in range(nkb):
                            ps_ = pb("g")
                            if kb < nhb:
                                KP = 128
                                for c in range(2):
                                    k.mm(ps_[0:128, c * 128:c * 128 + TB_], kt_[:, kb * 128:(kb + 1) * 128],
                                         (qdT if c == 0 else qd1)[:, h, q0:q0 + TB_])
                            else:
                                j = kb - nhb
                                KP = TB_
                                for c in range(2):
                                    k.mm(ps_[0:KP, c * 128:c * 128 + TB_], kdT[:, h, j * TB_:(j + 1) * TB_],
                                         (qdT if c == 0 else qd1)[:, h, q0:q0 + TB_])
                            pt = R(Pt, "Pt")
                            k.act(pt.v(pt.h[0:KP, :, 0:TB_]), ps_.v(ps_.h[0:KP, 0:256].rearrange("p (c q) -> p c q", q=128)[:, :, 0:TB_]),
                                  AF.Exp)
                            if kb == nkb - 1 and TB_ == 128:
                                k.memset(pt.v(pt.h[64:128, :, 0:64]), 0.0)
                            for c in range(2):
                                if kb < nhb:
                                    rhs = vh_[:, kb, 0:129]
                                else:
                                    rhs = vde.v(vde.h[0:KP, kb - nhb, h * VW:h * VW + 129], sub=kb - nhb)
                                k.mm(acc[c][0:TB_, 0:129], pt.v(pt.h[0:KP, c, 0:TB_]), rhs, start=(kb == 0), stop=(kb == nkb - 1),
                                     inc=(kb == nkb - 1))
                        a0, a1 = acc
                        rr0, rr1 = R(r0, "r0"), R(r1, "r1")
                        k.recip(rr0[0:TB_, :], a0[0:TB_, 128:129])
                        k.recip(rr1[0:TB_, :], a1[0:TB_, 128:129])
                        k.tt(rr1[0:TB_, :], rr1[0:TB_, :], nlam.v(nlam.h[0:TB_, l:l + 1]), ALU.mult)
                        tt0, odd = R(t0, "t0"), R(od, "od")
                        k.ts(tt0[0:TB_, :], a0[0:TB_, 0:128], rr0[0:TB_, :], ALU.mult)
                        k.stt(odd[0:TB_, :], a1[0:TB_, 0:128], rr1[0:TB_, :], tt0[0:TB_, :], ALU.mult, ALU.add)
                        s1, s2, s3 = R(ss, "ss"), R(sl, "sl"), R(sr, "sr")
                        k.act(junk[0][0:TB_, :], odd[0:TB_, :], AF.Square, accum=s1[0:TB_, :])
                        k.act(s2[0:TB_, :], s1[0:TB_, :], AF.Ln, bias=epsc[0:TB_, 0:1], scale=1.0 / 128)
                        k.act(s3[0:TB_, :], s2[0:TB_, :], AF.Exp, scale=-0.5)
                        k.stt(tokb[0:TB_, qb, 512 + h * 128:512 + (h + 1) * 128], odd[0:TB_, :], s3[0:TB_, :],
                              dnorm.v(dnorm.h[0:TB_, l, :]), ALU.mult, ALU.mult)
                stage_(6.6)
                for kc in range(KC):
                    p = pb("g")
                    for blk in range(NB):
                        k.tr(p[:, blk * TB_:(blk + 1) * TB_], tokb[0:TB_, blk, kc * 128:(kc + 1) * 128], C_(0, TB_, TB_))
                    k.act(mixT[:, kc, 0:T], p[:, 0:T], AF.Copy)

                def ev_res(base):
                    def f(j, p):
                        jj = base + j
                        k.tt(hT[:, jj, 0:T], p[:, 0:T], hT[:, jj, 0:T], ALU.add)
                    return f
                for half in range(2):
                    wv = wget(l, "o%d" % half)
                    proj_fm(wv, 512, T, ev_res(half * 4), rhs=mixT)
                stage_(6.7)
                rmsnorm_fm(hT, lambda kc: gains.v(gains.h[:, l, 1, kc:kc + 1]), T, xn)
                mk_, mv_ = R(memk, "memk"), R(memv, "memv")
                k.dma(SP, mk_[:], V(mkT_sc.ap()[l, s].rearrange("p (c m) -> p c m", m=N_MEM), [b_mk[l][s]]))
                k.dma(SP, mv_[:], V(mv_sc.ap()[l, s].rearrange("(mb p) n -> p mb n", p=128), [b_mv[l][s]]))
                for half in range(2):
                    wv = wget(l, "mq%d" % half)
                    proj_fm(wv, 512, T, lambda j, p, half=half: k.act(qmT[:, half * 4 + j, 0:T], p[:, 0:T], AF.Copy, scale=1.0 / 16))
                for h in range(4):
                    pms = []
                    for mb in range(2):
                        p = pb("g")
                        for dc in range(2):
                            k.mm(p[:, 0:T], mk_.v(mk_.h[:, 2 * h + dc, mb * 128:(mb + 1) * 128]), qmT[:, 2 * h + dc, 0:T],
                                 start=(dc == 0), stop=(dc == 1), inc=(dc == 1))
                        pm_ = R(Pm, "Pm")
                        k.act(pm_[:, 0:T], p[:, 0:T], AF.Exp)
                        pms.append(pm_)
                    p = pb("g")
                    for mb in range(2):
                        k.mm(p[:, 0:T], onesb[:], pms[mb][:, 0:T], start=(mb == 0), stop=(mb == 1), inc=(mb == 1))
                    k.recip(rinv[:, 0:T], p[:, 0:T])
                    for dvc in range(2):
                        p = pb("g")
                        for mb in range(2):
                            k.mm(p[:, 0:T], mv_.v(mv_.h[:, mb, (2 * h + dvc) * 128:(2 * h + dvc + 1) * 128]), pms[mb][:, 0:T],
                                 start=(mb == 0), stop=(mb == 1), inc=(mb == 1))
                        k.tt(mixT[:, 2 * h + dvc, 0:T], p[:, 0:T], rinv[:, 0:T], ALU.mult)
                for half in range(2):
                    wv = wget(l, "mo%d" % half)
                    proj_fm(wv, 512, T, ev_res(half * 4), rhs=mixT)
                stage_(6.8)
                rmsnorm_fm(hT, lambda kc: gains.v(gains.h[:, l, 2, kc:kc + 1]), T, xn)
                for pc in range(8):
                    wv = wget(l, "f1_%d" % pc)

                    def ev_f1(j, p, pc=pc):
                        rt = R(rtmp, "rtmp")
                        k.act(rt[:, 0:T], p[:, 0:T], AF.Relu)
                        k.tt(h1[:, pc * 4 + j, 0:T], rt[:, 0:T], rt[:, 0:T], ALU.mult, eng="p")
                    proj_fm(wv, 512, T, ev_f1)
                for j in range(8):
                    wv = wget(l, "f2_%d" % j)
                    p = pb("g")
                    for kc in range(32):
                        k.mm(p[:, 0:T], wv(kc, 0, 128), h1[:, kc, 0:T], start=(kc == 0), stop=(kc == 31), inc=(kc == 31))
                    k.tt(hT[:, j, 0:T], p[:, 0:T], hT[:, j, 0:T], ALU.add)

            for (s, t, T) in tiles:
                is_s = s >= NPS
                L = DEC if is_s else SEQ
                TB_ = min(128, T)
                NB = T // TB_
                xin = inp("x_sample")[0] if is_s else inp("x_prompt")[s]
                for blk in range(NB):
                    k.dma(SP, tokb[0:TB_, blk, :], IN(xin[t * T + blk * TB_:t * T + (blk + 1) * TB_, :]))
                for kc in range(KC):
                    p = pb("g")
                    for blk in range(NB):
                        k.tr(p[:, blk * TB_:(blk + 1) * TB_], tokb[0:TB_, blk, kc * 128:(kc + 1) * 128], C_(0, TB_, TB_))
                    k.cp(hT[:, kc, 0:T], p[:, 0:T])
                nh = PAST if is_s else t * T
                for l in range(DEPTH):
                    tile_layer(l, s, t, T, nh, t == 0, t == L // T - 1)
                yT = kvf
                rmsnorm_fm(hT, lambda kc: gfin[:, kc:kc + 1], T, yT)
                for blk in range(NB):
                    for half in range(2):
                        p = pb("g")
                        for c4 in range(4):
                            kc = half * 4 + c4
                            k.tr(p[0:TB_, c4 * 128:(c4 + 1) * 128], yT[:, kc, blk * TB_:(blk + 1) * TB_], ident)
                        k.cp(tokb[0:TB_, blk, half * 512:(half + 1) * 512], p[0:TB_, :])
                    yo = O["y_sample"].ap()[0] if is_s else O["y_prompt"].ap()[s]
                    k.dma(POOL, OUT(yo[t * T + blk * TB_:t * T + (blk + 1) * TB_, :]), tokb[0:TB_, blk, :])
            assert ws["pos"] == len(order), (ws["pos"], len(order))
        except StopBuild:
            print('stopped early at', cfg.stop)
        final = {}
        for e in (PE, ACT, DVE, POOL):
            e.h.nop().then_inc(e.sem, 1)
            e.cnt += 1
            final[e.sid] = e.cnt
        for b in ALLBUFS:
            if b.dsem is not None:
                final[b.dsem[1]] = 16 * b.dcnt
        for sid, val in final.items():
            SP.h.wait_ge(k.sems[sid], val)
        print("instructions:", k.nins, "sems:", k.nsem)
    return nc


def make_consts():
    c = np.zeros((9, 128, 128), np.float32)
    i = np.arange(128)
    c[0] = np.eye(128)
    c[1] = (i[:, None] <= i[None, :])
    c[2] = BIG * (i[None, :] >= i[:, None])
    c[3] = -BIG * (i[None, :] <= i[:, None])
    c[4] = -BIG * (i[None, :] < i[:, None])
    c[5] = 1.0
    bi, bj = i[:, None], i[None, :]
    c[6] = (bi // 32 == bj // 32)
    c[7] = (bi // 64 == bj // 64) & (bi % 64 >= 32) & (bj % 64 < 32)
    c[8] = (bi >= 64) & (bj < 64)
    return c


def run(cfg, inputs, trace=False):
    nc = build(cfg)
    DEPTH, NPS = cfg.depth, cfg.nps
    f = lambda a: np.ascontiguousarray(np.asarray(a, dtype=np.float32))
    wnames = ["ln_mix", "w_in", "conv_w", "a_log", "dt_bias", "gdn_norm", "lambda_q1", "lambda_k1", "lambda_q2",
              "lambda_k2", "diff_norm", "w_out", "ln_mem_q", "ln_mem_kv", "w_mem_q", "w_mem_k", "w_mem_v", "w_mem_o",
              "ln_ffn", "w_ff1", "w_ff2"]
    shared = {n: f(inputs[n]) for n in wnames}
    shared["ln_final"] = f(inputs["ln_final"]).reshape(1, D)
    shared["consts"] = make_consts()
    in_maps = []
    for c in range(cfg.ncores):
        m = dict(shared)
        m["x_prompt"] = f(inputs["x_prompt"][c * NPS:(c + 1) * NPS])
        m["x_sample"] = f(inputs["x_sample"][c:c + 1])
        m["mem_prompt"] = f(inputs["mem_prompt"][c * NPS:(c + 1) * NPS])
        m["cache_diff_k"] = f(inputs["cache_diff_k"][:, c]).reshape(DEPTH, cfg.past, 512)
        m["cache_diff_v"] = f(inputs["cache_diff_v"][:, c]).reshape(DEPTH, cfg.past, 512)
        m["cache_mem_k"] = f(inputs["cache_mem_k"][:, c]).reshape(DEPTH, N_MEM, D)
        m["cache_mem_v"] = f(inputs["cache_mem_v"][:, c]).reshape(DEPTH, N_MEM, D)
        m["state_gdn"] = f(inputs["state_gdn"][:, c])
        m["state_gdn_conv"] = f(inputs["state_gdn_conv"][:, c])
        in_maps.append(m)
    res = run_bass_kernel_spmd(nc, in_maps, core_ids=list(range(cfg.ncores)))
    r = res.results
    cat0 = lambda n: np.concatenate([x[n] for x in r], axis=0)
    cat1 = lambda n: np.concatenate([x[n] for x in r], axis=1)
    B = cfg.ncores * NPS
    Bs = cfg.ncores
    return (cat0("y_prompt"), cat0("y_sample"),
            cat1("ndk_p").reshape(DEPTH, B, cfg.seq, 4, 128), cat1("ndv_p").reshape(DEPTH, B, cfg.seq, 4, 128),
            cat1("ngdn_p"), cat1("nconv_p"),
            cat1("nmk_p").reshape(DEPTH, B, N_MEM, 4, 256), cat1("nmv_p").reshape(DEPTH, B, N_MEM, 4, 256),
            cat1("ndk_s").reshape(DEPTH, Bs, cfg.dec, 4, 128), cat1("ndv_s").reshape(DEPTH, Bs, cfg.dec, 4, 128),
            cat1("ngdn_s"), cat1("nconv_s"))


def kernel(**inputs):
    return run(FULL, inputs)
```

```python
import math
import numpy as np
import concourse.bass as bass
import concourse.mybir as mybir
from concourse.bass_utils import run_bass_kernel_spmd
from contextlib import ExitStack

F32 = mybir.dt.float32
F32R = mybir.dt.float32r
BF16 = mybir.dt.bfloat16
AF = mybir.ActivationFunctionType
ALU = mybir.AluOpType

D = 1024
KC = 8
N_MEM = 256
CONV_CH = 1536
IN_COLS = 3592
D_FF = 4096
EPS = 1e-6
BIG = 30000.0
VW = 132
NSLOT = 3


class Cfg:
    def __init__(self, depth=4, seq=4096, nps=2, past=2048, dec=64, ncores=8, stop=99):
        self.depth, self.seq, self.nps, self.past, self.dec, self.ncores = depth, seq, nps, past, dec, ncores
        self.stop = stop


class StopBuild(Exception):
    pass


FULL = Cfg()

PIECES = ([("c0", "w_in", 0, 512), ("c1", "w_in", 512, 512), ("c2", "w_in", 1024, 512), ("z", "w_in", 1536, 512),
           ("ab", "w_in", 2048, 8), ("qd", "w_in", 2056, 512), ("kd", "w_in", 2568, 512), ("vd", "w_in", 3080, 512),
           ("o0", "w_out", 0, 512), ("o1", "w_out", 512, 512),
           ("mq0", "w_mem_q", 0, 512), ("mq1", "w_mem_q", 512, 512),
           ("mo0", "w_mem_o", 0, 512), ("mo1", "w_mem_o", 512, 512)]
          + [("f1_%d" % i, "w_ff1", 512 * i, 512) for i in range(8)]
          + [("f2_%d" % i, "w_ff2", 128 * i, 128) for i in range(8)]
          + [("mk0", "w_mem_k", 0, 512), ("mk1", "w_mem_k", 512, 512),
             ("mv0", "w_mem_v", 0, 512), ("mv1", "w_mem_v", 512, 512)])
PIDX = {p[0]: i for i, p in enumerate(PIECES)}
NPIECE = len(PIECES)
MAIN_ORDER = [p[0] for p in PIECES[:30]]
MEM_ORDER = ["mk0", "mk1", "mv0", "mv1"]


ALLBUFS = []


class Buf:
    __slots__ = ("name", "wr", "rds", "dsem", "dcnt")

    def __init__(self, name):
        self.name, self.wr, self.rds, self.dsem, self.dcnt = name, None, {}, None, 0
        ALLBUFS.append(self)


class V:
    __slots__ = ("ap", "bufs")

    def __init__(self, ap, bufs):
        self.ap, self.bufs = ap, bufs


class TB:
    def __init__(self, h, name, nsub=1, subs=None):
        self.h = h
        if subs is None:
            subs = [[Buf("%s.%d" % (name, i))] for i in range(nsub)]
        self.subs = subs
        self.nsub = len(subs)
        self.bufs = []
        for sl_ in subs:
            for b in sl_:
                if b not in self.bufs:
                    self.bufs.append(b)

    def __getitem__(self, idx):
        ap = self.h[idx]
        if self.nsub > 1 and isinstance(idx, tuple) and len(idx) >= 2 and isinstance(idx[1], int):
            return V(ap, self.subs[idx[1]])
        return V(ap, self.bufs)

    def v(self, ap, sub=None):
        return V(ap, self.bufs if sub is None else self.subs[sub])


def alias(tb, el0, shape, dt, nsub=1):
    n = 1
    for d_ in shape[1:]:
        n *= d_
    nb = n * (2 if dt == F32 else 1)
    flat = tb.h[:].rearrange("p a b -> p (a b)")[:, el0:el0 + nb]
    if dt == F32:
        flat = flat.bitcast(F32)
    if len(shape) == 3:
        ap = flat.rearrange("p (a b) -> p a b", b=shape[2])
    elif len(shape) == 4:
        ap = flat.rearrange("p (a b c) -> p a b c", b=shape[2], c=shape[3])
    else:
        ap = flat
    per = nb // nsub
    subs = []
    for i in range(nsub):
        lo, hi = el0 + i * per, el0 + (i + 1) * per - 1
        bl = []
        for j in range(lo // 512, hi // 512 + 1):
            bl.extend(tb.subs[j])
        subs.append(bl)
    return TB(ap, "alias", subs=subs)


class Eng:
    def __init__(self, name, h, sem, sid, selfraw):
        self.name, self.h, self.sem, self.sid, self.cnt, self.seen, self.selfraw = name, h, sem, sid, 0, {}, selfraw


class K:
    def __init__(self, nc, es):
        self.nc, self.es = nc, es
        self.sems = {}
        self.nsem = 0
        self.PE = self._eng("pe", nc.tensor, False)
        self.ACT = self._eng("act", nc.scalar, True)
        self.DVE = self._eng("dve", nc.vector, True)
        self.POOL = self._eng("pool", nc.gpsimd, True)
        self.SP = self._eng("sp", nc.sync, False)
        self.nins = 0

    def newsem(self, name):
        s = self.es.enter_context(self.nc.semaphore(name))
        self.nsem += 1
        sid = self.nsem
        self.sems[sid] = s
        return s, sid

    def _eng(self, name, h, selfraw):
        s, sid = self.newsem("e_" + name)
        return Eng(name, h, s, sid, selfraw)

    def sb(self, name, shape, dt, nsub=1):
        h = self.es.enter_context(self.nc.sbuf_tensor(name, list(shape), dt))
        return TB(h, name, nsub)

    def _wait(self, eng, need):
        for sid, val in need.items():
            if eng.seen.get(sid, 0) < val:
                eng.h.wait_ge(self.sems[sid], val)
                eng.seen[sid] = val

    def _deps(self, eng, rd, wr):
        need = {}

        def add(ev, raw):
            sid, val = ev
            if sid == eng.sid and not eng.selfraw:
                return
            if need.get(sid, 0) < val:
                need[sid] = val
        for b in rd:
            if b.wr is not None:
                add(b.wr, True)
        for b in wr:
            if b.wr is not None:
                add(b.wr, False)
            for sid, val in b.rds.items():
                add((sid, val), False)
        self._wait(eng, need)

    def op(self, eng, fn, ins, outs, inc=True):
        rd = [b for v in ins if isinstance(v, V) for b in v.bufs]
        wr = [b for v in outs for b in v.bufs]
        self._deps(eng, rd, wr)
        i = fn()
        self.nins += 1
        inc = True
        if inc:
            eng.cnt += 1
            i.then_inc(eng.sem, 1)
            ev = (eng.sid, eng.cnt)
        else:
            ev = (eng.sid, eng.cnt + 1)
        for b in rd:
            if b.rds.get(ev[0], 0) < ev[1]:
                b.rds[ev[0]] = ev[1]
        for b in wr:
            b.wr = ev
            b.rds = {}
        return i

    def dma(self, eng, out, in_, sembuf=None, slow=False):
        rd, wr = in_.bufs, out.bufs
        self._deps(eng, rd, wr)
        sb_ = sembuf if sembuf is not None else (wr[0] if wr else rd[0])
        if sb_.dsem is None:
            sb_.dsem = self.newsem("d_" + sb_.name.replace(".", "_"))
        sem, sid = sb_.dsem
        sb_.dcnt += 1
        kw = {}
        if slow:
            kw["allow_slow_non_contiguous"] = True
        eng.h.dma_start(out=out.ap, in_=in_.ap, **kw).then_inc(sem, 16)
        self.nins += 1
        ev = (sid, 16 * sb_.dcnt)
        for b in rd:
            if b.rds.get(ev[0], 0) < ev[1]:
                b.rds[ev[0]] = ev[1]
        for b in wr:
            b.wr = ev
            b.rds = {}

    def mm(self, out, lhsT, rhs, start=True, stop=True, inc=True):
        nc = self.nc
        return self.op(self.PE, lambda: nc.tensor.matmul(out.ap, lhsT=lhsT.ap, rhs=rhs.ap, start=start, stop=stop,
                                                         skip_group_check=True),
                       [lhsT, rhs], [out], inc=inc)

    def tr(self, out, in_, ident):
        nc = self.nc
        return self.op(self.PE, lambda: nc.tensor.transpose(out.ap, in_.ap, ident.ap), [in_, ident], [out])

    def act(self, out, in_, func, bias=None, scale=None, accum=None):
        nc = self.nc
        kw = {}
        ins = [in_]
        if bias is not None:
            kw["bias"] = bias.ap if isinstance(bias, V) else bias
            ins.append(bias)
        if scale is not None:
            kw["scale"] = scale.ap if isinstance(scale, V) else scale
            ins.append(scale)
        outs = [out]
        if accum is not None:
            kw["accum_out"] = accum.ap
            outs.append(accum)
        return self.op(self.ACT, lambda: nc.scalar.activation(out=out.ap, in_=in_.ap, func=func, **kw), ins, outs)

    def _e(self, eng):
        return self.DVE if eng == "v" else self.POOL

    def tt(self, out, a, b, op, eng="v"):
        e = self._e(eng)
        return self.op(e, lambda: e.h.tensor_tensor(out=out.ap, in0=a.ap, in1=b.ap, op=op), [a, b], [out])

    def ts(self, out, a, s1, op0, s2=None, op1=None, eng="v"):
        e = self._e(eng)
        kw = {}
        if op1 is not None:
            kw["op1"] = op1
        a1 = s1.ap if isinstance(s1, V) else s1
        a2 = s2.ap if isinstance(s2, V) else s2
        return self.op(e, lambda: e.h.tensor_scalar(out=out.ap, in0=a.ap, scalar1=a1, scalar2=a2, op0=op0, **kw),
                       [a, s1, s2], [out])

    def stt(self, out, a, s, b, op0, op1):
        e = self.DVE
        a1 = s.ap if isinstance(s, V) else s
        return self.op(e, lambda: e.h.scalar_tensor_tensor(out=out.ap, in0=a.ap, scalar=a1, in1=b.ap, op0=op0, op1=op1),
                       [a, s, b], [out])

    def cp(self, out, in_, eng="v"):
        e = self._e(eng)
        return self.op(e, lambda: e.h.tensor_copy(out=out.ap, in_=in_.ap), [in_], [out])

    def recip(self, out, in_):
        e = self.DVE
        return self.op(e, lambda: e.h.reciprocal(out=out.ap, in_=in_.ap), [in_], [out])

    def memset(self, out, val, eng="p"):
        e = self._e(eng)
        return self.op(e, lambda: e.h.memset(out.ap, val), [], [out])


def dram(nc, name, shape, dt, kind):
    return nc.dram_tensor(name, list(shape), dt, kind=kind)


def build(cfg):
    nc = bass.Bass("TRN2", target_bir_lowering=False)
    del ALLBUFS[:]
    DEPTH, SEQ, NPS, PAST, DEC = cfg.depth, cfg.seq, cfg.nps, cfg.past, cfg.dec
    NSEQ = NPS + 1
    HMAX = max(SEQ - 512, PAST, 128)

    def din(name, shape):
        return dram(nc, name, shape, F32, "ExternalInput")

    def dout(name, shape):
        return dram(nc, name, shape, F32, "ExternalOutput")
    I = {}
    I["x_prompt"] = din("x_prompt", [NPS, SEQ, D])
    I["x_sample"] = din("x_sample", [1, DEC, D])
    I["mem_prompt"] = din("mem_prompt", [NPS, N_MEM, D])
    I["cache_diff_k"] = din("cache_diff_k", [DEPTH, PAST, 512])
    I["cache_diff_v"] = din("cache_diff_v", [DEPTH, PAST, 512])
    I["cache_mem_k"] = din("cache_mem_k", [DEPTH, N_MEM, D])
    I["cache_mem_v"] = din("cache_mem_v", [DEPTH, N_MEM, D])
    I["state_gdn"] = din("state_gdn", [DEPTH, 4, 128, 128])
    I["state_gdn_conv"] = din("state_gdn_conv", [DEPTH, 3, CONV_CH])
    for nm, shp in [("ln_mix", [DEPTH, D]), ("w_in", [DEPTH, D, IN_COLS]), ("conv_w", [DEPTH, 4, CONV_CH]),
                    ("a_log", [DEPTH, 4]), ("dt_bias", [DEPTH, 4]), ("gdn_norm", [DEPTH, 128]),
                    ("lambda_q1", [DEPTH, 64]), ("lambda_k1", [DEPTH, 64]), ("lambda_q2", [DEPTH, 64]),
                    ("lambda_k2", [DEPTH, 64]), ("diff_norm", [DEPTH, 128]), ("w_out", [DEPTH, D, D]),
                    ("ln_mem_q", [DEPTH, D]), ("ln_mem_kv", [DEPTH, D]), ("w_mem_q", [DEPTH, D, D]),
                    ("w_mem_k", [DEPTH, D, D]), ("w_mem_v", [DEPTH, D, D]), ("w_mem_o", [DEPTH, D, D]),
                    ("ln_ffn", [DEPTH, D]), ("w_ff1", [DEPTH, D, D_FF]), ("w_ff2", [DEPTH, D_FF, D]),
                    ("ln_final", [1, D]), ("consts", [9, 128, 128])]:
        I[nm] = din(nm, shp)
    O = {}
    O["y_prompt"] = dout("y_prompt", [NPS, SEQ, D])
    O["y_sample"] = dout("y_sample", [1, DEC, D])
    O["ndk_p"] = dout("ndk_p", [DEPTH, NPS, SEQ, 512])
    O["ndv_p"] = dout("ndv_p", [DEPTH, NPS, SEQ, 512])
    O["ngdn_p"] = dout("ngdn_p", [DEPTH, NPS, 4, 128, 128])
    O["nconv_p"] = dout("nconv_p", [DEPTH, NPS, 3, CONV_CH])
    O["nmk_p"] = dout("nmk_p", [DEPTH, NPS, N_MEM, D])
    O["nmv_p"] = dout("nmv_p", [DEPTH, NPS, N_MEM, D])
    O["ndk_s"] = dout("ndk_s", [DEPTH, 1, DEC, 512])
    O["ndv_s"] = dout("ndv_s", [DEPTH, 1, DEC, 512])
    O["ngdn_s"] = dout("ngdn_s", [DEPTH, 1, 4, 128, 128])
    O["nconv_s"] = dout("nconv_s", [DEPTH, 1, 3, CONV_CH])
    HL = max(SEQ, PAST)
    wbf = dram(nc, "wbf", [DEPTH, NPIECE, 128, 4096], BF16, "Internal")
    khist = dram(nc, "khist", [DEPTH, NSEQ, 4, 128, HL], BF16, "Internal")
    vhist = dram(nc, "vhist", [DEPTH, NSEQ, HL, 4 * VW], BF16, "Internal")
    mkT_sc = dram(nc, "mkT_sc", [DEPTH, NSEQ, 128, 8 * N_MEM], BF16, "Internal")
    mv_sc = dram(nc, "mv_sc", [DEPTH, NSEQ, N_MEM, D], BF16, "Internal")

    es = ExitStack()
    with es:
        k = K(nc, es)
        PE, ACT, DVE, POOL, SP = k.PE, k.ACT, k.DVE, k.POOL, k.SP
        cst = k.sb("cst", [128, 9, 128], F32)
        onesb = k.sb("onesb", [128, 128], BF16)
        identb = k.sb("identb", [128, 128], BF16)
        gains = k.sb("gains", [128, DEPTH, 4, KC], F32)
        gfin = k.sb("gfin", [128, KC], F32)
        cw = k.sb("cw", [128, DEPTH, 12, 4], F32)
        dtb8 = k.sb("dtb8", [128, DEPTH, 8], F32)
        negA = k.sb("negA", [128, DEPTH, 4], F32)
        gnorm = k.sb("gnorm", [128, DEPTH, 128], F32)
        dnorm = k.sb("dnorm", [128, DEPTH, 128], F32)
        nlam = k.sb("nlam", [128, DEPTH], F32)
        lamt = k.sb("lamt", [128, 4, 64], F32)
        lams = k.sb("lams", [128, 4], F32)
        hT = k.sb("hT", [128, KC, 512], F32, nsub=KC)
        xn = k.sb("xn", [128, KC, 512], BF16, nsub=KC)
        Sst = k.sb("Sst", [128, DEPTH * 4, 128], F32, nsub=DEPTH * 4)
        Sbf = k.sb("Sbf", [128, 4, 128], BF16, nsub=4)
        halo = k.sb("halo", [128, DEPTH, 12, 3], F32, nsub=DEPTH)
        wslot = [k.sb("wslot%d" % i, [128, 4096], BF16) for i in range(NSLOT)]
        cin = [k.sb("cin%d" % i, [128, 515], F32) for i in range(2)]
        ctmp = [k.sb("ctmp%d" % i, [128, 512], F32) for i in range(2)]
        sqb = [k.sb("sqb%d" % i, [128, 512], BF16) for i in range(2)]
        lnv = k.sb("lnv", [128, 512], F32)
        rstd = k.sb("rstd", [128, 512], F32)
        kvf = k.sb("kvf", [128, 8, 512], F32, nsub=8)
        qkb = k.sb("qkb", [128, 8, 512], BF16, nsub=8)
        zg = k.sb("zg", [128, 4, 512], F32, nsub=4)
        vde = k.sb("vde", [128, 4, 4 * VW], BF16, nsub=4)
        stage = [k.sb("stage%d" % i, [128, 512], F32) for i in range(2)]
        h1 = k.sb("h1", [128, 32, 512], BF16, nsub=32)
        assert HMAX <= 3584
        kTh = [alias(h1, 0, [128, HMAX], BF16)]
        vh = [alias(h1, 3584, [128, HMAX // 128, VW], BF16)]
        tokb = alias(h1, 7296, [128, 4, D], F32, nsub=4)
        Pt = [k.sb("Pt%d" % i, [128, 2, 128], BF16) for i in range(3)]
        mixT = k.sb("mixT", [128, KC, 512], BF16, nsub=KC)
        qmT = qkb
        qdT = TB(mixT.h[:, 0:4, :], "qdT", subs=mixT.subs[0:4])
        qd1 = k.sb("qd1", [128, 4, 512], BF16, nsub=4)
        kdT = TB(mixT.h[:, 4:8, :], "kdT", subs=mixT.subs[4:8])
        memk = [k.sb("memk%d" % i, [128, 8, N_MEM], BF16) for i in range(1)]
        memv = [k.sb("memv%d" % i, [128, 2, D], BF16) for i in range(1)]
        Pm = [k.sb("Pm%d" % i, [128, 512], BF16) for i in range(4)]
        rinv = lnv
        rtmp = ctmp
        abx = k.sb("abx", [128, 8], F32)
        abm = k.sb("abm", [128, 8], F32)
        abl = k.sb("abl", [128, 8], F32)
        gtok = k.sb("gtok", [128, 4], F32)
        lnb = k.sb("lnb", [128, 4], F32)
        beta = k.sb("beta", [128, 4], F32)
        Gtok = k.sb("Gtok", [128, 4], F32)
        negG = k.sb("negG", [128, 4], F32)
        GpL = k.sb("GpL", [128, 4], F32)
        bexpG = k.sb("bexpG", [128, 4], F32)
        dl = k.sb("dl", [128, 4], F32)
        elast = k.sb("elast", [128, 4], F32)
        glt = k.sb("glt", [128, 4], F32)
        glb = k.sb("glb", [128, 4], F32)
        osb = [k.sb("osb%d" % i, [128, 128], F32) for i in range(2)]
        gbc = [k.sb("gbc%d" % i, [128, 128], F32) for i in range(2)]
        lbc = [k.sb("lbc%d" % i, [128, 128], F32) for i in range(2)]
        expGr = [k.sb("expGr%d" % i, [128, 128], F32) for i in range(2)]
        E2 = [k.sb("E2_%d" % i, [128, 128], F32) for i in range(2)]
        Eb = [k.sb("Eb_%d" % i, [128, 128], F32) for i in range(2)]
        E3 = [k.sb("E3_%d" % i, [128, 128], F32) for i in range(2)]
        Pk = [k.sb("Pk%d" % i, [128, 128], F32) for i in range(4)]
        Mk = [k.sb("Mk%d" % i, [128, 128], F32) for i in range(4)]
        Rk = [k.sb("Rk%d" % i, [128, 128], F32) for i in range(4)]
        TTb = [k.sb("TTb%d" % i, [128, 128], BF16) for i in range(2)]
        PkS = [k.sb("PkS%d" % i, [64, 64], F32) for i in range(4)]
        MkS = [k.sb("MkS%d" % i, [64, 64], F32) for i in range(4)]
        RkS = [k.sb("RkS%d" % i, [64, 64], F32) for i in range(4)]
        Ao1 = [k.sb("Ao1_%d" % i, [128, 128], F32) for i in range(2)]
        Ao2 = [k.sb("Ao2_%d" % i, [128, 128], F32) for i in range(2)]
        identR = k.sb("identR", [128, 128], F32R)
        qkT = [k.sb("qkT%d" % i, [128, 128], BF16) for i in range(2)]
        qgT = [k.sb("qgT%d" % i, [128, 128], BF16) for i in range(2)]
        kbg = [k.sb("kbg%d" % i, [128, 128], BF16) for i in range(2)]
        kdec = [k.sb("kdec%d" % i, [128, 128], BF16) for i in range(2)]
        bvt = [k.sb("bvt%d" % i, [128, 128], BF16) for i in range(2)]
        nwT = [k.sb("nwT%d" % i, [128, 128], BF16) for i in range(2)]
        vnb = [k.sb("vnb%d" % i, [128, 128], BF16) for i in range(2)]
        junk = [k.sb("junk%d" % i, [128, 128], F32) for i in range(2)]
        ss = [k.sb("ss%d" % i, [128, 1], F32) for i in range(2)]
        sl = [k.sb("sl%d" % i, [128, 1], F32) for i in range(2)]
        sr = [k.sb("sr%d" % i, [128, 1], F32) for i in range(2)]
        r0 = [k.sb("r0_%d" % i, [128, 1], F32) for i in range(2)]
        r1 = [k.sb("r1_%d" % i, [128, 1], F32) for i in range(2)]
        t0 = [k.sb("t0_%d" % i, [128, 128], F32) for i in range(2)]
        od = [k.sb("od%d" % i, [128, 128], F32) for i in range(2)]
        onecol = k.sb("onecol", [128, 16, 4, 4], BF16)
        psb = [TB(es.enter_context(nc.psum_tensor("ps%d" % i, [128, 512], F32)), "ps%d" % i) for i in range(8)]
        rr = {"g": 0, "h": 0, "a": 0}

        def pb(pool="g"):
            i = rr[pool]
            if pool == "a":
                rr[pool] = (i + 1) % 8
                return psb[i]
            rr[pool] = (i + 1) % 4
            return psb[i + (0 if pool == "g" else 4)]
        rot = {}

        def R(lst, key):
            i = rot.get(key, 0)
            rot[key] = (i + 1) % len(lst)
            return lst[i]

        def DV(t, name):
            return TB(t.ap() if hasattr(t, "ap") and callable(t.ap) else t, name)

        def dv(ap, buf):
            return V(ap, [buf])

        b_in = Buf("inputs")
        b_out = Buf("outputs")
        b_wbf = [Buf("wbf%d" % l) for l in range(DEPTH)]
        b_kh = [[Buf("kh%d_%d" % (l, s)) for s in range(NSEQ)] for l in range(DEPTH)]
        b_vh = [[Buf("vh%d_%d" % (l, s)) for s in range(NSEQ)] for l in range(DEPTH)]
        b_mk = [[Buf("mk%d_%d" % (l, s)) for s in range(NSEQ)] for l in range(DEPTH)]
        b_mv = [[Buf("mv%d_%d" % (l, s)) for s in range(NSEQ)] for l in range(DEPTH)]

        def inp(name):
            return I[name].ap()

        def IN(ap):
            return V(ap, [])

        def OUT(ap):
            return V(ap, [])

        ident = cst[:, 0, :]
        utri = cst[:, 1, :]

        def C_(i, P, F):
            return cst.v(cst.h[0:P, i, 0:F])

        def stage_(n):
            if cfg.stop <= n:
                raise StopBuild()
        try:
            k.dma(SP, cst[:], IN(inp("consts").rearrange("c p f -> p c f")))
            k.cp(onesb[:], cst[:, 5, :])
            k.cp(identb[:], cst[:, 0, :])
            for gi, nm in enumerate(["ln_mix", "ln_mem_q", "ln_ffn", "ln_mem_kv"]):
                for l in range(DEPTH):
                    k.dma(SP, gains.v(gains.h[:, l, gi, :]), IN(inp(nm)[l].rearrange("(kc p) -> p kc", p=128)), slow=True)
            k.dma(SP, gfin[:], IN(inp("ln_final").rearrange("o (kc p) -> p (o kc)", p=128)), slow=True)
            for l in range(DEPTH):
                for i in range(4):
                    k.dma(SP, cw.v(cw.h[:, l, :, i]), IN(inp("conv_w")[l, i].rearrange("(j p) -> p j", p=128)), slow=True)
            k.memset(dtb8[:], 0.0)
            k.dma(SP, dtb8.v(dtb8.h[:, :, 0:4]), IN(inp("dt_bias").partition_broadcast(128)), slow=True)
            k.dma(SP, negA[:], IN(inp("a_log").partition_broadcast(128)), slow=True)
            k.act(negA[:], negA[:], AF.Exp)
            k.ts(negA[:], negA[:], -1.0, ALU.mult)
            k.dma(SP, gnorm[:], IN(inp("gdn_norm").partition_broadcast(128)), slow=True)
            k.dma(SP, dnorm[:], IN(inp("diff_norm").partition_broadcast(128)), slow=True)
            for l in range(DEPTH):
                lam_init = 0.8 - 0.6 * math.exp(-0.3 * l)
                k.ts(dnorm.v(dnorm.h[:, l, :]), dnorm.v(dnorm.h[:, l, :]), 1.0 - lam_init, ALU.mult)
                for j, nm in enumerate(["lambda_q1", "lambda_k1", "lambda_q2", "lambda_k2"]):
                    k.dma(SP, lamt.v(lamt.h[:, j, :]), IN(inp(nm)[l].partition_broadcast(128)), slow=True)
                k.tt(lamt.v(lamt.h[:, 0, :]), lamt.v(lamt.h[:, 0, :]), lamt.v(lamt.h[:, 1, :]), ALU.mult)
                k.tt(lamt.v(lamt.h[:, 2, :]), lamt.v(lamt.h[:, 2, :]), lamt.v(lamt.h[:, 3, :]), ALU.mult)
                k.op(DVE, lambda: nc.vector.reduce_sum(out=lams.h[:, 0:1], in_=lamt.h[:, 0, :], axis=mybir.AxisListType.X),
                     [lamt[:]], [lams[:]])
                k.op(DVE, lambda: nc.vector.reduce_sum(out=lams.h[:, 1:2], in_=lamt.h[:, 2, :], axis=mybir.AxisListType.X),
                     [lamt[:]], [lams[:]])
                k.act(lams.v(lams.h[:, 2:4]), lams.v(lams.h[:, 0:2]), AF.Exp)
                k.tt(lams.v(lams.h[:, 0:1]), lams.v(lams.h[:, 3:4]), lams.v(lams.h[:, 2:3]), ALU.subtract)
                k.ts(nlam.v(nlam.h[:, l:l + 1]), lams.v(lams.h[:, 0:1]), -lam_init, ALU.add)
            k.memset(onecol[:], 0.0)
            k.memset(onecol.v(onecol.h[:, :, :, 0:1]), 1.0)
            k.memset(vde[:], 0.0)
            k.memset(qd1[:], 0.0)
            k.act(identR[:], cst[:, 0, :], AF.Copy)
            for b4 in range(4):
                k.memset(vde.v(vde.h[:, b4, :].rearrange("p (h c) -> p h c", c=VW)[:, :, 128:129]), 1.0)

            stage_(1)
            for l in range(DEPTH):
                for pi, (pn, src, c0, ncol) in enumerate(PIECES):
                    w = inp(src)[l]
                    if src == "w_ff2":
                        s_ap = w.rearrange("(kc p) n -> p kc n", p=128)[:, :, c0:c0 + ncol]
                        d_ap = wbf.ap()[l, pi].rearrange("p (kc n) -> p kc n", n=ncol)
                    else:
                        s_ap = w.rearrange("(kc p) n -> p kc n", p=128)[:, :, c0:c0 + ncol]
                        d_ap = wbf.ap()[l, pi][:, 0:8 * ncol].rearrange("p (kc n) -> p kc n", n=ncol)
                    k.dma(POOL, V(d_ap, [b_wbf[l]]), IN(s_ap), sembuf=b_wbf[l])

            stage_(2)
            order = []
            for s in range(NPS):
                for l in range(DEPTH):
                    for pn in MEM_ORDER:
                        order.append((l, pn))
            tiles = []
            for s in range(NSEQ):
                L = SEQ if s < NPS else DEC
                T = min(512, L)
                for t in range(L // T):
                    tiles.append((s, t, T))
            for (s, t, T) in tiles:
                for l in range(DEPTH):
                    for pn in MAIN_ORDER:
                        order.append((l, pn))
            ws = {"issued": 0, "pos": 0}

            def wissue():
                i = ws["issued"]
                if i >= len(order):
                    return
                l, pn = order[i]
                pi = PIDX[pn]
                ncol = PIECES[pi][3]
                nk = 32 if PIECES[pi][1] == "w_ff2" else 8
                slot = wslot[i % NSLOT]
                k.dma(SP, slot.v(slot.h[:, 0:nk * ncol]), V(wbf.ap()[l, pi][:, 0:nk * ncol], [b_wbf[l]]))
                ws["issued"] = i + 1

            def wget(l, pn):
                i = ws["pos"]
                assert order[i] == (l, pn), (order[i], l, pn)
                while ws["issued"] < min(i + NSLOT, len(order)):
                    wissue()
                ws["pos"] = i + 1
                slot = wslot[i % NSLOT]
                pi = PIDX[pn]
                ncol = PIECES[pi][3]
                nk = 32 if PIECES[pi][1] == "w_ff2" else 8
                view = slot.h[:, 0:nk * ncol].rearrange("p (kc n) -> p kc n", n=ncol)
                return lambda kc, c0, c1: slot.v(view[:, kc, c0:c1])

            def rmsnorm_fm(src, gain_fn, T, dst, dst_is_bf=True):
                p = pb("g")
                for kc in range(KC):
                    s = R(sqb, "sqb")
                    k.act(s[:, 0:T], src[:, kc, 0:T], AF.Square)
                    k.mm(p[:, 0:T], onesb[:], s[:, 0:T], start=(kc == 0), stop=(kc == KC - 1), inc=(kc == KC - 1))
                k.act(lnv[:, 0:T], p[:, 0:T], AF.Ln, bias=epsc[:, 0:1], scale=1.0 / D)
                k.act(rstd[:, 0:T], lnv[:, 0:T], AF.Exp, scale=-0.5)
                for kc in range(KC):
                    k.stt(dst[:, kc, 0:T], src[:, kc, 0:T], gain_fn(kc), rstd[:, 0:T], ALU.mult, ALU.mult)

            epsc = k.sb("epsc", [128, 2], F32)
            k.memset(epsc.v(epsc.h[:, 0:1]), EPS)
            k.memset(epsc.v(epsc.h[:, 1:2]), 1.0)

            def proj_fm(wv, ncols, T, evac, rhs=None):
                src = xn if rhs is None else rhs
                for j in range(ncols // 128):
                    p = pb("g")
                    for kc in range(KC):
                        k.mm(p[:, 0:T], wv(kc, j * 128, (j + 1) * 128), src[:, kc, 0:T], start=(kc == 0), stop=(kc == KC - 1),
                             inc=(kc == KC - 1))
                    evac(j, p)

            def proj_tm(wv, ncols, T, evac, src=None):
                src = xn if src is None else src
                TB_ = min(128, T)
                for blk in range(T // TB_):
                    p = pb("g")
                    for kc in range(KC):
                        k.mm(p[0:TB_, 0:ncols], src[:, kc, blk * TB_:(blk + 1) * TB_], wv(kc, 0, ncols), start=(kc == 0),
                             stop=(kc == KC - 1), inc=(kc == KC - 1))
                    evac(blk, TB_, p)

            memT = alias(h1, 0, [128, KC, N_MEM], F32, nsub=KC)
            mnb1 = alias(h1, 4096, [128, KC, N_MEM], BF16, nsub=KC)
            mnb = [mnb1 for _ in range(max(NPS, 1))]
            mrs1 = alias(h1, 6144, [128, N_MEM], F32)
            mrs = [mrs1 for _ in range(max(NPS, 1))]
            mkTs = TB(xn.h[:, 0:4, :].rearrange("p a (b m) -> p (a b) m", m=N_MEM), "mkTs", subs=[sum(xn.subs[0:4], [])])
            mvs = TB(xn.h[:, 4:8, :], "mvs", subs=[sum(xn.subs[4:8], [])])
            for s in range(NPS):
                for mb in range(2):
                    k.dma(SP, tokb[0:128, mb, :], IN(inp("mem_prompt")[s, mb * 128:(mb + 1) * 128, :]))
                stage_(2.5)
                for kc in range(KC):
                    p = pb("g")
                    for mb in range(2):
                        k.tr(p[:, mb * 128:(mb + 1) * 128], tokb[0:128, mb, kc * 128:(kc + 1) * 128], ident)
                    k.cp(memT[:, kc, :], p[:, 0:N_MEM])
                stage_(2.6)
                p = pb("g")
                for kc in range(KC):
                    sq = R(sqb, "sqb")
                    k.act(sq[:, 0:N_MEM], memT[:, kc, :], AF.Square)
                    k.mm(p[:, 0:N_MEM], onesb[:], sq[:, 0:N_MEM], start=(kc == 0), stop=(kc == KC - 1), inc=(kc == KC - 1))
                stage_(2.7)
                k.act(lnv[:, 0:N_MEM], p[:, 0:N_MEM], AF.Ln, bias=epsc[:, 0:1], scale=1.0 / D)
                stage_(2.8)
                k.act(mrs[s][:], lnv[:, 0:N_MEM], AF.Exp, scale=-0.5)
                stage_(3.1)
                for l in range(DEPTH):
                    for kc in range(KC):
                        k.stt(mnb[s][:, kc, :], memT[:, kc, :], gains.v(gains.h[:, l, 3, kc:kc + 1]), mrs[s][:], ALU.mult, ALU.mult)
                    stage_(3.2)
                    for half in range(2):
                        wv = wget(l, "mk%d" % half)
                        for mb in range(2):
                            p = pb("g")
                            for kc in range(KC):
                                k.mm(p[:, 0:512], mnb[s][:, kc, mb * 128:(mb + 1) * 128], wv(kc, 0, 512), start=(kc == 0),
                                     stop=(kc == KC - 1), inc=(kc == KC - 1))
                            st = R(stage, "stage")
                            k.cp(st[:], p[:])
                            k.dma(POOL, OUT(O["nmk_p"].ap()[l, s, mb * 128:(mb + 1) * 128, half * 512:(half + 1) * 512]), st[:])
                        for j in range(4):
                            p = pb("g")
                            for kc in range(KC):
                                k.mm(p[:, 0:N_MEM], wv(kc, j * 128, (j + 1) * 128), mnb[s][:, kc, :], start=(kc == 0),
                                     stop=(kc == KC - 1), inc=(kc == KC - 1))
                            k.act(mkTs.v(mkTs.h[:, half * 4 + j, :]), p[:, 0:N_MEM], AF.Copy)
                    stage_(3.4)
                    for half in range(2):
                        wv = wget(l, "mv%d" % half)
                        for mb in range(2):
                            p = pb("g")
                            for kc in range(KC):
                                k.mm(p[:, 0:512], mnb[s][:, kc, mb * 128:(mb + 1) * 128], wv(kc, 0, 512), start=(kc == 0),
                                     stop=(kc == KC - 1), inc=(kc == KC - 1))
                            st = R(stage, "stage")
                            k.cp(st[:], p[:])
                            k.dma(POOL, OUT(O["nmv_p"].ap()[l, s, mb * 128:(mb + 1) * 128, half * 512:(half + 1) * 512]), st[:])
                            k.act(mvs.v(mvs.h[:, mb * 2 + half, :]), st[:], AF.Copy)
                            stage_(3.45)
                    stage_(3.5)
                    k.dma(POOL, V(mkT_sc.ap()[l, s].rearrange("p (c m) -> p c m", m=N_MEM), [b_mk[l][s]]), mkTs[:])
                    for mb in range(2):
                        k.dma(POOL, V(mv_sc.ap()[l, s, mb * 128:(mb + 1) * 128, :], [b_mv[l][s]]),
                              mvs.v(mvs.h[:, 2 * mb:2 * mb + 2, :].rearrange("p a b -> p (a b)")))

            stage_(4)
            sS = NPS
            for l in range(DEPTH):
                k.dma(POOL, V(mv_sc.ap()[l, sS], [b_mv[l][sS]]), IN(inp("cache_mem_v")[l]), sembuf=b_mv[l][sS])
                for mb in range(2):
                    k.dma(SP, tokb[0:128, mb, :], IN(inp("cache_mem_k")[l, mb * 128:(mb + 1) * 128, :]))
                for kc in range(KC):
                    p = pb("g")
                    for mb in range(2):
                        k.tr(p[:, mb * 128:(mb + 1) * 128], tokb[0:128, mb, kc * 128:(kc + 1) * 128], ident)
                    k.act(mkTs.v(mkTs.h[:, kc, :]), p[:, 0:N_MEM], AF.Copy)
                k.dma(POOL, V(mkT_sc.ap()[l, sS].rearrange("p (c m) -> p c m", m=N_MEM), [b_mk[l][sS]]), mkTs[:])
                k.dma(POOL, V(vhist.ap()[l, sS, 0:PAST, :].rearrange("t (h c) -> t h c", c=VW)[:, :, 0:128], [b_vh[l][sS]]),
                      IN(inp("cache_diff_v")[l].rearrange("t (h c) -> t h c", c=128)), sembuf=b_vh[l][sS])
                for g0 in range(PAST // 128):
                    k.dma(POOL, V(vhist.ap()[l, sS, g0 * 128:(g0 + 1) * 128, :].rearrange("p (h c) -> p h c", c=VW)[:, :, 128:132],
                                  [b_vh[l][sS]]), onecol.v(onecol.h[:, 0, :, :]), sembuf=b_vh[l][sS], slow=True)
                for g0 in range(0, PAST // 128, 4):
                    for gb in range(4):
                        k.dma(SP, tokb[0:128, gb, 0:512], IN(inp("cache_diff_k")[l, (g0 + gb) * 128:(g0 + gb + 1) * 128, :]))
                    for h in range(4):
                        p = pb("g")
                        for gb in range(4):
                            k.tr(p[:, gb * 128:(gb + 1) * 128], tokb[0:128, gb, h * 128:(h + 1) * 128], ident)
                        k.act(kdT[:, h, :], p[:], AF.Copy)
                        k.dma(POOL, V(khist.ap()[l, sS, h, :, g0 * 128:(g0 + 4) * 128], [b_kh[l][sS]]), kdT[:, h, :])

            stage_(5)
            def gdn_chunk(l, s, blk, C, T):
                c0 = blk * C
                merge = (C == 128)
                Pk_, Mk_, Rk_ = (Pk, Mk, Rk) if merge else (PkS, MkS, RkS)

                def rv(v):
                    return V(v.ap.bitcast(F32R), v.bufs) if merge else v
                idm = identR[0:C, 0:C] if merge else C_(0, C, C)

                def head_gen(h):
                    par = h % 2
                    H = lambda lst: lst[par]
                    pp_ = {"P": 0, "M": 0, "R": 0}

                    def nxt(kind):
                        lst = {"P": Pk_, "M": Mk_, "R": Rk_}[kind]
                        i = pp_[kind]
                        pp_[kind] = 1 - i
                        return lst[2 * par + i]
                    gb_ = H(gbc)
                    lb_ = H(lbc)
                    k.ts(gb_[0:C, :], C_(5, C, 128), gtok[0:C, h:h + 1], ALU.mult)
                    k.ts(lb_[0:C, :], C_(5, C, 128), lnb[0:C, h:h + 1], ALU.mult)
                    yield
                    p = pb("a")
                    k.mm(p[:, 0:C], gb_[0:C, :], C_(1, C, C))
                    p2 = pb("a")
                    k.mm(p2[0:C, 0:C], gb_[0:C, 0:C], C_(1, C, C), start=True, stop=False)
                    k.mm(p2[0:C, 0:C], C_(0, C, C), C_(2, C, C), start=False, stop=True)
                    yield
                    eg = H(expGr)
                    k.act(eg[:, 0:C], p[:, 0:C], AF.Exp)
                    e2 = H(E2)
                    k.act(e2[0:C, 0:C], p2[0:C, 0:C], AF.Exp, bias=GpL[0:C, h:h + 1], scale=-1.0)
                    p3 = pb("a")
                    k.mm(p3[0:C, 0:C], gb_[0:C, 0:C], C_(1, C, C), start=True, stop=False)
                    k.mm(p3[0:C, 0:C], lb_[0:C, 0:C], C_(0, C, C), start=False, stop=False)
                    k.mm(p3[0:C, 0:C], C_(0, C, C), C_(3, C, C), start=False, stop=True)
                    p4 = pb("a")
                    k.mm(p4[0:C, 0:C], gb_[0:C, 0:C], C_(1, C, C), start=True, stop=False)
                    k.mm(p4[0:C, 0:C], C_(0, C, C), C_(4, C, C), start=False, stop=True)
                    yield
                    eb = H(Eb)
                    k.act(eb[0:C, 0:C], p3[0:C, 0:C], AF.Exp, bias=negG[0:C, h:h + 1])
                    e3 = H(E3)
                    k.act(e3[0:C, 0:C], p4[0:C, 0:C], AF.Exp, bias=negG[0:C, h:h + 1])
                    qg = H(qgT)
                    k.tt(qg[:, 0:C], qkb[:, h, c0:c0 + C], eg[:, 0:C], ALU.mult)
                    kT_ = qkb[:, 4 + h, c0:c0 + C]
                    pk = pb("a")
                    k.mm(pk[0:C, 0:C], kT_, kT_)
                    pq = pb("a")
                    k.mm(pq[0:C, 0:C], kT_, qkb[:, h, c0:c0 + C])
                    yield
                    P0 = nxt("P")
                    M0 = nxt("M")
                    k.stt(rv(P0[0:C, 0:C]), pk[0:C, 0:C], -1.0, e2[0:C, 0:C], ALU.mult, ALU.mult)
                    k.stt(rv(M0[0:C, 0:C]), pk[0:C, 0:C], -1.0, eb[0:C, 0:C], ALU.mult, ALU.mult)
                    qk_ = H(qkT)
                    k.tt(qk_[0:C, 0:C], pq[0:C, 0:C], e3[0:C, 0:C], ALU.mult)
                    yield
                    if merge:
                        P0r, M0r = nxt("P"), nxt("M")
                        a1, a2 = H(Ao1), H(Ao2)
                        k.tt(rv(P0r[:, :]), P0[:, :], C_(6, 128, 128), ALU.mult, eng="p")
                        k.tt(rv(M0r[:, :]), M0[:, :], C_(6, 128, 128), ALU.mult, eng="p")
                        k.tt(rv(a1[:, :]), P0[:, :], C_(7, 128, 128), ALU.mult, eng="p")
                        k.tt(rv(a2[:, :]), P0[:, :], C_(8, 128, 128), ALU.mult, eng="p")
                        P0, M0 = P0r, M0r
                    Rc = nxt("R")
                    k.tt(rv(Rc[0:C, 0:C]), M0[0:C, 0:C], C_(0, C, C), ALU.add, eng="p")
                    yield
                    nlev = 4 if merge else 5
                    Pc, Mc = P0, M0
                    for lev in range(1, nlev + 1):
                        pp = pb("a")
                        k.mm(pp[0:C, 0:C], rv(Mc[0:C, 0:C]), rv(Pc[0:C, 0:C]))
                        if lev < nlev:
                            pm = pb("a")
                            k.mm(pm[0:C, 0:C], rv(Pc[0:C, 0:C]), rv(Mc[0:C, 0:C]))
                        yield
                        Pn = nxt("P")
                        k.act(rv(Pn[0:C, 0:C]), pp[0:C, 0:C], AF.Copy)
                        if lev < nlev:
                            Mn = nxt("M")
                            k.cp(rv(Mn[0:C, 0:C]), pm[0:C, 0:C])
                        pr = pb("a")
                        k.mm(pr[0:C, 0:C], idm, rv(Rc[0:C, 0:C]), start=True, stop=False)
                        k.mm(pr[0:C, 0:C], rv(Pn[0:C, 0:C]), rv(Rc[0:C, 0:C]), start=False, stop=True)
                        yield
                        last = (lev == nlev)
                        if last and not merge:
                            Rn = H(TTb)
                            k.cp(Rn[0:C, 0:C], pr[0:C, 0:C])
                        else:
                            Rn = nxt("R")
                            k.cp(rv(Rn[0:C, 0:C]), pr[0:C, 0:C])
                        Pc = Pn
                        if lev < nlev:
                            Mc = Mn
                        Rc = Rn
                    if merge:
                        W = Rc
                        for mi, ao in enumerate((a1, a2)):
                            pt_ = pb("a")
                            k.op(k.PE, lambda: nc.tensor.transpose(pt_.h[:, 0:128].bitcast(F32R), W.h[:, :].bitcast(F32R),
                                                                   identR.h[:, :]), [W[:, :], identR[:, :]], [pt_[:, 0:128]])
                            py = pb("a")
                            k.mm(py[:, 0:128], rv(ao[:, :]), rv(W[:, :]))
                            yield
                            Tbd = nxt("P")
                            k.act(rv(Tbd[:, :]), pt_[:, 0:128], AF.Copy)
                            Ysb = nxt("M")
                            k.cp(rv(Ysb[:, :]), py[:, 0:128])
                            px = pb("a")
                            k.mm(px[:, 0:128], rv(Tbd[:, :]), rv(Ysb[:, :]))
                            yield
                            if mi == 0:
                                Wn = nxt("R")
                                k.tt(rv(Wn[:, :]), px[:, 0:128], W[:, :], ALU.add)
                            else:
                                Wn = H(TTb)
                                k.tt(Wn[:, :], px[:, 0:128], W[:, :], ALU.add)
                            W = Wn
                        Rc = W
                    TT = Rc
                    ptk = pb("a")
                    k.tr(ptk[0:C, 0:128], kvf[:, h, c0:c0 + C], ident)
                    ptv = pb("a")
                    k.tr(ptv[0:C, 0:128], kvf[:, 4 + h, c0:c0 + C], ident)
                    yield
                    kb_ = H(kbg)
                    kd_ = H(kdec)
                    k.ts(kb_[0:C, :], ptk[0:C, 0:128], bexpG[0:C, h:h + 1], ALU.mult)
                    k.ts(kd_[0:C, :], ptk[0:C, 0:128], elast[0:C, h:h + 1], ALU.mult)
                    bv_ = H(bvt)
                    k.ts(bv_[0:C, :], ptv[0:C, 0:128], beta[0:C, h:h + 1], ALU.mult)
                    pw = pb("a")
                    k.mm(pw[:, 0:C], kb_[0:C, :], TT[0:C, 0:C])
                    yield
                    nw_ = H(nwT)
                    k.act(nw_[:, 0:C], pw[:, 0:C], AF.Copy, scale=-1.0)
                    pv = pb("a")
                    k.mm(pv[0:C, 0:128], TT[0:C, 0:C], bv_[0:C, :], start=True, stop=False)
                    k.mm(pv[0:C, 0:128], nw_[:, 0:C], Sbf[:, h, :], start=False, stop=True)
                    yield
                    vn_ = H(vnb)
                    k.cp(vn_[0:C, :], pv[0:C, 0:128])
                    po = pb("a")
                    k.mm(po[0:C, 0:128], qg[:, 0:C], Sbf[:, h, :], start=True, stop=False)
                    k.mm(po[0:C, 0:128], qk_[0:C, 0:C], vn_[0:C, :], start=False, stop=True)
                    psn = pb("a")
                    k.mm(psn[:, 0:128], kd_[0:C, :], vn_[0:C, :])
                    yield
                    Sv = Sst[:, l * 4 + h, :]
                    k.stt(Sv, Sv, glt[:, h:h + 1], psn[:, 0:128], ALU.mult, ALU.add)
                    k.cp(Sbf[:, h, :], Sv, eng="p")
                    s1, s2, s3 = H(ss), H(sl), H(sr)
                    ob_ = H(osb)
                    k.cp(ob_[0:C, :], po[0:C, 0:128])
                    yield
                    k.act(H(junk)[0:C, :], ob_[0:C, :], AF.Square, accum=s1[0:C, :])
                    k.act(s2[0:C, :], s1[0:C, :], AF.Ln, bias=epsc[0:C, 0:1], scale=1.0 / 128)
                    k.act(s3[0:C, :], s2[0:C, :], AF.Exp, scale=-0.5)
                    yield
                    k.stt(tokb[0:C, blk, h * 128:(h + 1) * 128], ob_[0:C, :], s3[0:C, :], zg[0:C, blk, h * 128:(h + 1) * 128],
                          ALU.mult, ALU.mult)

                for pair in ((0, 1), (2, 3)):
                    gens = [head_gen(h) for h in pair]
                    while gens:
                        for g in list(gens):
                            try:
                                next(g)
                            except StopIteration:
                                gens.remove(g)

            def tile_layer(l, s, t, T, nh, first_tile, last_tile):
                is_s = (s >= NPS)
                TB_ = min(128, T)
                NB = T // TB_
                C = TB_
                tok0 = t * T
                okey = "_s" if is_s else "_p"
                so = 0 if is_s else s
                lam_init = 0.8 - 0.6 * math.exp(-0.3 * l)
                if first_tile:
                    if is_s:
                        for h in range(4):
                            k.dma(SP, Sst[:, l * 4 + h, :], IN(inp("state_gdn")[l, h]))
                        for i in range(3):
                            k.dma(SP, halo.v(halo.h[:, l, :, i], sub=l),
                                  IN(inp("state_gdn_conv")[l, i].rearrange("(j p) -> p j", p=128)), slow=True)
                    else:
                        for h in range(4):
                            k.memset(Sst[:, l * 4 + h, :], 0.0)
                        k.memset(halo.v(halo.h[:, l, :, :], sub=l), 0.0)
                for h in range(4):
                    k.cp(Sbf[:, h, :], Sst[:, l * 4 + h, :], eng="p")
                rmsnorm_fm(hT, lambda kc: gains.v(gains.h[:, l, 0, kc:kc + 1]), T, xn)
                stage_(6.1)
                for pc in range(3):
                    wv = wget(l, "c%d" % pc)

                    def ev_conv(jj, p, pc=pc):
                        j = pc * 4 + jj
                        ci = R(cin, "cin")
                        k.cp(ci[:, 0:3], halo.v(halo.h[:, l, j, :], sub=l), eng="p")
                        k.act(ci[:, 3:3 + T], p[:, 0:T], AF.Copy)
                        k.cp(halo.v(halo.h[:, l, j, :], sub=l), ci[:, T:T + 3], eng="p")
                        ct = R(ctmp, "ctmp")
                        k.ts(ct[:, 0:T], ci[:, 0:T], cw.v(cw.h[:, l, j, 0:1]), ALU.mult)
                        for i in range(1, 4):
                            k.stt(ct[:, 0:T], ci[:, i:i + T], cw.v(cw.h[:, l, j, i:i + 1]), ct[:, 0:T], ALU.mult, ALU.add)
                        if j >= 8:
                            k.act(kvf[:, j - 4, 0:T], ct[:, 0:T], AF.Silu)
                            return
                        k.act(ct[:, 0:T], ct[:, 0:T], AF.Silu)
                        sq = R(sqb, "sqb")
                        k.act(sq[:, 0:T], ct[:, 0:T], AF.Square)
                        p2 = pb("h")
                        k.mm(p2[:, 0:T], onesb[:], sq[:, 0:T])
                        k.act(lnv[:, 0:T], p2[:, 0:T], AF.Ln, bias=epsc[:, 0:1])
                        k.act(rstd[:, 0:T], lnv[:, 0:T], AF.Exp, scale=-0.5)
                        if j < 4:
                            k.stt(qkb[:, j, 0:T], ct[:, 0:T], 128.0 ** -0.5, rstd[:, 0:T], ALU.mult, ALU.mult)
                        else:
                            k.tt(kvf[:, j - 4, 0:T], ct[:, 0:T], rstd[:, 0:T], ALU.mult)
                            k.cp(qkb[:, j, 0:T], kvf[:, j - 4, 0:T], eng="p")
                    proj_fm(wv, 512, T, ev_conv)
                if last_tile:
                    for i in range(3):
                        k.dma(POOL, OUT(O["nconv" + okey].ap()[l, so, i].rearrange("(j p) -> p j", p=128)),
                              halo.v(halo.h[:, l, :, i], sub=l), slow=True)
                stage_(6.2)
                wv = wget(l, "z")

                def ev_z(blk, TBx, p):
                    k.act(zg[0:TBx, blk, :], p[0:TBx, 0:512], AF.Silu)
                    k.tt(zg.v(zg.h[0:TBx, blk, :].rearrange("p (h c) -> p h c", c=128), sub=blk),
                         zg.v(zg.h[0:TBx, blk, :].rearrange("p (h c) -> p h c", c=128), sub=blk),
                         gnorm.v(gnorm.h[0:TBx, l:l + 1, :].to_broadcast([TBx, 4, 128])), ALU.mult, eng="p")
                proj_tm(wv, 512, T, ev_z)
                stage_(6.3)
                wv_ab = wget(l, "ab")
                for blk in range(NB):
                    p = pb("g")
                    for kc in range(KC):
                        k.mm(p[0:C, 0:8], xn[:, kc, blk * C:(blk + 1) * C], wv_ab(kc, 0, 8), start=(kc == 0), stop=(kc == KC - 1),
                             inc=(kc == KC - 1))
                    k.tt(abx[0:C, :], p[0:C, 0:8], dtb8.v(dtb8.h[0:C, l, :]), ALU.add)
                    k.act(abm[0:C, :], abx[0:C, :], AF.Abs)
                    k.act(abm[0:C, :], abm[0:C, :], AF.Exp, scale=-1.0)
                    k.act(abl[0:C, :], abm[0:C, :], AF.Ln, bias=epsc[0:C, 1:2])
                    k.stt(gtok[0:C, :], abx[0:C, 0:4], 0.0, abl[0:C, 0:4], ALU.max, ALU.add)
                    k.tt(gtok[0:C, :], gtok[0:C, :], negA.v(negA.h[0:C, l, :]), ALU.mult)
                    k.stt(lnb[0:C, :], abx[0:C, 4:8], 0.0, abl[0:C, 4:8], ALU.min, ALU.subtract)
                    k.act(beta[0:C, :], lnb[0:C, :], AF.Exp)
                    p = pb("h")
                    k.mm(p[0:C, 0:4], C_(1, C, C), gtok[0:C, :])
                    k.mm(p[:, 8:12], C_(5, C, 128), gtok[0:C, :])
                    k.cp(Gtok[0:C, :], p[0:C, 0:4])
                    k.ts(negG[0:C, :], p[0:C, 0:4], -1.0, ALU.mult)
                    k.tt(GpL[0:C, :], p[0:C, 0:4], lnb[0:C, :], ALU.add)
                    k.cp(glb[:, :], p[:, 8:12])
                    k.act(glt[:, :], glb[:, :], AF.Exp)
                    k.tt(dl[0:C, :], p[0:C, 8:12], Gtok[0:C, :], ALU.subtract)
                    k.act(elast[0:C, :], dl[0:C, :], AF.Exp)
                    k.act(bexpG[0:C, :], GpL[0:C, :], AF.Exp)
                    gdn_chunk(l, s, blk, C, T)
                if last_tile:
                    for h in range(4):
                        k.dma(POOL, OUT(O["ngdn" + okey].ap()[l, so, h]), Sst[:, l * 4 + h, :])
                stage_(6.4)
                wv = wget(l, "qd")
                def ev_qd(j, p):
                    k.memset(qdT[64:128, j, 0:T], 0.0)
                    k.act(qdT[0:64, j, 0:T], p[0:64, 0:T], AF.Copy, scale=0.125)
                    k.act(qd1[64:128, j, 0:T], p[64:128, 0:T], AF.Copy, scale=0.125)
                proj_fm(wv, 512, T, ev_qd)
                wv = wget(l, "kd")

                def ev_kd(j, p):
                    k.act(kdT[:, j, 0:T], p[:, 0:T], AF.Copy)
                    if not is_s and not last_tile:
                        k.dma(POOL, V(khist.ap()[l, s, j, :, tok0:tok0 + T], [b_kh[l][s]]), kdT[:, j, 0:T])
                proj_fm(wv, 512, T, ev_kd)

                def ev_kd_tm(blk, TBx, p):
                    st = R(stage, "stage")
                    k.cp(st[0:TBx, :], p[0:TBx, 0:512])
                    k.dma(POOL, OUT(O["ndk" + okey].ap()[l, so, tok0 + blk * TBx:tok0 + (blk + 1) * TBx, :]), st[0:TBx, :])
                proj_tm(wv, 512, T, ev_kd_tm)
                wv = wget(l, "vd")

                def ev_vd(blk, TBx, p):
                    st = R(stage, "stage")
                    k.cp(st[0:TBx, :], p[0:TBx, 0:512])
                    k.dma(POOL, OUT(O["ndv" + okey].ap()[l, so, tok0 + blk * TBx:tok0 + (blk + 1) * TBx, :]), st[0:TBx, :])
                    k.act(vde.v(vde.h[0:TBx, blk, :].rearrange("p (h c) -> p h c", c=VW)[:, :, 0:128], sub=blk),
                          st.v(st.h[0:TBx, 0:512].rearrange("p (h c) -> p h c", c=128)), AF.Copy)
                    if not is_s and not last_tile:
                        k.dma(POOL, V(vhist.ap()[l, s, tok0 + blk * TBx:tok0 + (blk + 1) * TBx, :], [b_vh[l][s]]),
                              vde[0:TBx, blk, :])
                proj_tm(wv, 512, T, ev_vd)
                stage_(6.5)
                nhb = nh // 128
                for h in range(4):
                    kt_, vh_ = R(kTh, "kTh"), R(vh, "vh")
                    if nh > 0:
                        k.dma(SP, kt_[:, 0:nh], V(khist.ap()[l, s, h, :, 0:nh], [b_kh[l][s]]))
                        k.dma(SP, vh_[:, 0:nhb, :],
                              V(vhist.ap()[l, s, 0:nh, h * VW:(h + 1) * VW].rearrange("(g p) c -> p g c", p=128), [b_vh[l][s]]))
                    for qb in range(NB):
                        q0 = qb * TB_
                        acc = [pb("h"), pb("h")]
                        nkb = nhb + qb + 1
                        LA = 2

                        def stage1(kb):
                            ps_ = pb("g")
                            if kb < nhb:
                                KP = 128
                                for c in range(2):
                                    k.mm(ps_[0:128, c * 128:c * 128 + TB_], kt_[:, kb * 128:(kb + 1) * 128],
                                         (qdT if c == 0 else qd1)[:, h, q0:q0 + TB_])
                            else:
                                j = kb - nhb
                                KP = TB_
                                for c in range(2):
                                    k.mm(ps_[0:KP, c * 128:c * 128 + TB_], kdT[:, h, j * TB_:(j + 1) * TB_],
                                         (qdT if c == 0 else qd1)[:, h, q0:q0 + TB_])
                            pt = R(Pt, "Pt")
                            k.act(pt.v(pt.h[0:KP, :, 0:TB_]), ps_.v(ps_.h[0:KP, 0:256].rearrange("p (c q) -> p c q", q=128)[:, :, 0:TB_]),
                                  AF.Exp)
                            if kb == nkb - 1 and TB_ == 128:
                                k.memset(pt.v(pt.h[64:128, :, 0:64]), 0.0)
                            return pt, KP

                        def stage2(kb, pt, KP):
                            for c in range(2):
                                if kb < nhb:
                                    rhs = vh_[:, kb, 0:129]
                                else:
                                    rhs = vde.v(vde.h[0:KP, kb - nhb, h * VW:h * VW + 129], sub=kb - nhb)
                                k.mm(acc[c][0:TB_, 0:129], pt.v(pt.h[0:KP, c, 0:TB_]), rhs, start=(kb == 0), stop=(kb == nkb - 1))
                        pend = {}
                        for kb in range(min(LA, nkb)):
                            pend[kb] = stage1(kb)
                        for kb in range(nkb):
                            if kb + LA < nkb:
                                pend[kb + LA] = stage1(kb + LA)
                            pt, KP = pend.pop(kb)
                            stage2(kb, pt, KP)
                        a0, a1 = acc
                        rr0, rr1 = R(r0, "r0"), R(r1, "r1")
                        k.recip(rr0[0:TB_, :], a0[0:TB_, 128:129])
                        k.recip(rr1[0:TB_, :], a1[0:TB_, 128:129])
                        k.tt(rr1[0:TB_, :], rr1[0:TB_, :], nlam.v(nlam.h[0:TB_, l:l + 1]), ALU.mult)
                        tt0, odd = R(t0, "t0"), R(od, "od")
                        k.ts(tt0[0:TB_, :], a0[0:TB_, 0:128], rr0[0:TB_, :], ALU.mult)
                        k.stt(odd[0:TB_, :], a1[0:TB_, 0:128], rr1[0:TB_, :], tt0[0:TB_, :], ALU.mult, ALU.add)
                        s1, s2, s3 = R(ss, "ss"), R(sl, "sl"), R(sr, "sr")
                        k.act(junk[0][0:TB_, :], odd[0:TB_, :], AF.Square, accum=s1[0:TB_, :])
                        k.act(s2[0:TB_, :], s1[0:TB_, :], AF.Ln, bias=epsc[0:TB_, 0:1], scale=1.0 / 128)
                        k.act(s3[0:TB_, :], s2[0:TB_, :], AF.Exp, scale=-0.5)
                        k.stt(tokb[0:TB_, qb, 512 + h * 128:512 + (h + 1) * 128], odd[0:TB_, :], s3[0:TB_, :],
                              dnorm.v(dnorm.h[0:TB_, l, :]), ALU.mult, ALU.mult)
                stage_(6.6)
                for kc in range(KC):
                    p = pb("g")
                    for blk in range(NB):
                        k.tr(p[:, blk * TB_:(blk + 1) * TB_], tokb[0:TB_, blk, kc * 128:(kc + 1) * 128], C_(0, TB_, TB_))
                    k.act(mixT[:, kc, 0:T], p[:, 0:T], AF.Copy)

                def ev_res(base):
                    def f(j, p):
                        jj = base + j
                        k.tt(hT[:, jj, 0:T], p[:, 0:T], hT[:, jj, 0:T], ALU.add)
                    return f
                for half in range(2):
                    wv = wget(l, "o%d" % half)
                    proj_fm(wv, 512, T, ev_res(half * 4), rhs=mixT)
                stage_(6.7)
                rmsnorm_fm(hT, lambda kc: gains.v(gains.h[:, l, 1, kc:kc + 1]), T, xn)
                mk_, mv_ = R(memk, "memk"), R(memv, "memv")
                k.dma(SP, mk_[:], V(mkT_sc.ap()[l, s].rearrange("p (c m) -> p c m", m=N_MEM), [b_mk[l][s]]))
                k.dma(SP, mv_[:], V(mv_sc.ap()[l, s].rearrange("(mb p) n -> p mb n", p=128), [b_mv[l][s]]))
                for half in range(2):
                    wv = wget(l, "mq%d" % half)
                    proj_fm(wv, 512, T, lambda j, p, half=half: k.act(qmT[:, half * 4 + j, 0:T], p[:, 0:T], AF.Copy, scale=1.0 / 16))
                for h in range(4):
                    pms = []
                    for mb in range(2):
                        p = pb("g")
                        for dc in range(2):
                            k.mm(p[:, 0:T], mk_.v(mk_.h[:, 2 * h + dc, mb * 128:(mb + 1) * 128]), qmT[:, 2 * h + dc, 0:T],
                                 start=(dc == 0), stop=(dc == 1), inc=(dc == 1))
                        pm_ = R(Pm, "Pm")
                        k.act(pm_[:, 0:T], p[:, 0:T], AF.Exp)
                        pms.append(pm_)
                    p = pb("g")
                    for mb in range(2):
                        k.mm(p[:, 0:T], onesb[:], pms[mb][:, 0:T], start=(mb == 0), stop=(mb == 1), inc=(mb == 1))
                    k.recip(rinv[:, 0:T], p[:, 0:T])
                    for dvc in range(2):
                        p = pb("g")
                        for mb in range(2):
                            k.mm(p[:, 0:T], mv_.v(mv_.h[:, mb, (2 * h + dvc) * 128:(2 * h + dvc + 1) * 128]), pms[mb][:, 0:T],
                                 start=(mb == 0), stop=(mb == 1), inc=(mb == 1))
                        k.tt(mixT[:, 2 * h + dvc, 0:T], p[:, 0:T], rinv[:, 0:T], ALU.mult)
                for half in range(2):
                    wv = wget(l, "mo%d" % half)
                    proj_fm(wv, 512, T, ev_res(half * 4), rhs=mixT)
                stage_(6.8)
                rmsnorm_fm(hT, lambda kc: gains.v(gains.h[:, l, 2, kc:kc + 1]), T, xn)
                for pc in range(8):
                    wv = wget(l, "f1_%d" % pc)

                    def ev_f1(j, p, pc=pc):
                        rt = R(rtmp, "rtmp")
                        k.act(rt[:, 0:T], p[:, 0:T], AF.Relu)
                        k.tt(h1[:, pc * 4 + j, 0:T], rt[:, 0:T], rt[:, 0:T], ALU.mult, eng="p")
                    proj_fm(wv, 512, T, ev_f1)
                for j in range(8):
                    wv = wget(l, "f2_%d" % j)
                    p = pb("g")
                    for kc in range(32):
                        k.mm(p[:, 0:T], wv(kc, 0, 128), h1[:, kc, 0:T], start=(kc == 0), stop=(kc == 31), inc=(kc == 31))
                    k.tt(hT[:, j, 0:T], p[:, 0:T], hT[:, j, 0:T], ALU.add)

            for (s, t, T) in tiles:
                is_s = s >= NPS
                L = DEC if is_s else SEQ
                TB_ = min(128, T)
                NB = T // TB_
                xin = inp("x_sample")[0] if is_s else inp("x_prompt")[s]
                for blk in range(NB):
                    k.dma(SP, tokb[0:TB_, blk, :], IN(xin[t * T + blk * TB_:t * T + (blk + 1) * TB_, :]))
                for kc in range(KC):
                    p = pb("g")
                    for blk in range(NB):
                        k.tr(p[:, blk * TB_:(blk + 1) * TB_], tokb[0:TB_, blk, kc * 128:(kc + 1) * 128], C_(0, TB_, TB_))
                    k.cp(hT[:, kc, 0:T], p[:, 0:T])
                nh = PAST if is_s else t * T
                for l in range(DEPTH):
                    tile_layer(l, s, t, T, nh, t == 0, t == L // T - 1)
                yT = kvf
                rmsnorm_fm(hT, lambda kc: gfin[:, kc:kc + 1], T, yT)
                for blk in range(NB):
                    for half in range(2):
                        p = pb("g")
                        for c4 in range(4):
                            kc = half * 4 + c4
                            k.tr(p[0:TB_, c4 * 128:(c4 + 1) * 128], yT[:, kc, blk * TB_:(blk + 1) * TB_], ident)
                        k.cp(tokb[0:TB_, blk, half * 512:(half + 1) * 512], p[0:TB_, :])
                    yo = O["y_sample"].ap()[0] if is_s else O["y_prompt"].ap()[s]
                    k.dma(POOL, OUT(yo[t * T + blk * TB_:t * T + (blk + 1) * TB_, :]), tokb[0:TB_, blk, :])
            assert ws["pos"] == len(order), (ws["pos"], len(order))
        except StopBuild:
            print('stopped early at', cfg.stop)
        final = {}
        for e in (PE, ACT, DVE, POOL):
            e.h.nop().then_inc(e.sem, 1)
            e.cnt += 1
            final[e.sid] = e.cnt
        for b in ALLBUFS:
            if b.dsem is not None:
                final[b.dsem[1]] = 16 * b.dcnt
        for sid, val in final.items():
            SP.h.wait_ge(k.sems[sid], val)
        print("instructions:", k.nins, "sems:", k.nsem)
    return nc


def make_consts():
    c = np.zeros((9, 128, 128), np.float32)
    i = np.arange(128)
    c[0] = np.eye(128)
    c[1] = (i[:, None] <= i[None, :])
    c[2] = BIG * (i[None, :] >= i[:, None])
    c[3] = -BIG * (i[None, :] <= i[:, None])
    c[4] = -BIG * (i[None, :] < i[:, None])
    c[5] = 1.0
    bi, bj = i[:, None], i[None, :]
    c[6] = (bi // 32 == bj // 32)
    c[7] = (bi // 64 == bj // 64) & (bi % 64 >= 32) & (bj % 64 < 32)
    c[8] = (bi >= 64) & (bj < 64)
    return c


def run(cfg, inputs, trace=False):
    nc = build(cfg)
    DEPTH, NPS = cfg.depth, cfg.nps
    f = lambda a: np.ascontiguousarray(np.asarray(a, dtype=np.float32))
    wnames = ["ln_mix", "w_in", "conv_w", "a_log", "dt_bias", "gdn_norm", "lambda_q1", "lambda_k1", "lambda_q2",
              "lambda_k2", "diff_norm", "w_out", "ln_mem_q", "ln_mem_kv", "w_mem_q", "w_mem_k", "w_mem_v", "w_mem_o",
              "ln_ffn", "w_ff1", "w_ff2"]
    shared = {n: f(inputs[n]) for n in wnames}
    shared["ln_final"] = f(inputs["ln_final"]).reshape(1, D)
    shared["consts"] = make_consts()
    in_maps = []
    for c in range(cfg.ncores):
        m = dict(shared)
        m["x_prompt"] = f(inputs["x_prompt"][c * NPS:(c + 1) * NPS])
        m["x_sample"] = f(inputs["x_sample"][c:c + 1])
        m["mem_prompt"] = f(inputs["mem_prompt"][c * NPS:(c + 1) * NPS])
        m["cache_diff_k"] = f(inputs["cache_diff_k"][:, c]).reshape(DEPTH, cfg.past, 512)
        m["cache_diff_v"] = f(inputs["cache_diff_v"][:, c]).reshape(DEPTH, cfg.past, 512)
        m["cache_mem_k"] = f(inputs["cache_mem_k"][:, c]).reshape(DEPTH, N_MEM, D)
        m["cache_mem_v"] = f(inputs["cache_mem_v"][:, c]).reshape(DEPTH, N_MEM, D)
        m["state_gdn"] = f(inputs["state_gdn"][:, c])
        m["state_gdn_conv"] = f(inputs["state_gdn_conv"][:, c])
        in_maps.append(m)
    res = run_bass_kernel_spmd(nc, in_maps, core_ids=list(range(cfg.ncores)))
    r = res.results
    cat0 = lambda n: np.concatenate([x[n] for x in r], axis=0)
    cat1 = lambda n: np.concatenate([x[n] for x in r], axis=1)
    B = cfg.ncores * NPS
    Bs = cfg.ncores
    return (cat0("y_prompt"), cat0("y_sample"),
            cat1("ndk_p").reshape(DEPTH, B, cfg.seq, 4, 128), cat1("ndv_p").reshape(DEPTH, B, cfg.seq, 4, 128),
            cat1("ngdn_p"), cat1("nconv_p"),
            cat1("nmk_p").reshape(DEPTH, B, N_MEM, 4, 256), cat1("nmv_p").reshape(DEPTH, B, N_MEM, 4, 256),
            cat1("ndk_s").reshape(DEPTH, Bs, cfg.dec, 4, 128), cat1("ndv_s").reshape(DEPTH, Bs, cfg.dec, 4, 128),
            cat1("ngdn_s"), cat1("nconv_s"))


def kernel(**inputs):
    return run(FULL, inputs)
```
